# Optimizing a Trainium2 kernel written in Bass

```python
import jax, jax.numpy as jnp
from jax import lax
import numpy as np

D_MODEL = 1024
BATCH = 4
SEQ = 8192
DEPTH = 4

SB_HEADS = 8
SB_HEAD_DIM = 64
SB_WIDTH = SB_HEADS * SB_HEAD_DIM
SB_BLOCK = 128
POOL_WINDOWS = (2, 4, 8, 16)
POOL_GROUPS = 4
POOL_GROUP_DIM = 64
POOL_WIDTH = POOL_GROUPS * POOL_GROUP_DIM
GM_GROUPS = 4
GM_GROUP_DIM = 64
GM_WIDTH = GM_GROUPS * GM_GROUP_DIM
GM_CHUNK = 128
N_BRANCH = 3
D_FF = 4 * D_MODEL
RMS_EPS = 1e-6
IN_SIZES = (SB_WIDTH, SB_WIDTH, SB_WIDTH, POOL_WIDTH, GM_WIDTH, GM_WIDTH, N_BRANCH * D_MODEL)
D_IN = sum(IN_SIZES)
IN_SPLITS = tuple(int(s) for s in np.cumsum(IN_SIZES)[:-1])

kernel_name = "hybrid_stickbreak_pool_gmlp_block"


def rms_norm(x, gain):
    xf = x.astype(jnp.float32)
    y = xf * lax.rsqrt(jnp.mean(xf * xf, axis=-1, keepdims=True) + RMS_EPS)
    return (y * gain.astype(jnp.float32)).astype(x.dtype)


def stick_breaking_attention(q, k, v):
    B, S, H, Dh = q.shape
    scale = Dh ** -0.5
    outs = []
    for blk in range(S // SB_BLOCK):
        q0 = blk * SB_BLOCK
        q1 = q0 + SB_BLOCK
        qb = q[:, q0:q1]
        kb = k[:, :q1]
        vb = v[:, :q1]
        z = jnp.einsum('bthd,bshd->bhts', qb, kb).astype(jnp.float32) * scale
        t_idx = q0 + jnp.arange(SB_BLOCK)[:, None]
        s_idx = jnp.arange(q1)[None, :]
        strict = s_idx < t_idx
        log_not = jnp.where(strict, jax.nn.log_sigmoid(-z), 0.0)
        suffix = lax.cumsum(log_not, axis=3, reverse=True) - log_not
        a = jnp.where(strict, jnp.exp(jax.nn.log_sigmoid(z) + suffix), 0.0).astype(v.dtype)
        outs.append(jnp.einsum('bhts,bshd->bthd', a, vb))
    return jnp.concatenate(outs, axis=1)


def multiscale_pool(p, w_pool, pool_scale):
    B, S, _ = p.shape
    pg = p.reshape(B, S, POOL_GROUPS, POOL_GROUP_DIM)
    csum = jnp.cumsum(pg.astype(jnp.float32), axis=1)
    pos = jnp.arange(S, dtype=jnp.float32)
    pooled = []
    for g, w in enumerate(POOL_WINDOWS):
        cg = csum[:, :, g]
        shifted = jnp.pad(cg, ((0, 0), (w, 0), (0, 0)))[:, :S]
        count = jnp.minimum(pos + 1.0, float(w))[None, :, None]
        pooled.append((cg - shifted) / count - pg[:, :, g].astype(jnp.float32))
    pooled = jnp.stack(pooled, axis=2).astype(p.dtype)
    mixed = jnp.einsum('bsgc,gcd->bsgd', pooled, w_pool)
    return mixed.reshape(B, S, POOL_WIDTH) * pool_scale


def chunked_spatial_gating(u, v, gm_gain, w_spatial, b_spatial):
    B, S, _ = u.shape
    u = jax.nn.gelu(u)
    v = rms_norm(jax.nn.gelu(v), gm_gain)
    n_chunks = S // GM_CHUNK
    vc = v.reshape(B, n_chunks, GM_CHUNK, GM_GROUPS, GM_GROUP_DIM)
    causal = jnp.tril(jnp.ones((GM_CHUNK, GM_CHUNK), dtype=bool))
    ws = jnp.where(causal[None], w_spatial, 0.0).astype(v.dtype)
    mixed = jnp.einsum('gtp,bnpgc->bntgc', ws, vc) + b_spatial.T[:, :, None]
    return u * mixed.reshape(B, S, GM_WIDTH)


def setup_inputs(seed: int = 0) -> dict:
    key = jax.random.key(seed)
    ks = jax.random.split(key, 18)

    def nrm(k, shape, scale):
        return jax.random.normal(k, shape, jnp.float32) * scale

    def gain(k, shape):
        return 1.0 + 0.05 * jax.random.normal(k, shape, jnp.float32)

    return {
        "x": nrm(ks[0], (BATCH, SEQ, D_MODEL), 1.0),
        "w_in": nrm(ks[1], (DEPTH, D_MODEL, D_IN), D_MODEL ** -0.5),
        "w_pool": nrm(ks[2], (DEPTH, POOL_GROUPS, POOL_GROUP_DIM, POOL_GROUP_DIM), POOL_GROUP_DIM ** -0.5),
        "pool_scale": gain(ks[3], (DEPTH, POOL_WIDTH)),
        "gm_gain": gain(ks[4], (DEPTH, GM_WIDTH)),
        "w_spatial": nrm(ks[5], (DEPTH, GM_GROUPS, GM_CHUNK, GM_CHUNK), GM_CHUNK ** -0.5),
        "b_spatial": gain(ks[6], (DEPTH, GM_GROUPS, GM_CHUNK)),
        "w_br_sb": nrm(ks[7], (DEPTH, SB_WIDTH, D_MODEL), SB_WIDTH ** -0.5),
        "w_br_pool": nrm(ks[8], (DEPTH, POOL_WIDTH, D_MODEL), POOL_WIDTH ** -0.5),
        "w_br_gm": nrm(ks[9], (DEPTH, GM_WIDTH, D_MODEL), GM_WIDTH ** -0.5),
        "w_out": nrm(ks[10], (DEPTH, D_MODEL, D_MODEL), D_MODEL ** -0.5),
        "g_mix_pre": gain(ks[11], (DEPTH, D_MODEL)),
        "g_mix_post": gain(ks[12], (DEPTH, D_MODEL)),
        "g_ff_pre": gain(ks[13], (DEPTH, D_MODEL)),
        "g_ff_post": gain(ks[14], (DEPTH, D_MODEL)),
        "w_ff_in": nrm(ks[15], (DEPTH, D_MODEL, D_FF), D_MODEL ** -0.5),
        "w_ff_out": nrm(ks[16], (DEPTH, D_FF, D_MODEL), D_FF ** -0.5),
    }


def reference(x, w_in, w_pool, pool_scale, gm_gain, w_spatial, b_spatial, w_br_sb, w_br_pool,
              w_br_gm, w_out, g_mix_pre, g_mix_post, g_ff_pre, g_ff_post, w_ff_in, w_ff_out):
    B, S, D = x.shape
    for l in range(DEPTH):
        h = rms_norm(x, g_mix_pre[l])
        proj = h @ w_in[l]
        q, k, v, p_in, gm_u, gm_v, gate_in = jnp.split(proj, IN_SPLITS, axis=-1)
        o_sb = stick_breaking_attention(q.reshape(B, S, SB_HEADS, SB_HEAD_DIM),
                                        k.reshape(B, S, SB_HEADS, SB_HEAD_DIM),
                                        v.reshape(B, S, SB_HEADS, SB_HEAD_DIM)).reshape(B, S, SB_WIDTH)
        o_pool = multiscale_pool(p_in, w_pool[l], pool_scale[l])
        o_gm = chunked_spatial_gating(gm_u, gm_v, gm_gain[l], w_spatial[l], b_spatial[l])
        gates = jax.nn.sigmoid(gate_in.reshape(B, S, N_BRANCH, D))
        merged = (gates[:, :, 0] * (o_sb @ w_br_sb[l])
                  + gates[:, :, 1] * (o_pool @ w_br_pool[l])
                  + gates[:, :, 2] * (o_gm @ w_br_gm[l]))
        x = x + rms_norm(merged @ w_out[l], g_mix_post[l])
        h = rms_norm(x, g_ff_pre[l])
        ff = jnp.square(jax.nn.relu(h @ w_ff_in[l])) @ w_ff_out[l]
        x = x + rms_norm(ff, g_ff_post[l])
    return x
```

```python
import numpy as np
from contextlib import ExitStack
import concourse.bass as bass
import concourse.mybir as mybir
from concourse.bass_utils import run_bass_kernel_spmd

F32 = mybir.dt.float32
BF16 = mybir.dt.bfloat16
AF = mybir.ActivationFunctionType
ALU = mybir.AluOpType
AX = mybir.AxisListType

D = 1024
T = 8192
DEPTH = 4
DIN = 5376
DFF = 4096
EPS = 1e-6
SEM_ROT = 30000
NCORES = 4
GELU_C = 1.5957691216057308
DBG = {}


class Buf:
    __slots__ = ("wr", "rd")

    def __init__(self):
        self.wr = {}
        self.rd = {}


class Builder:
    def __init__(self, nc):
        self.nc = nc
        self.q = {k: [] for k in ("pe", "act", "dve", "pool", "sp")}
        self.sems = []
        self.esem = {}
        self.ecnt = {}
        self.dsem = {}
        self.dcnt = {}
        self.last = {}
        self.waited = {k: {} for k in self.q}
        self.pend = {}
        self.sem_eng = {}

    def newsem(self):
        self.sems.append(self.nc.alloc_semaphore(f"sm{len(self.sems)}"))
        return len(self.sems) - 1

    def _push(self, eng, fn, waits, sem, inc):
        pend = self.pend.pop(eng, None)
        if pend:
            for k, v in pend.items():
                if waits.get(k, 0) < v:
                    waits[k] = v
        ws = []
        wd = self.waited[eng]
        for k, v in waits.items():
            if wd.get(k, 0) >= v:
                continue
            if eng == "pe" and self.sem_eng.get(k) == "pe":
                continue
            wd[k] = v
            ws.append((k, v))
        self.q[eng].append((ws, fn, sem, inc))

    @staticmethod
    def _merge(dst, src):
        for k, v in src.items():
            if dst.get(k, 0) < v:
                dst[k] = v

    def _deps(self, reads, writes):
        w = {}
        for b in reads:
            self._merge(w, b.wr)
        for b in writes:
            self._merge(w, b.wr)
            self._merge(w, b.rd)
        return w

    def _mark(self, reads, writes, sem, val):
        for b in reads:
            if b.rd.get(sem, 0) < val:
                b.rd[sem] = val
        for b in writes:
            b.wr = {sem: val}
            b.rd = {}

    def op(self, eng, fn, reads=(), writes=()):
        if eng not in self.esem or self.ecnt[eng] >= SEM_ROT:
            self.esem[eng] = self.newsem()
            self.sem_eng[self.esem[eng]] = eng
            self.ecnt[eng] = 0
        w = self._deps(reads, writes)
        self.ecnt[eng] += 1
        sem = self.esem[eng]
        val = self.ecnt[eng]
        self._push(eng, fn, w, sem, 1)
        self._mark(reads, writes, sem, val)
        self.last[("e", eng)] = (sem, val)

    def raw(self, eng, fn):
        self.q[eng].append(([], fn, None, 0))

    def dma(self, eng, key, out, in_, reads=(), writes=()):
        if key not in self.dsem:
            self.dsem[key] = self.newsem()
            self.dcnt[key] = 0
        w = self._deps(reads, writes)
        self.dcnt[key] += 16
        sem = self.dsem[key]
        val = self.dcnt[key]
        self._push(eng, lambda e: e.dma_start(out=out, in_=in_), w, sem, 16)
        self._mark(reads, writes, sem, val)
        self.last[("d", key)] = (sem, val)

    def group_done(self, key, bufs):
        sem = self.dsem[key]
        val = self.dcnt[key]
        for b in bufs:
            b.wr = {sem: val}

    def barrier(self):
        toks = {}
        for (sem, val) in self.last.values():
            if toks.get(sem, 0) < val:
                toks[sem] = val
        for eng in self.q:
            self.pend[eng] = dict(toks)

    def final(self):
        self.barrier()
        for eng in self.q:
            self._push(eng, lambda e: e.nop(), {}, None, 0)

    def emit(self, block):
        decs = {"pe": block.tensor, "act": block.scalar, "dve": block.vector,
                "pool": block.gpsimd, "sp": block.sync}
        for k, dec in decs.items():
            def f(e, k=k):
                sems = self.sems
                for ws, fn, sem, inc in self.q[k]:
                    for (wi, wv) in ws:
                        e.wait_ge(sems[wi], wv)
                    ins = fn(e)
                    if sem is not None:
                        ins.then_inc(sems[sem], inc)
            dec(f)

    def mmg(self, out, pairs, reads, writes, start=True, stop=True):
        n = len(pairs)
        for i, (l, r) in enumerate(pairs):
            st = start and i == 0
            sp = stop and i == n - 1
            fn = (lambda e, l=l, r=r, st=st, sp=sp: e.matmul(out, l, r, start=st, stop=sp))
            if n == 1:
                self.op("pe", fn, reads, writes)
            elif i == 0:
                w = self._deps(reads, writes)
                self._push("pe", fn, w, None, 0)
            elif i == n - 1:
                self.op("pe", fn, reads, writes)
            else:
                self.raw("pe", fn)

    def act(self, out, in_, func, reads, writes, scale=1.0, bias=0.0):
        self.op("act", lambda e: e.activation(out=out, in_=in_, func=func, bias=bias, scale=scale),
                reads, writes)

    def tt(self, eng, out, in0, in1, op, reads, writes):
        self.op(eng, lambda e: e.tensor_tensor(out=out, in0=in0, in1=in1, op=op), reads, writes)

    def ts(self, eng, out, in0, s1, s2, op0, op1, reads, writes):
        self.op(eng, lambda e: e.tensor_scalar(out=out, in0=in0, scalar1=s1, scalar2=s2, op0=op0, op1=op1),
                reads, writes)

    def ts1(self, eng, out, in0, s1, op0, reads, writes):
        self.op(eng, lambda e: e.tensor_scalar(out=out, in0=in0, scalar1=s1, scalar2=None, op0=op0),
                reads, writes)

    def stt(self, eng, out, in0, scalar, in1, op0, op1, reads, writes):
        self.op(eng, lambda e: e.scalar_tensor_tensor(out=out, in0=in0, scalar=scalar, in1=in1,
                                                      op0=op0, op1=op1), reads, writes)

    def cp(self, eng, out, in_, reads, writes):
        self.op(eng, lambda e: e.tensor_copy(out=out, in_=in_), reads, writes)


class Rot:
    def __init__(self, items):
        self.items = items
        self.i = 0

    def next(self):
        it = self.items[self.i % len(self.items)]
        self.i += 1
        return it


def build(nlayers=DEPTH, dbg=False, stages="ABDE"):
    nc = bass.Bass("TRN2", target_bir_lowering=False)

    def din(name, shape):
        return nc.dram_tensor(name, shape, F32, kind="ExternalInput").ap()

    xin = din("xT", [D, T])
    w_in = din("w_in", [DEPTH, D, DIN])
    w_pool = din("w_pool", [DEPTH, 4, 64, 64])
    gm_gain = din("gm_gain", [DEPTH, 256])
    w_spT = din("w_spT", [DEPTH, 4, 128, 128])
    b_spatial = din("b_spatial", [DEPTH, 4, 128])
    w_br_sb = din("w_br_sb", [DEPTH, 512, 1024])
    w_br_pool = din("w_br_pool", [DEPTH, 256, 1024])
    w_br_gm = din("w_br_gm", [DEPTH, 256, 1024])
    w_out = din("w_out", [DEPTH, 1024, 1024])
    w_ff_in = din("w_ff_in", [DEPTH, 1024, DFF])
    w_ff_out = din("w_ff_out", [DEPTH, DFF, 1024])
    pvec = din("pvec", [DEPTH, 128, 34])
    c_tril = din("c_tril", [128, 128])
    c_negtri = din("c_negtri", [128, 128])
    c_maskL = din("c_maskL", [128, 4, 512])
    c_invcnt = din("c_invcnt", [2, 256, 512])

    kind_s = "ExternalOutput" if dbg else "Internal"
    yT = nc.dram_tensor("yT", [D, T], F32, kind="ExternalOutput").ap()
    QT = nc.dram_tensor("QT", [512, T], BF16, kind=kind_s).ap()
    KT = nc.dram_tensor("KT", [512, T], BF16, kind=kind_s).ap()
    Vd = nc.dram_tensor("Vd", [T, 512], BF16, kind=kind_s).ap()
    OSB = nc.dram_tensor("OSB", [512, T], BF16, kind=kind_s).ap()
    OPL = nc.dram_tensor("OPL", [256, T], BF16, kind=kind_s).ap()
    OGM = nc.dram_tensor("OGM", [256, T], BF16, kind=kind_s).ap()

    B = Builder(nc)

    def fm(ap):
        return ap.rearrange("(c p) t -> p c t", p=128)

    with ExitStack() as top:
        ps_t = top.enter_context(nc.psum_tensor("ps", [128, 8, 512], F32))
        ps = [ps_t[:, k, :] for k in range(8)]
        PS = [Buf() for _ in range(8)]

        ones_t = top.enter_context(nc.sbuf_tensor("ones", [128, 128], BF16))
        nones_t = top.enter_context(nc.sbuf_tensor("nones", [128, 128], BF16))
        negtri_t = top.enter_context(nc.sbuf_tensor("negtri", [128, 128], BF16))
        tril_t = top.enter_context(nc.sbuf_tensor("tril", [128, 128], F32))
        pv_t = top.enter_context(nc.sbuf_tensor("pv", [128, DEPTH, 34], F32))
        CONST = Buf()
        B.op("pool", lambda e: e.memset(ones_t[:], 1.0), [], [CONST])
        B.op("pool", lambda e: e.memset(nones_t[:], -1.0), [], [CONST])
        B.dma("pool", "w", negtri_t[:], c_negtri, [], [CONST])
        B.dma("pool", "w", tril_t[:], c_tril, [], [CONST])
        B.dma("pool", "w", pv_t[:], pvec.rearrange("l p c -> p l c"), [], [CONST])
        B.group_done("w", [CONST])
        B.barrier()

        wst = [top.enter_context(nc.sbuf_tensor(f"wst{i}", [128, 512], F32)) for i in range(2)]
        WST = [Buf(), Buf()]
        wscr_off = [0]
        wl_i = [0]

        def wload(dst, src, shape, Wb):
            n = shape[1]
            if n > 512:
                for o in range(0, n, 512):
                    wload(dst[:, o:o + 512], src[:, o:o + 512], [128, 512], Wb)
                return
            k = wl_i[0] % 2
            wl_i[0] += 1
            B.dma("sp", f"wst{k}", wst[k][:, 0:n], src, [], [WST[k]])
            B.cp("dve" if k == 0 else "pool", dst, wst[k][:, 0:n], [WST[k]], [Wb])

        def norm_tile(xt, XT, sq, SQ, hT, HT, rs, RS, gcol0, l, ncols, msbank):
            B.act(sq[:], xt[:], AF.Square, [XT], [SQ])
            B.mmg(ps[msbank][:, 0:ncols], [(ones_t[:], sq[:, c, :]) for c in range(8)],
                  [SQ, CONST], [PS[msbank]])
            B.act(rs[:], ps[msbank][:, 0:ncols], AF.Ln, [PS[msbank]], [RS], scale=1.0 / D, bias=EPS)
            B.act(rs[:], rs[:], AF.Exp, [RS], [RS], scale=-0.5)
            for c in range(8):
                B.stt("dve", hT[:, c, :], xt[:, c, :], pv_t[:, l, gcol0 + c:gcol0 + c + 1], rs[:],
                      ALU.mult, ALU.mult, [XT, RS, CONST], [HT])

        def gelu(eng2, x, X, tmp, TMP, sg, SG, out, OUT):
            B.tt("dve", tmp, x, x, ALU.mult, [X], [TMP])
            B.ts("dve", tmp, tmp, 0.044715, 1.0, ALU.mult, ALU.add, [TMP], [TMP])
            B.tt("dve", tmp, tmp, x, ALU.mult, [TMP, X], [TMP])
            B.act(sg, tmp, AF.Sigmoid, [TMP], [SG], scale=GELU_C)
            B.tt(eng2, out, x, sg, ALU.mult, [X, SG], [OUT])

        def stage_A(l, xsrc):
            NT = DBG.get("NT", T // 512)
            parts = DBG.get("parts", "qk,pu,v,gv,pool,gm")
            with ExitStack() as st:
                def sb(n, s, d):
                    return st.enter_context(nc.sbuf_tensor(f"A{l}_{n}", s, d))
                wa = sb("wa", [128, 8, 2304], BF16)
                wpbd = sb("wpbd", [128, 2, 128], BF16)
                wsT32 = sb("wsT32", [128, 4, 128], F32)
                wsT = sb("wsT", [128, 4, 128], BF16)
                brep = sb("brep", [128, 2, 128], F32)
                gainrep = sb("gainrep", [128, 256], F32)
                ic_t = sb("ic", [128, 2, 2, 512], F32)
                W = Buf()
                for wch in range(2):
                    B.dma("pool", "w", ic_t[:, wch, :, :], c_invcnt[wch].rearrange("(j p) t -> p j t", p=128), [], [W])
                B.op("pool", lambda e: e.memset(wpbd[:], 0.0), [], [W])
                wv = w_in[l].rearrange("(c p) n -> p c n", p=128)
                for c in range(8):
                    B.dma("pool", "w", wa[:, c, :], wv[:, c, 0:2304], [], [W])
                for g in range(4):
                    j, hh = g // 2, g % 2
                    B.dma("pool", "w", wpbd[hh * 64:(hh + 1) * 64, j, hh * 64:(hh + 1) * 64],
                          w_pool[l, g], [], [W])
                    B.dma("pool", "w", brep[hh * 64:(hh + 1) * 64, j, :],
                          b_spatial[l, g, :].partition_broadcast(64), [], [W])
                B.dma("pool", "w", wsT32[:], w_spT[l].rearrange("g p t -> p g t"), [], [W])
                B.dma("pool", "w", gainrep[:], gm_gain[l, :].partition_broadcast(128), [], [W])
                B.group_done("w", [W])
                for g in range(4):
                    B.tt("dve", wsT[:, g, :], wsT32[:, g, :], tril_t[:], ALU.mult, [W, CONST], [W])

                xts = [sb(f"xt{i}", [128, 8, 512], F32) for i in range(2)]
                XTS = [Buf(), Buf()]
                sq = sb("sq", [128, 8, 512], BF16); SQ = Buf()
                hT = sb("hT", [128, 8, 512], BF16); HT = Buf()
                rs = sb("rs", [128, 512], F32); RS = Buf()
                qk_st = sb("qkst", [128, 8, 512], BF16); QST = Buf(); KST = Buf()
                v_st = sb("vst", [128, 4, 512], BF16); VST = Buf()
                ptile = sb("ptile", [128, 2, 528], F32); PT = Buf()
                s2h = sb("s2h", [128, 2, 528], F32); s4h = sb("s4h", [128, 2, 528], F32)
                s8h = sb("s8h", [128, 2, 528], F32); s16h = sb("s16h", [128, 2, 528], F32)
                SH = Buf()
                ptmp = sb("ptmp", [128, 2, 512], F32); PTMP = Buf()
                pooled = sb("pooled", [128, 2, 512], BF16); PLD = Buf()
                opl_st = sb("oplst", [128, 2, 512], BF16); OPST = Buf()
                uf = sb("uf", [128, 2, 512], F32); UF = Buf()
                utmp = sb("utmp", [128, 2, 512], F32); UTMP = Buf()
                usg = sb("usg", [128, 2, 512], F32); USG = Buf()
                ug = sb("ug", [128, 2, 512], F32); UG = Buf()
                gvf = sb("gvf", [128, 4, 256], F32); GVF = Buf()
                gtmp = sb("gtmp", [128, 4, 256], F32); GTMP = Buf()
                gsg = sb("gsg", [128, 4, 256], F32); GSG = Buf()
                gvg = sb("gvg", [128, 4, 256], F32); GVG = Buf()
                ss = sb("ss", [128, 4], F32); SS = Buf()
                gvn = sb("gvn", [128, 4, 256], BF16); GVN = Buf()
                mtmp = sb("mtmp", [128, 2, 512], F32); MTMP = Buf()
                ogm_st = sb("ogmst", [128, 2, 512], BF16); OGST = Buf()
                B.op("pool", lambda e: e.memset(ptile[:], 0.0), [], [PT])

                rot = Rot([2, 3, 4, 5])
                xv = fm(xsrc)
                for i in range(NT):
                    tsl = slice(i * 512, (i + 1) * 512)
                    xt, XT = xts[i % 2], XTS[i % 2]
                    B.dma("sp", f"A.xt{i % 2}", xt[:], xv[:, :, tsl], [], [XT])
                    norm_tile(xt, XT, sq, SQ, hT, HT, rs, RS, 0, l, 512, 0)
                    for j in (range(8) if "qk" in parts else []):
                        bk = rot.next()
                        B.mmg(ps[bk], [(wa[:, c, j * 128:(j + 1) * 128], hT[:, c, :]) for c in range(8)],
                              [W, HT], [PS[bk]])
                        if j < 4:
                            B.act(qk_st[:, j, :], ps[bk], AF.Identity, [PS[bk]], [QST], scale=0.125)
                        else:
                            B.cp("dve", qk_st[:, j, :], ps[bk], [PS[bk]], [KST])
                    if "qk" in parts:
                        B.dma("sp", "A.qst", fm(QT)[:, :, tsl], qk_st[:, 0:4, :], [QST], [])
                        B.dma("sp", "A.kst", fm(KT)[:, :, tsl], qk_st[:, 4:8, :], [KST], [])
                    if "pu" in parts:
                        for j in range(2):
                            bk = rot.next()
                            B.mmg(ps[bk], [(wa[:, c, 1536 + j * 128:1536 + (j + 1) * 128], hT[:, c, :])
                                           for c in range(8)], [W, HT], [PS[bk]])
                            B.act(ptile[:, j, 16:528], ps[bk], AF.Identity, [PS[bk]], [PT])
                        for j in range(2):
                            bk = rot.next()
                            B.mmg(ps[bk], [(wa[:, c, 1792 + j * 128:1792 + (j + 1) * 128], hT[:, c, :])
                                           for c in range(8)], [W, HT], [PS[bk]])
                            B.act(uf[:, j, :], ps[bk], AF.Identity, [PS[bk]], [UF])
                    if "v" in parts:
                        for b in range(4):
                            bk = rot.next()
                            B.mmg(ps[bk], [(hT[:, c, b * 128:(b + 1) * 128], wa[:, c, 1024:1536])
                                           for c in range(8)], [W, HT], [PS[bk]])
                            B.cp("dve", v_st[:, b, :], ps[bk], [PS[bk]], [VST])
                        B.dma("sp", "A.vst", Vd.rearrange("(n p) c -> p n c", p=128)[:, i * 4:(i + 1) * 4, :],
                              v_st[:], [VST], [])
                    if "gv" in parts:
                        for b2 in range(2):
                            bk = rot.next()
                            for bb in range(2):
                                b = b2 * 2 + bb
                                B.mmg(ps[bk][:, bb * 256:(bb + 1) * 256],
                                      [(hT[:, c, b * 128:(b + 1) * 128], wa[:, c, 2048:2304]) for c in range(8)],
                                      [W, HT], [PS[bk]])
                            B.act(gvf[:, b2 * 2:b2 * 2 + 2, :], ps[bk].rearrange("p (a n) -> p a n", a=2), AF.Identity,
                                  [PS[bk]], [GVF])
                    if "pool" in parts:
                        B.tt("pool", s2h[:, :, 1:528], ptile[:, :, 1:528], ptile[:, :, 0:527], ALU.add, [PT], [SH])
                        B.tt("pool", s4h[:, :, 3:528], s2h[:, :, 3:528], s2h[:, :, 1:526], ALU.add, [SH], [SH])
                        B.tt("pool", s8h[:, :, 7:528], s4h[:, :, 7:528], s4h[:, :, 3:524], ALU.add, [SH], [SH])
                        B.tt("pool", s16h[:, :, 15:528], s8h[:, :, 15:528], s8h[:, :, 7:520], ALU.add, [SH], [SH])
                        which = 0 if i == 0 else 1
                        for g, sh in enumerate((s2h, s4h, s8h, s16h)):
                            j, hh = g // 2, g % 2
                            prt = slice(hh * 64, (hh + 1) * 64)
                            B.tt("pool", ptmp[prt, j, :], sh[prt, j, 16:528], ic_t[prt, which, j, :], ALU.mult,
                                 [SH, W], [PTMP])
                        B.tt("pool", pooled[:], ptmp[:], ptile[:, :, 16:528], ALU.subtract, [PTMP, PT], [PLD])
                        B.cp("pool", ptile[:, :, 0:16], ptile[:, :, 512:528], [SH, PLD], [PT])
                        for j in range(2):
                            bk = rot.next()
                            B.mmg(ps[bk], [(wpbd[:, j, :], pooled[:, j, :])], [W, PLD], [PS[bk]])
                            B.ts1("dve", opl_st[:, j, :], ps[bk], pv_t[:, l, 32 + j:33 + j], ALU.mult,
                                  [PS[bk], CONST], [OPST])
                        B.dma("sp", "A.oplst", fm(OPL)[:, :, tsl], opl_st[:], [OPST], [])
                    if "gm" in parts:
                        gelu("pool", uf[:], UF, utmp[:], UTMP, usg[:], USG, ug[:], UG)
                        gelu("pool", gvf[:], GVF, gtmp[:], GTMP, gsg[:], GSG, gvg[:], GVG)
                        B.tt("dve", gtmp[:], gvg[:], gvg[:], ALU.mult, [GVG], [GTMP])
                        B.op("dve", lambda e: e.reduce_sum(out=ss[:], in_=gtmp[:], axis=AX.X), [GTMP], [SS])
                        B.act(ss[:], ss[:], AF.Ln, [SS], [SS], scale=1.0 / 256, bias=EPS)
                        B.act(ss[:], ss[:], AF.Exp, [SS], [SS], scale=-0.5)
                        for b in range(4):
                            B.stt("dve", gvn[:, b, :], gvg[:, b, :], ss[:, b:b + 1], gainrep[:], ALU.mult, ALU.mult,
                                  [GVG, SS, W], [GVN])
                        for b in range(4):
                            for g in range(4):
                                j, hh = g // 2, g % 2
                                B.mmg(ps_t[hh * 64:(hh + 1) * 64, 6 + j, b * 128:(b + 1) * 128],
                                      [(gvn[:, b, g * 64:(g + 1) * 64], wsT[:, g, :])], [GVN, W], [PS[6], PS[7]])
                        for b in range(4):
                            B.tt("dve", mtmp[:, :, b * 128:(b + 1) * 128], ps_t[:, 6:8, b * 128:(b + 1) * 128], brep[:],
                                 ALU.add, [PS[6], PS[7], W], [MTMP])
                        B.tt("pool", ogm_st[:], mtmp[:], ug[:], ALU.mult, [MTMP, UG], [OGST])
                        B.dma("sp", "A.ogmst", fm(OGM)[:, :, tsl], ogm_st[:], [OGST], [])
            B.barrier()

        def stage_B(l):
            NG = T // 512
            with ExitStack() as st:
                def sb(n, s, d):
                    return st.enter_context(nc.sbuf_tensor(f"B{l}_{n}", s, d))
                maskL_t = sb("maskL", [128, 4, 512], BF16)
                MK = Buf()
                B.dma("pool", "w", maskL_t[:], c_maskL, [], [MK])
                B.group_done("w", [MK])
                ktp = [sb(f"ktp{i}", [128, T], BF16) for i in range(2)]
                vp = [sb(f"vp{i}", [128, T // 128, 128], BF16) for i in range(2)]
                KV = [Buf(), Buf()]
                qs = [sb(f"q{i}", [128, 512], BF16) for i in range(2)]
                QS = [Buf(), Buf()]
                ex = [sb(f"ex{i}", [128, 2, 512], F32) for i in range(2)]
                EX = [Buf(), Buf()]
                spt = [sb(f"sp{i}", [128, 2, 512], BF16) for i in range(2)]
                SPT = [Buf(), Buf()]
                at = [sb(f"at{i}", [128, 2, 512], BF16) for i in range(2)]
                AT = [Buf(), Buf()]
                racc = sb("racc", [128, 2, 512], F32); RACC = Buf()
                racc16 = [sb(f"racc16_{i}", [128, 2, 512], BF16) for i in range(2)]
                RACC16 = [Buf(), Buf()]
                ost = [sb(f"ost{i}", [128, 512], BF16) for i in range(2)]
                OST = [Buf(), Buf()]
                zb = [(0, 1), (2, 3)]
                sbk = (4, 5)
                ob = [6, 7]
                it = 0
                gi = 0
                for hp in range(4):
                    kt, vv, KVb = ktp[hp % 2], vp[hp % 2], KV[hp % 2]
                    B.dma("sp", f"B.kt{hp % 2}", kt[:], KT[hp * 128:(hp + 1) * 128, :], [], [KVb])
                    B.dma("sp", f"B.kt{hp % 2}", vv[:],
                          Vd.rearrange("(n p) c -> p n c", p=128)[:, :, hp * 128:(hp + 1) * 128], [], [KVb])
                    for g in range(NG):
                        q, Q = qs[gi % 2], QS[gi % 2]
                        o_bank = ob[gi % 2]
                        osb_t, OSb = ost[gi % 2], OST[gi % 2]
                        gi += 1
                        B.dma("sp", f"B.q{gi % 2}", q[:], QT[hp * 128:(hp + 1) * 128, g * 512:(g + 1) * 512], [], [Q])
                        nkb = 4 * g + 4
                        first = True
                        B.op("dve", lambda e: e.memset(racc[:], 0.0), [], [RACC])
                        for kb in range(nkb - 1, -1, -1):
                            d = kb - 4 * g
                            c0 = d * 128 if d in (1, 2) else 0
                            cs = slice(c0, 512)
                            z0, z1 = zb[it % 2]
                            e_t, E = ex[it % 2], EX[it % 2]
                            s_t, S = spt[it % 2], SPT[it % 2]
                            a_t, A = at[it % 2], AT[it % 2]
                            r16, R16 = racc16[it % 2], RACC16[it % 2]
                            r16p, R16p = racc16[(it + 1) % 2], RACC16[(it + 1) % 2]
                            it += 1
                            ksl = slice(kb * 128, (kb + 1) * 128)
                            for h in range(2):
                                pr = slice(h * 64, (h + 1) * 64)
                                B.mmg(ps[(z0, z1)[h]][:, cs], [(kt[pr, ksl], q[pr, cs])], [KVb, Q], [PS[(z0, z1)[h]]])
                            zz = ps_t[:, z0:z0 + 2, cs]
                            B.act(e_t[:, :, cs], zz, AF.Exp, [PS[z0], PS[z1]], [E])
                            B.act(s_t[:, :, cs], e_t[:, :, cs], AF.Ln, [E], [S], bias=1.0)
                            if d >= 0:
                                B.tt("dve", s_t[:, :, cs], s_t[:, :, cs],
                                     maskL_t[:, d:d + 1, cs].to_broadcast([128, 2, 512 - c0]), ALU.mult,
                                     [S, MK], [S])
                            for h in range(2):
                                pr = slice(h * 64, (h + 1) * 64)
                                pairs = [(negtri_t[:], s_t[:, h, cs])]
                                rds = [S, CONST, KVb, Q]
                                if not first:
                                    pairs.append((nones_t[:], r16p[:, h, cs]))
                                    rds.append(R16p)
                                pairs.append((kt[pr, ksl], q[pr, cs]))
                                B.mmg(ps[sbk[h]][:, cs], pairs, rds, [PS[sbk[h]]])
                            B.act(a_t[:, :, cs], ps_t[:, sbk[0]:sbk[0] + 2, cs], AF.Exp, [PS[sbk[0]], PS[sbk[1]]], [A])
                            if d >= 0:
                                B.tt("pool", a_t[:, :, cs], a_t[:, :, cs],
                                     maskL_t[:, d:d + 1, cs].to_broadcast([128, 2, 512 - c0]), ALU.mult,
                                     [A, MK], [A])
                            for h in range(2):
                                fn = (lambda e, h=h, a_t=a_t, cs=cs, kb=kb, first=first, o_bank=o_bank, vv=vv:
                                      e.matmul(ps_t[h * 64:(h + 1) * 64, o_bank, cs], vv[:, kb, h * 64:(h + 1) * 64],
                                               a_t[:, h, cs], start=first, stop=(kb == 0)))
                                B.op("pe", fn, [A, KVb], [PS[o_bank]])
                            if kb > 0:
                                B.tt("dve", racc[:, :, cs], racc[:, :, cs], s_t[:, :, cs], ALU.add, [S, RACC], [RACC])
                                B.cp("pool", r16[:], racc[:], [RACC], [R16])
                            first = False
                        B.cp("dve", osb_t[:], ps[o_bank], [PS[o_bank]], [OSb])
                        B.dma("sp", f"B.ost{gi % 2}", OSB[hp * 128:(hp + 1) * 128, g * 512:(g + 1) * 512], osb_t[:],
                              [OSb], [])
            B.barrier()

        def post_norm_residual(xt, XT, ysb, YSB, sq, SQ, rs, RS, tmp, TMP, l, gcol0, ncols, msbank):
            B.mmg(ps[msbank][:, 0:ncols], [(ones_t[:], sq[:, c, :]) for c in range(8)], [SQ, CONST], [PS[msbank]])
            B.act(rs[:], ps[msbank][:, 0:ncols], AF.Ln, [PS[msbank]], [RS], scale=1.0 / D, bias=EPS)
            B.act(rs[:], rs[:], AF.Exp, [RS], [RS], scale=-0.5)
            for c in range(8):
                B.stt("dve", tmp[c % 2][:], ysb[:, c, :], pv_t[:, l, gcol0 + c:gcol0 + c + 1], rs[:],
                      ALU.mult, ALU.mult, [YSB, RS, CONST], [TMP[c % 2]])
                B.tt("pool", xt[:, c, :], xt[:, c, :], tmp[c % 2][:], ALU.add, [TMP[c % 2], XT], [XT])

        def stage_D(l, xsrc):
            NT = DBG.get("NTD", T // 512)
            with ExitStack() as st:
                def sb(n, s, d):
                    return st.enter_context(nc.sbuf_tensor(f"D{l}_{n}", s, d))
                wscr_off[0] = 0
                wg = sb("wg", [128, 8, 3072], BF16)
                wbr = sb("wbr", [128, 8, 1024], BF16)
                wo = sb("wo", [128, 8, 1024], BF16)
                W = Buf()
                wv = w_in[l].rearrange("(c p) n -> p c n", p=128)
                for c in range(8):
                    for hf in range(2):
                        wload(wg[:, c, hf * 1536:(hf + 1) * 1536],
                              wv[:, c, 2304 + hf * 1536:2304 + (hf + 1) * 1536], [128, 1536], W)
                for c in range(4):
                    wload(wbr[:, c, :], w_br_sb[l, c * 128:(c + 1) * 128, :], [128, 1024], W)
                for c in range(2):
                    wload(wbr[:, 4 + c, :], w_br_pool[l, c * 128:(c + 1) * 128, :], [128, 1024], W)
                    wload(wbr[:, 6 + c, :], w_br_gm[l, c * 128:(c + 1) * 128, :], [128, 1024], W)
                for c in range(8):
                    wload(wo[:, c, :], w_out[l, c * 128:(c + 1) * 128, :], [128, 1024], W)
                xts = [sb(f"xt{i}", [128, 8, 512], F32) for i in range(1)] * 2
                XTS = [Buf()] * 2
                brin = [sb(f"brin{i}", [128, 8, 512], BF16) for i in range(1)] * 2
                BRIN = [Buf()] * 2
                sq = sb("sq", [128, 8, 512], BF16); SQ = Buf()
                hT = sb("hT", [128, 8, 512], BF16); HT = Buf()
                rs = sb("rs", [128, 512], F32); RS = Buf()
                gsb = [sb(f"gsb{i}", [128, 512], BF16) for i in range(3)]
                GSB = [Buf() for _ in range(3)]
                t0 = sb("t0", [128, 512], F32); T0 = Buf()
                t1 = sb("t1", [128, 512], F32); T1 = Buf()
                t2 = sb("t2", [128, 512], F32); T2 = Buf()
                merged = sb("merged", [128, 8, 512], BF16); MG = Buf()
                ysb = sb("ysb", [128, 8, 512], F32); YSB = Buf()
                tmp = [sb(f"tmp{i}", [128, 512], F32) for i in range(2)]; TMP = [Buf(), Buf()]
                xv = fm(xsrc)
                yv = fm(yT)
                yrot = Rot([6, 7])
                for i in range(NT):
                    tsl = slice(i * 512, (i + 1) * 512)
                    xt, XT = xts[i % 2], XTS[i % 2]
                    bi, BI = brin[i % 2], BRIN[i % 2]
                    B.dma("sp", f"D.xt{i % 2}", xt[:], xv[:, :, tsl], [], [XT])
                    B.dma("sp", f"D.br{i % 2}", bi[:, 0:4, :], fm(OSB)[:, :, tsl], [], [BI])
                    B.dma("sp", f"D.br{i % 2}", bi[:, 4:6, :], fm(OPL)[:, :, tsl], [], [BI])
                    B.dma("sp", f"D.br{i % 2}", bi[:, 6:8, :], fm(OGM)[:, :, tsl], [], [BI])
                    norm_tile(xt, XT, sq, SQ, hT, HT, rs, RS, 0, l, 512, 6)
                    for c in range(8):
                        csl = slice(c * 128, (c + 1) * 128)
                        for b in range(3):
                            B.mmg(ps[b], [(wg[:, k, b * 1024 + c * 128:b * 1024 + (c + 1) * 128], hT[:, k, :])
                                          for k in range(8)], [W, HT], [PS[b]])
                            B.act(gsb[b][:], ps[b], AF.Sigmoid, [PS[b]], [GSB[b]])
                        B.mmg(ps[3], [(wbr[:, k, csl], bi[:, k, :]) for k in range(0, 4)], [W, BI], [PS[3]])
                        B.mmg(ps[4], [(wbr[:, k, csl], bi[:, k, :]) for k in range(4, 6)], [W, BI], [PS[4]])
                        B.mmg(ps[5], [(wbr[:, k, csl], bi[:, k, :]) for k in range(6, 8)], [W, BI], [PS[5]])
                        B.tt("dve", t0[:], ps[3], gsb[0][:], ALU.mult, [PS[3], GSB[0]], [T0])
                        B.tt("dve", t1[:], ps[4], gsb[1][:], ALU.mult, [PS[4], GSB[1]], [T1])
                        B.tt("dve", t2[:], ps[5], gsb[2][:], ALU.mult, [PS[5], GSB[2]], [T2])
                        B.tt("pool", t0[:], t0[:], t1[:], ALU.add, [T1, T0], [T0])
                        B.tt("pool", merged[:, c, :], t0[:], t2[:], ALU.add, [T0, T2], [MG])
                    for c in range(8):
                        csl = slice(c * 128, (c + 1) * 128)
                        bk = yrot.next()
                        B.mmg(ps[bk], [(wo[:, k, csl], merged[:, k, :]) for k in range(8)], [W, MG], [PS[bk]])
                        B.cp("dve", ysb[:, c, :], ps[bk], [PS[bk]], [YSB])
                        B.act(sq[:, c, :], ysb[:, c, :], AF.Square, [YSB], [SQ])
                    post_norm_residual(xt, XT, ysb, YSB, sq, SQ, rs, RS, tmp, TMP, l, 8, 512, 6)
                    B.dma("sp", f"D.xo{i % 2}", yv[:, :, tsl], xt[:], [XT], [])
            B.barrier()

        def stage_E(l):
            NW = 256
            NT = DBG.get("NTE", T // NW)
            with ExitStack() as st:
                def sb(n, s, d):
                    return st.enter_context(nc.sbuf_tensor(f"E{l}_{n}", s, d))
                wscr_off[0] = 0
                wf1 = sb("wf1", [128, 8, DFF], BF16)
                wf2 = sb("wf2", [128, 32, 1024], BF16)
                W = Buf()
                w1v = w_ff_in[l].rearrange("(c p) n -> p c n", p=128)
                w2v = w_ff_out[l].rearrange("(c p) n -> p c n", p=128)
                for c in range(8 if "nowf1" not in DBG else 0):
                    for hf in range(2):
                        wload(wf1[:, c, hf * 2048:(hf + 1) * 2048], w1v[:, c, hf * 2048:(hf + 1) * 2048], [128, 2048], W)
                for c in range(32 if "nowf2" not in DBG else 0):
                    wload(wf2[:, c, :], w_ff_out[l, c * 128:(c + 1) * 128, :], [128, 1024], W)
                xts = [sb(f"xt{i}", [128, 8, NW], F32) for i in range(2)]
                XTS = [Buf(), Buf()]
                sq = sb("sq", [128, 8, NW], BF16); SQ = Buf()
                hT = sb("hT", [128, 8, NW], BF16); HT = Buf()
                rs = sb("rs", [128, NW], F32); RS = Buf()
                rl = [sb(f"rl{i}", [128, 2, NW], F32) for i in range(2)]
                RL = [Buf(), Buf()]
                aT = sb("aT", [128, 32, NW], BF16); ATB = Buf()
                ysb = sb("ysb", [128, 8, NW], F32); YSB = Buf()
                tmp = [sb(f"tmp{i}", [128, NW], F32) for i in range(2)]; TMP = [Buf(), Buf()]
                yv = fm(yT)
                arot = Rot([0, 1, 2, 3])
                yrot = Rot([4, 5])
                ri = 0
                for i in range(NT):
                    tsl = slice(i * NW, (i + 1) * NW)
                    xt, XT = xts[i % 2], XTS[i % 2]
                    B.dma("sp", f"E.xt{i % 2}", xt[:], yv[:, :, tsl], [], [XT])
                    EP = DBG.get("Eparts", "nabps")
                    if "n" in EP:
                        norm_tile(xt, XT, sq, SQ, hT, HT, rs, RS, 16, l, NW, 6)
                    for j2 in (range(16) if "a" in EP else []):
                        bk = arot.next()
                        for jj in range(2):
                            j = j2 * 2 + jj
                            B.mmg(ps[bk][:, jj * NW:(jj + 1) * NW],
                                  [(wf1[:, k, j * 128:(j + 1) * 128], hT[:, k, :]) for k in range(8)],
                                  [W, HT], [PS[bk]])
                        r, R = rl[ri % 2], RL[ri % 2]
                        ri += 1
                        B.act(r[:], ps[bk].rearrange("p (a n) -> p a n", a=2), AF.Relu, [PS[bk]], [R])
                        B.tt("pool", aT[:, j2 * 2:j2 * 2 + 2, :], r[:], r[:], ALU.mult, [R], [ATB])
                    for c2 in (range(4) if "b" in EP else []):
                        bk = yrot.next()
                        for cc in range(2):
                            c = c2 * 2 + cc
                            B.mmg(ps[bk][:, cc * NW:(cc + 1) * NW],
                                  [(wf2[:, k, c * 128:(c + 1) * 128], aT[:, k, :]) for k in range(32)],
                                  [W, ATB], [PS[bk]])
                        pv2 = ps[bk].rearrange("p (a n) -> p a n", a=2)
                        B.cp("dve", ysb[:, c2 * 2:c2 * 2 + 2, :], pv2, [PS[bk]], [YSB])
                        B.act(sq[:, c2 * 2:c2 * 2 + 2, :], ysb[:, c2 * 2:c2 * 2 + 2, :], AF.Square, [YSB], [SQ])
                    if "p" in EP:
                        post_norm_residual(xt, XT, ysb, YSB, sq, SQ, rs, RS, tmp, TMP, l, 24, NW, 6)
                    if "s" in EP:
                        B.dma("sp", f"E.xo{i % 2}", yv[:, :, tsl], xt[:], [XT], [])
            B.barrier()

        for l in range(nlayers):
            xsrc = xin if l == 0 else yT
            if "A" in stages:
                stage_A(l, xsrc)
            if "B" in stages:
                stage_B(l)
            if "D" in stages:
                stage_D(l, xsrc)
            if "E" in stages:
                stage_E(l)
        B.final()
        with nc.Block() as block:
            B.emit(block)
    return nc


def _consts():
    p = np.arange(128)
    tril = (p[:, None] <= p[None, :]).astype(np.float32)
    negtri = -(p[:, None] >= p[None, :]).astype(np.float32)
    maskL = np.zeros((128, 4, 512), np.float32)
    qidx = np.arange(512)
    for d in range(4):
        kidx = d * 128 + p
        maskL[:, d, :] = (kidx[:, None] < qidx[None, :]).astype(np.float32)
    invcnt = np.zeros((2, 256, 512), np.float32)
    pos = np.arange(512, dtype=np.float32)
    for g, w in enumerate((2, 4, 8, 16)):
        invcnt[0, g * 64:(g + 1) * 64, :] = 1.0 / np.minimum(pos + 1.0, float(w))[None, :]
        invcnt[1, g * 64:(g + 1) * 64, :] = 1.0 / float(w)
    return tril, negtri, maskL, invcnt


_NC_CACHE = {}


def kernel(x, w_in, w_pool, pool_scale, gm_gain, w_spatial, b_spatial, w_br_sb, w_br_pool,
           w_br_gm, w_out, g_mix_pre, g_mix_post, g_ff_pre, g_ff_post, w_ff_in, w_ff_out):
    f = lambda a: np.ascontiguousarray(np.asarray(a, dtype=np.float32))
    x = f(x)
    tril, negtri, maskL, invcnt = _consts()
    def pc(v, n):
        return f(v).reshape(DEPTH, n, 128).transpose(0, 2, 1)
    pvec = np.concatenate([pc(g_mix_pre, 8), pc(g_mix_post, 8), pc(g_ff_pre, 8), pc(g_ff_post, 8),
                           pc(pool_scale, 2)], axis=2)
    shared = {
        "w_in": f(w_in), "w_pool": f(w_pool), "gm_gain": f(gm_gain),
        "w_spT": np.ascontiguousarray(f(w_spatial).transpose(0, 1, 3, 2)),
        "b_spatial": f(b_spatial), "w_br_sb": f(w_br_sb), "w_br_pool": f(w_br_pool),
        "w_br_gm": f(w_br_gm), "w_out": f(w_out), "w_ff_in": f(w_ff_in), "w_ff_out": f(w_ff_out),
        "pvec": np.ascontiguousarray(pvec), "c_tril": tril, "c_negtri": negtri, "c_maskL": maskL,
        "c_invcnt": invcnt,
    }
    if "nc" not in _NC_CACHE:
        _NC_CACHE["nc"] = build()
    nc = _NC_CACHE["nc"]
    in_maps = []
    for c in range(NCORES):
        m = dict(shared)
        m["xT"] = np.ascontiguousarray(x[c].T)
        in_maps.append(m)
    res = run_bass_kernel_spmd(nc, in_maps, core_ids=list(range(NCORES)))
    out = np.stack([np.ascontiguousarray(res.results[c]["yT"].T) for c in range(NCORES)], axis=0)
    return out.astype(np.float32)
```

```python
import numpy as np
from contextlib import ExitStack
import concourse.bass as bass
import concourse.mybir as mybir
from concourse.bass_utils import run_bass_kernel_spmd

F32 = mybir.dt.float32
BF16 = mybir.dt.bfloat16
AF = mybir.ActivationFunctionType
ALU = mybir.AluOpType
AX = mybir.AxisListType

D = 1024
T = 8192
DEPTH = 4
DIN = 5376
DFF = 4096
EPS = 1e-6
SEM_ROT = 30000
NCORES = 4
GELU_C = 1.5957691216057308
DBG = {}


class Buf:
    __slots__ = ("wr", "rd")

    def __init__(self):
        self.wr = {}
        self.rd = {}


class Builder:
    def __init__(self, nc):
        self.nc = nc
        self.q = {k: [] for k in ("pe", "act", "dve", "pool", "sp")}
        self.sems = []
        self.esem = {}
        self.ecnt = {}
        self.dsem = {}
        self.dcnt = {}
        self.last = {}
        self.waited = {k: {} for k in self.q}
        self.pend = {}
        self.sem_eng = {}

    def newsem(self):
        self.sems.append(self.nc.alloc_semaphore(f"sm{len(self.sems)}"))
        return len(self.sems) - 1

    def _push(self, eng, fn, waits, sem, inc):
        pend = self.pend.pop(eng, None)
        if pend:
            for k, v in pend.items():
                if waits.get(k, 0) < v:
                    waits[k] = v
        ws = []
        wd = self.waited[eng]
        for k, v in waits.items():
            if wd.get(k, 0) >= v:
                continue
            if eng == "pe" and self.sem_eng.get(k) == "pe":
                continue
            wd[k] = v
            ws.append((k, v))
        self.q[eng].append((ws, fn, sem, inc))

    @staticmethod
    def _merge(dst, src):
        for k, v in src.items():
            if dst.get(k, 0) < v:
                dst[k] = v

    def _deps(self, reads, writes):
        w = {}
        for b in reads:
            self._merge(w, b.wr)
        for b in writes:
            self._merge(w, b.wr)
            self._merge(w, b.rd)
        return w

    def _mark(self, reads, writes, sem, val):
        for b in reads:
            if b.rd.get(sem, 0) < val:
                b.rd[sem] = val
        for b in writes:
            b.wr = {sem: val}
            b.rd = {}

    def op(self, eng, fn, reads=(), writes=()):
        if eng not in self.esem or self.ecnt[eng] >= SEM_ROT:
            self.esem[eng] = self.newsem()
            self.sem_eng[self.esem[eng]] = eng
            self.ecnt[eng] = 0
        w = self._deps(reads, writes)
        self.ecnt[eng] += 1
        sem = self.esem[eng]
        val = self.ecnt[eng]
        self._push(eng, fn, w, sem, 1)
        self._mark(reads, writes, sem, val)
        self.last[("e", eng)] = (sem, val)

    def raw(self, eng, fn):
        self.q[eng].append(([], fn, None, 0))

    def dma(self, eng, key, out, in_, reads=(), writes=()):
        if key not in self.dsem:
            self.dsem[key] = self.newsem()
            self.dcnt[key] = 0
        w = self._deps(reads, writes)
        self.dcnt[key] += 16
        sem = self.dsem[key]
        val = self.dcnt[key]
        self._push(eng, lambda e: e.dma_start(out=out, in_=in_), w, sem, 16)
        self._mark(reads, writes, sem, val)
        self.last[("d", key)] = (sem, val)

    def group_done(self, key, bufs):
        sem = self.dsem[key]
        val = self.dcnt[key]
        for b in bufs:
            b.wr = {sem: val}

    def barrier(self):
        toks = {}
        for (sem, val) in self.last.values():
            if toks.get(sem, 0) < val:
                toks[sem] = val
        for eng in self.q:
            self.pend[eng] = dict(toks)

    def final(self):
        self.barrier()
        for eng in self.q:
            self._push(eng, lambda e: e.nop(), {}, None, 0)

    def emit(self, block):
        decs = {"pe": block.tensor, "act": block.scalar, "dve": block.vector,
                "pool": block.gpsimd, "sp": block.sync}
        for k, dec in decs.items():
            def f(e, k=k):
                sems = self.sems
                for ws, fn, sem, inc in self.q[k]:
                    for (wi, wv) in ws:
                        e.wait_ge(sems[wi], wv)
                    ins = fn(e)
                    if sem is not None:
                        ins.then_inc(sems[sem], inc)
            dec(f)

    def mmg(self, out, pairs, reads, writes, start=True, stop=True):
        n = len(pairs)
        for i, (l, r) in enumerate(pairs):
            st = start and i == 0
            sp = stop and i == n - 1
            fn = (lambda e, l=l, r=r, st=st, sp=sp: e.matmul(out, l, r, start=st, stop=sp))
            if n == 1:
                self.op("pe", fn, reads, writes)
            elif i == 0:
                w = self._deps(reads, writes)
                self._push("pe", fn, w, None, 0)
            elif i == n - 1:
                self.op("pe", fn, reads, writes)
            else:
                self.raw("pe", fn)

    def act(self, out, in_, func, reads, writes, scale=1.0, bias=0.0):
        self.op("act", lambda e: e.activation(out=out, in_=in_, func=func, bias=bias, scale=scale),
                reads, writes)

    def tt(self, eng, out, in0, in1, op, reads, writes):
        self.op(eng, lambda e: e.tensor_tensor(out=out, in0=in0, in1=in1, op=op), reads, writes)

    def ts(self, eng, out, in0, s1, s2, op0, op1, reads, writes):
        self.op(eng, lambda e: e.tensor_scalar(out=out, in0=in0, scalar1=s1, scalar2=s2, op0=op0, op1=op1),
                reads, writes)

    def ts1(self, eng, out, in0, s1, op0, reads, writes):
        self.op(eng, lambda e: e.tensor_scalar(out=out, in0=in0, scalar1=s1, scalar2=None, op0=op0),
                reads, writes)

    def stt(self, eng, out, in0, scalar, in1, op0, op1, reads, writes):
        self.op(eng, lambda e: e.scalar_tensor_tensor(out=out, in0=in0, scalar=scalar, in1=in1,
                                                      op0=op0, op1=op1), reads, writes)

    def cp(self, eng, out, in_, reads, writes):
        self.op(eng, lambda e: e.tensor_copy(out=out, in_=in_), reads, writes)


class Rot:
    def __init__(self, items):
        self.items = items
        self.i = 0

    def next(self):
        it = self.items[self.i % len(self.items)]
        self.i += 1
        return it


def build(nlayers=DEPTH, dbg=False, stages="ABDE"):
    nc = bass.Bass("TRN2", target_bir_lowering=False)

    def din(name, shape):
        return nc.dram_tensor(name, shape, F32, kind="ExternalInput").ap()

    xin = din("xT", [D, T])
    w_in = din("w_in", [DEPTH, D, DIN])
    w_pool = din("w_pool", [DEPTH, 4, 64, 64])
    gm_gain = din("gm_gain", [DEPTH, 256])
    w_spT = din("w_spT", [DEPTH, 4, 128, 128])
    b_spatial = din("b_spatial", [DEPTH, 4, 128])
    w_br_sb = din("w_br_sb", [DEPTH, 512, 1024])
    w_br_pool = din("w_br_pool", [DEPTH, 256, 1024])
    w_br_gm = din("w_br_gm", [DEPTH, 256, 1024])
    w_out = din("w_out", [DEPTH, 1024, 1024])
    w_ff_in = din("w_ff_in", [DEPTH, 1024, DFF])
    w_ff_out = din("w_ff_out", [DEPTH, DFF, 1024])
    pvec = din("pvec", [DEPTH, 128, 34])
    c_tril = din("c_tril", [128, 128])
    c_negtri = din("c_negtri", [128, 128])
    c_maskL = din("c_maskL", [128, 4, 512])
    c_invcnt = din("c_invcnt", [2, 256, 512])

    kind_s = "ExternalOutput" if dbg else "Internal"
    yT = nc.dram_tensor("yT", [D, T], F32, kind="ExternalOutput").ap()
    QT = nc.dram_tensor("QT", [512, T], BF16, kind=kind_s).ap()
    KT = nc.dram_tensor("KT", [512, T], BF16, kind=kind_s).ap()
    Vd = nc.dram_tensor("Vd", [T, 512], BF16, kind=kind_s).ap()
    OSB = nc.dram_tensor("OSB", [512, T], BF16, kind=kind_s).ap()
    OPL = nc.dram_tensor("OPL", [256, T], BF16, kind=kind_s).ap()
    OGM = nc.dram_tensor("OGM", [256, T], BF16, kind=kind_s).ap()

    B = Builder(nc)

    def fm(ap):
        return ap.rearrange("(c p) t -> p c t", p=128)

    with ExitStack() as top:
        ps_t = top.enter_context(nc.psum_tensor("ps", [128, 8, 512], F32))
        ps = [ps_t[:, k, :] for k in range(8)]
        PS = [Buf() for _ in range(8)]

        ones_t = top.enter_context(nc.sbuf_tensor("ones", [128, 128], BF16))
        nones_t = top.enter_context(nc.sbuf_tensor("nones", [128, 128], BF16))
        negtri_t = top.enter_context(nc.sbuf_tensor("negtri", [128, 128], BF16))
        tril_t = top.enter_context(nc.sbuf_tensor("tril", [128, 128], F32))
        pv_t = top.enter_context(nc.sbuf_tensor("pv", [128, DEPTH, 34], F32))
        CONST = Buf()
        B.op("pool", lambda e: e.memset(ones_t[:], 1.0), [], [CONST])
        B.op("pool", lambda e: e.memset(nones_t[:], -1.0), [], [CONST])
        B.dma("pool", "w", negtri_t[:], c_negtri, [], [CONST])
        B.dma("pool", "w", tril_t[:], c_tril, [], [CONST])
        B.dma("pool", "w", pv_t[:], pvec.rearrange("l p c -> p l c"), [], [CONST])
        B.group_done("w", [CONST])
        B.barrier()

        wst = [top.enter_context(nc.sbuf_tensor(f"wst{i}", [128, 512], F32)) for i in range(2)]
        WST = [Buf(), Buf()]
        wscr_off = [0]
        wl_i = [0]

        def wload(dst, src, shape, Wb):
            n = shape[1]
            if n > 512:
                for o in range(0, n, 512):
                    wload(dst[:, o:o + 512], src[:, o:o + 512], [128, 512], Wb)
                return
            k = wl_i[0] % 2
            wl_i[0] += 1
            B.dma("sp", f"wst{k}", wst[k][:, 0:n], src, [], [WST[k]])
            B.cp("dve" if k == 0 else "pool", dst, wst[k][:, 0:n], [WST[k]], [Wb])

        def norm_tile(xt, XT, sq, SQ, hT, HT, rs, RS, gcol0, l, ncols, msbank):
            B.act(sq[:], xt[:], AF.Square, [XT], [SQ])
            B.mmg(ps[msbank][:, 0:ncols], [(ones_t[:], sq[:, c, :]) for c in range(8)],
                  [SQ, CONST], [PS[msbank]])
            B.act(rs[:], ps[msbank][:, 0:ncols], AF.Ln, [PS[msbank]], [RS], scale=1.0 / D, bias=EPS)
            B.act(rs[:], rs[:], AF.Exp, [RS], [RS], scale=-0.5)
            for c in range(8):
                B.stt("dve", hT[:, c, :], xt[:, c, :], pv_t[:, l, gcol0 + c:gcol0 + c + 1], rs[:],
                      ALU.mult, ALU.mult, [XT, RS, CONST], [HT])

        def gelu(eng2, x, X, tmp, TMP, sg, SG, out, OUT):
            B.tt("dve", tmp, x, x, ALU.mult, [X], [TMP])
            B.ts("dve", tmp, tmp, 0.044715, 1.0, ALU.mult, ALU.add, [TMP], [TMP])
            B.tt("dve", tmp, tmp, x, ALU.mult, [TMP, X], [TMP])
            B.act(sg, tmp, AF.Sigmoid, [TMP], [SG], scale=GELU_C)
            B.tt(eng2, out, x, sg, ALU.mult, [X, SG], [OUT])

        def stage_A(l, xsrc):
            NT = DBG.get("NT", T // 512)
            parts = DBG.get("parts", "qk,pu,v,gv,pool,gm")
            with ExitStack() as st:
                def sb(n, s, d):
                    return st.enter_context(nc.sbuf_tensor(f"A{l}_{n}", s, d))
                wa = sb("wa", [128, 8, 2304], BF16)
                wpbd = sb("wpbd", [128, 2, 128], BF16)
                wsT32 = sb("wsT32", [128, 4, 128], F32)
                wsT = sb("wsT", [128, 4, 128], BF16)
                brep = sb("brep", [128, 2, 128], F32)
                gainrep = sb("gainrep", [128, 256], F32)
                ic_t = sb("ic", [128, 2, 2, 512], F32)
                W = Buf()
                for wch in range(2):
                    B.dma("pool", "w", ic_t[:, wch, :, :], c_invcnt[wch].rearrange("(j p) t -> p j t", p=128), [], [W])
                B.op("pool", lambda e: e.memset(wpbd[:], 0.0), [], [W])
                wv = w_in[l].rearrange("(c p) n -> p c n", p=128)
                for c in range(8):
                    B.dma("pool", "w", wa[:, c, :], wv[:, c, 0:2304], [], [W])
                for g in range(4):
                    j, hh = g // 2, g % 2
                    B.dma("pool", "w", wpbd[hh * 64:(hh + 1) * 64, j, hh * 64:(hh + 1) * 64],
                          w_pool[l, g], [], [W])
                    B.dma("pool", "w", brep[hh * 64:(hh + 1) * 64, j, :],
                          b_spatial[l, g, :].partition_broadcast(64), [], [W])
                B.dma("pool", "w", wsT32[:], w_spT[l].rearrange("g p t -> p g t"), [], [W])
                B.dma("pool", "w", gainrep[:], gm_gain[l, :].partition_broadcast(128), [], [W])
                B.group_done("w", [W])
                for g in range(4):
                    B.tt("dve", wsT[:, g, :], wsT32[:, g, :], tril_t[:], ALU.mult, [W, CONST], [W])

                xts = [sb(f"xt{i}", [128, 8, 512], F32) for i in range(2)]
                XTS = [Buf(), Buf()]
                sq = sb("sq", [128, 8, 512], BF16); SQ = Buf()
                hT = sb("hT", [128, 8, 512], BF16); HT = Buf()
                rs = sb("rs", [128, 512], F32); RS = Buf()
                qk_st = sb("qkst", [128, 8, 512], BF16); QST = Buf(); KST = Buf()
                v_st = sb("vst", [128, 4, 512], BF16); VST = Buf()
                ptile = sb("ptile", [128, 2, 528], F32); PT = Buf()
                s2h = sb("s2h", [128, 2, 528], F32); s4h = sb("s4h", [128, 2, 528], F32)
                s8h = sb("s8h", [128, 2, 528], F32); s16h = sb("s16h", [128, 2, 528], F32)
                SH = Buf()
                ptmp = sb("ptmp", [128, 2, 512], F32); PTMP = Buf()
                pooled = sb("pooled", [128, 2, 512], BF16); PLD = Buf()
                opl_st = sb("oplst", [128, 2, 512], BF16); OPST = Buf()
                uf = sb("uf", [128, 2, 512], F32); UF = Buf()
                utmp = sb("utmp", [128, 2, 512], F32); UTMP = Buf()
                usg = sb("usg", [128, 2, 512], F32); USG = Buf()
                ug = sb("ug", [128, 2, 512], F32); UG = Buf()
                gvf = sb("gvf", [128, 4, 256], F32); GVF = Buf()
                gtmp = sb("gtmp", [128, 4, 256], F32); GTMP = Buf()
                gsg = sb("gsg", [128, 4, 256], F32); GSG = Buf()
                gvg = sb("gvg", [128, 4, 256], F32); GVG = Buf()
                ss = sb("ss", [128, 4], F32); SS = Buf()
                gvn = sb("gvn", [128, 4, 256], BF16); GVN = Buf()
                mtmp = sb("mtmp", [128, 2, 512], F32); MTMP = Buf()
                ogm_st = sb("ogmst", [128, 2, 512], BF16); OGST = Buf()
                B.op("pool", lambda e: e.memset(ptile[:], 0.0), [], [PT])

                rot = Rot([2, 3, 4, 5])
                xv = fm(xsrc)
                for i in range(NT):
                    tsl = slice(i * 512, (i + 1) * 512)
                    xt, XT = xts[i % 2], XTS[i % 2]
                    B.dma("sp", f"A.xt{i % 2}", xt[:], xv[:, :, tsl], [], [XT])
                    norm_tile(xt, XT, sq, SQ, hT, HT, rs, RS, 0, l, 512, 0)
                    for j in (range(8) if "qk" in parts else []):
                        bk = rot.next()
                        B.mmg(ps[bk], [(wa[:, c, j * 128:(j + 1) * 128], hT[:, c, :]) for c in range(8)],
                              [W, HT], [PS[bk]])
                        if j < 4:
                            B.act(qk_st[:, j, :], ps[bk], AF.Identity, [PS[bk]], [QST], scale=0.125)
                        else:
                            B.cp("dve", qk_st[:, j, :], ps[bk], [PS[bk]], [KST])
                    if "qk" in parts:
                        B.dma("sp", "A.qst", fm(QT)[:, :, tsl], qk_st[:, 0:4, :], [QST], [])
                        B.dma("sp", "A.kst", fm(KT)[:, :, tsl], qk_st[:, 4:8, :], [KST], [])
                    if "pu" in parts:
                        for j in range(2):
                            bk = rot.next()
                            B.mmg(ps[bk], [(wa[:, c, 1536 + j * 128:1536 + (j + 1) * 128], hT[:, c, :])
                                           for c in range(8)], [W, HT], [PS[bk]])
                            B.act(ptile[:, j, 16:528], ps[bk], AF.Identity, [PS[bk]], [PT])
                        for j in range(2):
                            bk = rot.next()
                            B.mmg(ps[bk], [(wa[:, c, 1792 + j * 128:1792 + (j + 1) * 128], hT[:, c, :])
                                           for c in range(8)], [W, HT], [PS[bk]])
                            B.act(uf[:, j, :], ps[bk], AF.Identity, [PS[bk]], [UF])
                    if "v" in parts:
                        for b in range(4):
                            bk = rot.next()
                            B.mmg(ps[bk], [(hT[:, c, b * 128:(b + 1) * 128], wa[:, c, 1024:1536])
                                           for c in range(8)], [W, HT], [PS[bk]])
                            B.cp("dve", v_st[:, b, :], ps[bk], [PS[bk]], [VST])
                        B.dma("sp", "A.vst", Vd.rearrange("(n p) c -> p n c", p=128)[:, i * 4:(i + 1) * 4, :],
                              v_st[:], [VST], [])
                    if "gv" in parts:
                        for b2 in range(2):
                            bk = rot.next()
                            for bb in range(2):
                                b = b2 * 2 + bb
                                B.mmg(ps[bk][:, bb * 256:(bb + 1) * 256],
                                      [(hT[:, c, b * 128:(b + 1) * 128], wa[:, c, 2048:2304]) for c in range(8)],
                                      [W, HT], [PS[bk]])
                            B.act(gvf[:, b2 * 2:b2 * 2 + 2, :], ps[bk].rearrange("p (a n) -> p a n", a=2), AF.Identity,
                                  [PS[bk]], [GVF])
                    if "pool" in parts:
                        B.tt("pool", s2h[:, :, 1:528], ptile[:, :, 1:528], ptile[:, :, 0:527], ALU.add, [PT], [SH])
                        B.tt("pool", s4h[:, :, 3:528], s2h[:, :, 3:528], s2h[:, :, 1:526], ALU.add, [SH], [SH])
                        B.tt("pool", s8h[:, :, 7:528], s4h[:, :, 7:528], s4h[:, :, 3:524], ALU.add, [SH], [SH])
                        B.tt("pool", s16h[:, :, 15:528], s8h[:, :, 15:528], s8h[:, :, 7:520], ALU.add, [SH], [SH])
                        which = 0 if i == 0 else 1
                        for g, sh in enumerate((s2h, s4h, s8h, s16h)):
                            j, hh = g // 2, g % 2
                            prt = slice(hh * 64, (hh + 1) * 64)
                            B.tt("pool", ptmp[prt, j, :], sh[prt, j, 16:528], ic_t[prt, which, j, :], ALU.mult,
                                 [SH, W], [PTMP])
                        B.tt("pool", pooled[:], ptmp[:], ptile[:, :, 16:528], ALU.subtract, [PTMP, PT], [PLD])
                        B.cp("pool", ptile[:, :, 0:16], ptile[:, :, 512:528], [SH, PLD], [PT])
                        for j in range(2):
                            bk = rot.next()
                            B.mmg(ps[bk], [(wpbd[:, j, :], pooled[:, j, :])], [W, PLD], [PS[bk]])
                            B.ts1("dve", opl_st[:, j, :], ps[bk], pv_t[:, l, 32 + j:33 + j], ALU.mult,
                                  [PS[bk], CONST], [OPST])
                        B.dma("sp", "A.oplst", fm(OPL)[:, :, tsl], opl_st[:], [OPST], [])
                    if "gm" in parts:
                        gelu("pool", uf[:], UF, utmp[:], UTMP, usg[:], USG, ug[:], UG)
                        gelu("pool", gvf[:], GVF, gtmp[:], GTMP, gsg[:], GSG, gvg[:], GVG)
                        B.tt("dve", gtmp[:], gvg[:], gvg[:], ALU.mult, [GVG], [GTMP])
                        B.op("dve", lambda e: e.reduce_sum(out=ss[:], in_=gtmp[:], axis=AX.X), [GTMP], [SS])
                        B.act(ss[:], ss[:], AF.Ln, [SS], [SS], scale=1.0 / 256, bias=EPS)
                        B.act(ss[:], ss[:], AF.Exp, [SS], [SS], scale=-0.5)
                        for b in range(4):
                            B.stt("dve", gvn[:, b, :], gvg[:, b, :], ss[:, b:b + 1], gainrep[:], ALU.mult, ALU.mult,
                                  [GVG, SS, W], [GVN])
                        for b in range(4):
                            for g in range(4):
                                j, hh = g // 2, g % 2
                                B.mmg(ps_t[hh * 64:(hh + 1) * 64, 6 + j, b * 128:(b + 1) * 128],
                                      [(gvn[:, b, g * 64:(g + 1) * 64], wsT[:, g, :])], [GVN, W], [PS[6], PS[7]])
                        for b in range(4):
                            B.tt("dve", mtmp[:, :, b * 128:(b + 1) * 128], ps_t[:, 6:8, b * 128:(b + 1) * 128], brep[:],
                                 ALU.add, [PS[6], PS[7], W], [MTMP])
                        B.tt("pool", ogm_st[:], mtmp[:], ug[:], ALU.mult, [MTMP, UG], [OGST])
                        B.dma("sp", "A.ogmst", fm(OGM)[:, :, tsl], ogm_st[:], [OGST], [])
            B.barrier()

        def stage_B(l):
            NG = T // 512
            NS = 3
            with ExitStack() as st:
                def sb(n, s, d):
                    return st.enter_context(nc.sbuf_tensor(f"B{l}_{n}", s, d))
                maskL_t = sb("maskL", [128, 4, 512], BF16)
                MK = Buf()
                B.dma("pool", "w", maskL_t[:], c_maskL, [], [MK])
                B.group_done("w", [MK])
                ktp = [sb(f"ktp{i}", [128, T], BF16) for i in range(2)]
                vp = [sb(f"vp{i}", [128, T // 128, 128], BF16) for i in range(2)]
                KV = [Buf(), Buf()]
                qs = [sb(f"q{i}", [128, 512], BF16) for i in range(2)]
                QS = [Buf(), Buf()]
                ex = [sb(f"ex{i}", [128, 2, 512], F32) for i in range(NS)]
                EX = [Buf() for _ in range(NS)]
                spt = [sb(f"sp{i}", [128, 2, 512], BF16) for i in range(NS)]
                SPT = [Buf() for _ in range(NS)]
                at = [sb(f"at{i}", [128, 2, 512], BF16) for i in range(NS)]
                AT = [Buf() for _ in range(NS)]
                r16 = [sb(f"r16_{i}", [128, 2, 512], BF16) for i in range(NS)]
                R16 = [Buf() for _ in range(NS)]
                ost = [sb(f"ost{i}", [128, 512], BF16) for i in range(2)]
                OST = [Buf(), Buf()]
                zb = [(0, 1), (2, 3)]
                sbk = (4, 5)
                ob = [6, 7]
                its = []
                for hp in range(4):
                    for g in range(NG):
                        nkb = 4 * g + 4
                        for pos, kb in enumerate(range(nkb - 1, -1, -1)):
                            its.append(dict(hp=hp, g=g, kb=kb, d=kb - 4 * g, first=(pos == 0), last=(kb == 0),
                                            gi=hp * NG + g, pos=pos))
                N = len(its)
                Vv = Vd.rearrange("(n p) c -> p n c", p=128)

                def load_hp(hp):
                    if hp > 3:
                        return
                    sl = hp % 2
                    B.dma("sp", f"B.kt{sl}", ktp[sl][:], KT[hp * 128:(hp + 1) * 128, :], [], [KV[sl]])
                    B.dma("sp", f"B.kt{sl}", vp[sl][:], Vv[:, :, hp * 128:(hp + 1) * 128], [], [KV[sl]])

                def load_q(gi):
                    if gi >= 4 * NG:
                        return
                    hp, g = gi // NG, gi % NG
                    B.dma("sp", f"B.q{gi % 2}", qs[gi % 2][:], QT[hp * 128:(hp + 1) * 128, g * 512:(g + 1) * 512],
                          [], [QS[gi % 2]])

                def Zf(k):
                    if not 0 <= k < N:
                        return
                    I = its[k]
                    kt, KVb = ktp[I["hp"] % 2], KV[I["hp"] % 2]
                    q, Q = qs[I["gi"] % 2], QS[I["gi"] % 2]
                    ksl = slice(I["kb"] * 128, (I["kb"] + 1) * 128)
                    for h in range(2):
                        pr = slice(h * 64, (h + 1) * 64)
                        zk = zb[k % 2][h]
                        B.mmg(ps[zk], [(kt[pr, ksl], q[pr, :])], [KVb, Q], [PS[zk]])

                def Ef(k):
                    if not 0 <= k < N:
                        return
                    z0, z1 = zb[k % 2]
                    B.act(ex[k % NS][:], ps_t[:, z0:z0 + 2, :], AF.Exp, [PS[z0], PS[z1]], [EX[k % NS]])

                def Lf(k):
                    if not 0 <= k < N:
                        return
                    I = its[k]
                    B.act(spt[k % NS][:], ex[k % NS][:], AF.Ln, [EX[k % NS]], [SPT[k % NS]], bias=1.0)
                    if I["d"] >= 0:
                        d = I["d"]
                        B.tt("dve", spt[k % NS][:], spt[k % NS][:],
                             maskL_t[:, d:d + 1, :].to_broadcast([128, 2, 512]), ALU.mult,
                             [SPT[k % NS], MK], [SPT[k % NS]])

                def Rf(k):
                    if not 0 <= k < N:
                        return
                    I = its[k]
                    if I["last"]:
                        return
                    if I["first"]:
                        B.cp("dve", r16[k % NS][:], spt[k % NS][:], [SPT[k % NS]], [R16[k % NS]])
                    else:
                        B.tt("dve", r16[k % NS][:], r16[(k - 1) % NS][:], spt[k % NS][:], ALU.add,
                             [SPT[k % NS], R16[(k - 1) % NS]], [R16[k % NS]])

                def Sf(k):
                    I = its[k]
                    kt, KVb = ktp[I["hp"] % 2], KV[I["hp"] % 2]
                    q, Q = qs[I["gi"] % 2], QS[I["gi"] % 2]
                    ksl = slice(I["kb"] * 128, (I["kb"] + 1) * 128)
                    for h in range(2):
                        pr = slice(h * 64, (h + 1) * 64)
                        pairs = [(negtri_t[:], spt[k % NS][:, h, :])]
                        rds = [SPT[k % NS], CONST, KVb, Q]
                        if not I["first"]:
                            pairs.append((nones_t[:], r16[(k - 1) % NS][:, h, :]))
                            rds.append(R16[(k - 1) % NS])
                        pairs.append((kt[pr, ksl], q[pr, :]))
                        B.mmg(ps[sbk[h]], pairs, rds, [PS[sbk[h]]])

                def Xf(k):
                    I = its[k]
                    B.act(at[k % NS][:], ps_t[:, sbk[0]:sbk[0] + 2, :], AF.Exp, [PS[sbk[0]], PS[sbk[1]]], [AT[k % NS]])
                    if I["d"] >= 0:
                        d = I["d"]
                        B.tt("pool", at[k % NS][:], at[k % NS][:],
                             maskL_t[:, d:d + 1, :].to_broadcast([128, 2, 512]), ALU.mult,
                             [AT[k % NS], MK], [AT[k % NS]])

                def AVf(k):
                    if not 0 <= k < N:
                        return
                    I = its[k]
                    vv, KVb = vp[I["hp"] % 2], KV[I["hp"] % 2]
                    o_bank = ob[I["gi"] % 2]
                    a_t = at[k % NS]
                    for h in range(2):
                        fn = (lambda e, h=h, a_t=a_t, kb=I["kb"], first=I["first"], last=I["last"], o_bank=o_bank, vv=vv:
                              e.matmul(ps_t[h * 64:(h + 1) * 64, o_bank, :], vv[:, kb, h * 64:(h + 1) * 64],
                                       a_t[:, h, :], start=first, stop=last))
                        B.op("pe", fn, [AT[k % NS], KVb], [PS[o_bank]])
                    if I["last"]:
                        gi = I["gi"]
                        B.cp("dve", ost[gi % 2][:], ps[o_bank], [PS[o_bank]], [OST[gi % 2]])
                        B.dma("sp", f"B.ost{gi % 2}",
                              OSB[I["hp"] * 128:(I["hp"] + 1) * 128, I["g"] * 512:(I["g"] + 1) * 512],
                              ost[gi % 2][:], [OST[gi % 2]], [])

                load_hp(0)
                load_q(0)
                Zf(0)
                Zf(1)
                Ef(0)
                Lf(0)
                Rf(0)
                for k in range(N):
                    I = its[k]
                    Ef(k + 1)
                    Lf(k + 1)
                    Rf(k + 1)
                    Sf(k)
                    AVf(k - 1)
                    if I["pos"] == 1:
                        load_q(I["gi"] + 1)
                        if I["g"] == 1:
                            load_hp(I["hp"] + 1)
                    Zf(k + 2)
                    Xf(k)
                AVf(N - 1)
            B.barrier()

        def post_norm_residual(xt, XT, ysb, YSB, sq, SQ, rs, RS, tmp, TMP, l, gcol0, ncols, msbank):
            B.mmg(ps[msbank][:, 0:ncols], [(ones_t[:], sq[:, c, :]) for c in range(8)], [SQ, CONST], [PS[msbank]])
            B.act(rs[:], ps[msbank][:, 0:ncols], AF.Ln, [PS[msbank]], [RS], scale=1.0 / D, bias=EPS)
            B.act(rs[:], rs[:], AF.Exp, [RS], [RS], scale=-0.5)
            for c in range(8):
                B.stt("dve", tmp[c % 2][:], ysb[:, c, :], pv_t[:, l, gcol0 + c:gcol0 + c + 1], rs[:],
                      ALU.mult, ALU.mult, [YSB, RS, CONST], [TMP[c % 2]])
                B.tt("pool", xt[:, c, :], xt[:, c, :], tmp[c % 2][:], ALU.add, [TMP[c % 2], XT], [XT])

        def stage_D(l, xsrc):
            NT = DBG.get("NTD", T // 512)
            with ExitStack() as st:
                def sb(n, s, d):
                    return st.enter_context(nc.sbuf_tensor(f"D{l}_{n}", s, d))
                wscr_off[0] = 0
                wg = sb("wg", [128, 8, 3072], BF16)
                wbr = sb("wbr", [128, 8, 1024], BF16)
                wo = sb("wo", [128, 8, 1024], BF16)
                W = Buf()
                wv = w_in[l].rearrange("(c p) n -> p c n", p=128)
                for c in range(8):
                    for hf in range(2):
                        wload(wg[:, c, hf * 1536:(hf + 1) * 1536],
                              wv[:, c, 2304 + hf * 1536:2304 + (hf + 1) * 1536], [128, 1536], W)
                for c in range(4):
                    wload(wbr[:, c, :], w_br_sb[l, c * 128:(c + 1) * 128, :], [128, 1024], W)
                for c in range(2):
                    wload(wbr[:, 4 + c, :], w_br_pool[l, c * 128:(c + 1) * 128, :], [128, 1024], W)
                    wload(wbr[:, 6 + c, :], w_br_gm[l, c * 128:(c + 1) * 128, :], [128, 1024], W)
                for c in range(8):
                    wload(wo[:, c, :], w_out[l, c * 128:(c + 1) * 128, :], [128, 1024], W)
                xts = [sb(f"xt{i}", [128, 8, 512], F32) for i in range(1)] * 2
                XTS = [Buf()] * 2
                brin = [sb(f"brin{i}", [128, 8, 512], BF16) for i in range(1)] * 2
                BRIN = [Buf()] * 2
                sq = sb("sq", [128, 8, 512], BF16); SQ = Buf()
                hT = sb("hT", [128, 8, 512], BF16); HT = Buf()
                rs = sb("rs", [128, 512], F32); RS = Buf()
                gsb = [sb(f"gsb{i}", [128, 512], BF16) for i in range(3)]
                GSB = [Buf() for _ in range(3)]
                t0 = sb("t0", [128, 512], F32); T0 = Buf()
                t1 = sb("t1", [128, 512], F32); T1 = Buf()
                t2 = sb("t2", [128, 512], F32); T2 = Buf()
                merged = sb("merged", [128, 8, 512], BF16); MG = Buf()
                ysb = sb("ysb", [128, 8, 512], F32); YSB = Buf()
                tmp = [sb(f"tmp{i}", [128, 512], F32) for i in range(2)]; TMP = [Buf(), Buf()]
                xv = fm(xsrc)
                yv = fm(yT)
                yrot = Rot([6, 7])
                for i in range(NT):
                    tsl = slice(i * 512, (i + 1) * 512)
                    xt, XT = xts[i % 2], XTS[i % 2]
                    bi, BI = brin[i % 2], BRIN[i % 2]
                    B.dma("sp", f"D.xt{i % 2}", xt[:], xv[:, :, tsl], [], [XT])
                    B.dma("sp", f"D.br{i % 2}", bi[:, 0:4, :], fm(OSB)[:, :, tsl], [], [BI])
                    B.dma("sp", f"D.br{i % 2}", bi[:, 4:6, :], fm(OPL)[:, :, tsl], [], [BI])
                    B.dma("sp", f"D.br{i % 2}", bi[:, 6:8, :], fm(OGM)[:, :, tsl], [], [BI])
                    norm_tile(xt, XT, sq, SQ, hT, HT, rs, RS, 0, l, 512, 6)
                    for c in range(8):
                        csl = slice(c * 128, (c + 1) * 128)
                        for b in range(3):
                            B.mmg(ps[b], [(wg[:, k, b * 1024 + c * 128:b * 1024 + (c + 1) * 128], hT[:, k, :])
                                          for k in range(8)], [W, HT], [PS[b]])
                            B.act(gsb[b][:], ps[b], AF.Sigmoid, [PS[b]], [GSB[b]])
                        B.mmg(ps[3], [(wbr[:, k, csl], bi[:, k, :]) for k in range(0, 4)], [W, BI], [PS[3]])
                        B.mmg(ps[4], [(wbr[:, k, csl], bi[:, k, :]) for k in range(4, 6)], [W, BI], [PS[4]])
                        B.mmg(ps[5], [(wbr[:, k, csl], bi[:, k, :]) for k in range(6, 8)], [W, BI], [PS[5]])
                        B.tt("dve", t0[:], ps[3], gsb[0][:], ALU.mult, [PS[3], GSB[0]], [T0])
                        B.tt("dve", t1[:], ps[4], gsb[1][:], ALU.mult, [PS[4], GSB[1]], [T1])
                        B.tt("dve", t2[:], ps[5], gsb[2][:], ALU.mult, [PS[5], GSB[2]], [T2])
                        B.tt("pool", t0[:], t0[:], t1[:], ALU.add, [T1, T0], [T0])
                        B.tt("pool", merged[:, c, :], t0[:], t2[:], ALU.add, [T0, T2], [MG])
                    for c in range(8):
                        csl = slice(c * 128, (c + 1) * 128)
                        bk = yrot.next()
                        B.mmg(ps[bk], [(wo[:, k, csl], merged[:, k, :]) for k in range(8)], [W, MG], [PS[bk]])
                        B.cp("dve", ysb[:, c, :], ps[bk], [PS[bk]], [YSB])
                        B.act(sq[:, c, :], ysb[:, c, :], AF.Square, [YSB], [SQ])
                    post_norm_residual(xt, XT, ysb, YSB, sq, SQ, rs, RS, tmp, TMP, l, 8, 512, 6)
                    B.dma("sp", f"D.xo{i % 2}", yv[:, :, tsl], xt[:], [XT], [])
            B.barrier()

        def stage_E(l):
            NW = 256
            NT = DBG.get("NTE", T // NW)
            with ExitStack() as st:
                def sb(n, s, d):
                    return st.enter_context(nc.sbuf_tensor(f"E{l}_{n}", s, d))
                wscr_off[0] = 0
                wf1 = sb("wf1", [128, 8, DFF], BF16)
                wf2 = sb("wf2", [128, 32, 1024], BF16)
                W = Buf()
                w1v = w_ff_in[l].rearrange("(c p) n -> p c n", p=128)
                w2v = w_ff_out[l].rearrange("(c p) n -> p c n", p=128)
                for c in range(8 if "nowf1" not in DBG else 0):
                    for hf in range(2):
                        wload(wf1[:, c, hf * 2048:(hf + 1) * 2048], w1v[:, c, hf * 2048:(hf + 1) * 2048], [128, 2048], W)
                for c in range(32 if "nowf2" not in DBG else 0):
                    wload(wf2[:, c, :], w_ff_out[l, c * 128:(c + 1) * 128, :], [128, 1024], W)
                xts = [sb(f"xt{i}", [128, 8, NW], F32) for i in range(2)]
                XTS = [Buf(), Buf()]
                sq = sb("sq", [128, 8, NW], BF16); SQ = Buf()
                hT = sb("hT", [128, 8, NW], BF16); HT = Buf()
                rs = sb("rs", [128, NW], F32); RS = Buf()
                rl = [sb(f"rl{i}", [128, 2, NW], F32) for i in range(2)]
                RL = [Buf(), Buf()]
                aT = sb("aT", [128, 32, NW], BF16); ATB = Buf()
                ysb = sb("ysb", [128, 8, NW], F32); YSB = Buf()
                tmp = [sb(f"tmp{i}", [128, NW], F32) for i in range(2)]; TMP = [Buf(), Buf()]
                yv = fm(yT)
                arot = Rot([0, 1, 2, 3])
                yrot = Rot([4, 5])
                ri = 0
                for i in range(NT):
                    tsl = slice(i * NW, (i + 1) * NW)
                    xt, XT = xts[i % 2], XTS[i % 2]
                    B.dma("sp", f"E.xt{i % 2}", xt[:], yv[:, :, tsl], [], [XT])
                    EP = DBG.get("Eparts", "nabps")
                    if "n" in EP:
                        norm_tile(xt, XT, sq, SQ, hT, HT, rs, RS, 16, l, NW, 6)
                    for j2 in (range(16) if "a" in EP else []):
                        bk = arot.next()
                        for jj in range(2):
                            j = j2 * 2 + jj
                            B.mmg(ps[bk][:, jj * NW:(jj + 1) * NW],
                                  [(wf1[:, k, j * 128:(j + 1) * 128], hT[:, k, :]) for k in range(8)],
                                  [W, HT], [PS[bk]])
                        r, R = rl[ri % 2], RL[ri % 2]
                        ri += 1
                        B.act(r[:], ps[bk].rearrange("p (a n) -> p a n", a=2), AF.Relu, [PS[bk]], [R])
                        B.tt("pool", aT[:, j2 * 2:j2 * 2 + 2, :], r[:], r[:], ALU.mult, [R], [ATB])
                    for c2 in (range(4) if "b" in EP else []):
                        bk = yrot.next()
                        for cc in range(2):
                            c = c2 * 2 + cc
                            B.mmg(ps[bk][:, cc * NW:(cc + 1) * NW],
                                  [(wf2[:, k, c * 128:(c + 1) * 128], aT[:, k, :]) for k in range(32)],
                                  [W, ATB], [PS[bk]])
                        pv2 = ps[bk].rearrange("p (a n) -> p a n", a=2)
                        B.cp("dve", ysb[:, c2 * 2:c2 * 2 + 2, :], pv2, [PS[bk]], [YSB])
                        B.act(sq[:, c2 * 2:c2 * 2 + 2, :], ysb[:, c2 * 2:c2 * 2 + 2, :], AF.Square, [YSB], [SQ])
                    if "p" in EP:
                        post_norm_residual(xt, XT, ysb, YSB, sq, SQ, rs, RS, tmp, TMP, l, 24, NW, 6)
                    if "s" in EP:
                        B.dma("sp", f"E.xo{i % 2}", yv[:, :, tsl], xt[:], [XT], [])
            B.barrier()

        for l in range(nlayers):
            xsrc = xin if l == 0 else yT
            if "A" in stages:
                stage_A(l, xsrc)
            if "B" in stages:
                stage_B(l)
            if "D" in stages:
                stage_D(l, xsrc)
            if "E" in stages:
                stage_E(l)
        B.final()
        with nc.Block() as block:
            B.emit(block)
    return nc


def _consts():
    p = np.arange(128)
    tril = (p[:, None] <= p[None, :]).astype(np.float32)
    negtri = -(p[:, None] >= p[None, :]).astype(np.float32)
    maskL = np.zeros((128, 4, 512), np.float32)
    qidx = np.arange(512)
    for d in range(4):
        kidx = d * 128 + p
        maskL[:, d, :] = (kidx[:, None] < qidx[None, :]).astype(np.float32)
    invcnt = np.zeros((2, 256, 512), np.float32)
    pos = np.arange(512, dtype=np.float32)
    for g, w in enumerate((2, 4, 8, 16)):
        invcnt[0, g * 64:(g + 1) * 64, :] = 1.0 / np.minimum(pos + 1.0, float(w))[None, :]
        invcnt[1, g * 64:(g + 1) * 64, :] = 1.0 / float(w)
    return tril, negtri, maskL, invcnt


_NC_CACHE = {}


def kernel(x, w_in, w_pool, pool_scale, gm_gain, w_spatial, b_spatial, w_br_sb, w_br_pool,
           w_br_gm, w_out, g_mix_pre, g_mix_post, g_ff_pre, g_ff_post, w_ff_in, w_ff_out):
    f = lambda a: np.ascontiguousarray(np.asarray(a, dtype=np.float32))
    x = f(x)
    tril, negtri, maskL, invcnt = _consts()
    def pc(v, n):
        return f(v).reshape(DEPTH, n, 128).transpose(0, 2, 1)
    pvec = np.concatenate([pc(g_mix_pre, 8), pc(g_mix_post, 8), pc(g_ff_pre, 8), pc(g_ff_post, 8),
                           pc(pool_scale, 2)], axis=2)
    shared = {
        "w_in": f(w_in), "w_pool": f(w_pool), "gm_gain": f(gm_gain),
        "w_spT": np.ascontiguousarray(f(w_spatial).transpose(0, 1, 3, 2)),
        "b_spatial": f(b_spatial), "w_br_sb": f(w_br_sb), "w_br_pool": f(w_br_pool),
        "w_br_gm": f(w_br_gm), "w_out": f(w_out), "w_ff_in": f(w_ff_in), "w_ff_out": f(w_ff_out),
        "pvec": np.ascontiguousarray(pvec), "c_tril": tril, "c_negtri": negtri, "c_maskL": maskL,
        "c_invcnt": invcnt,
    }
    if "nc" not in _NC_CACHE:
        _NC_CACHE["nc"] = build()
    nc = _NC_CACHE["nc"]
    in_maps = []
    for c in range(NCORES):
        m = dict(shared)
        m["xT"] = np.ascontiguousarray(x[c].T)
        in_maps.append(m)
    res = run_bass_kernel_spmd(nc, in_maps, core_ids=list(range(NCORES)))
    out = np.stack([np.ascontiguousarray(res.results[c]["yT"].T) for c in range(NCORES)], axis=0)
    return out.astype(np.float32)
```

```python
import numpy as np
from contextlib import ExitStack
import concourse.bass as bass
import concourse.mybir as mybir
from concourse.bass_utils import run_bass_kernel_spmd

F32 = mybir.dt.float32
BF16 = mybir.dt.bfloat16
AF = mybir.ActivationFunctionType
ALU = mybir.AluOpType
AX = mybir.AxisListType

D = 1024
T = 8192
DEPTH = 4
DIN = 5376
DFF = 4096
EPS = 1e-6
SEM_ROT = 30000
NCORES = 4
GELU_C = 1.5957691216057308
DBG = {}


class Buf:
    __slots__ = ("wr", "rd")

    def __init__(self):
        self.wr = {}
        self.rd = {}


class Builder:
    def __init__(self, nc):
        self.nc = nc
        self.q = {k: [] for k in ("pe", "act", "dve", "pool", "sp")}
        self.sems = []
        self.esem = {}
        self.ecnt = {}
        self.dsem = {}
        self.dcnt = {}
        self.last = {}
        self.waited = {k: {} for k in self.q}
        self.pend = {}
        self.sem_eng = {}

    def newsem(self):
        self.sems.append(self.nc.alloc_semaphore(f"sm{len(self.sems)}"))
        return len(self.sems) - 1

    def _push(self, eng, fn, waits, sem, inc, nosame=False):
        pend = self.pend.pop(eng, None)
        if pend:
            for k, v in pend.items():
                if waits.get(k, 0) < v:
                    waits[k] = v
        ws = []
        wd = self.waited[eng]
        for k, v in waits.items():
            if wd.get(k, 0) >= v:
                continue
            if (eng == "pe" or nosame) and self.sem_eng.get(k) == eng:
                continue
            wd[k] = v
            ws.append((k, v))
        self.q[eng].append((ws, fn, sem, inc))

    @staticmethod
    def _merge(dst, src):
        for k, v in src.items():
            if dst.get(k, 0) < v:
                dst[k] = v

    def _deps(self, reads, writes):
        w = {}
        for b in reads:
            self._merge(w, b.wr)
        for b in writes:
            self._merge(w, b.wr)
            self._merge(w, b.rd)
        return w

    def _mark(self, reads, writes, sem, val):
        for b in reads:
            if b.rd.get(sem, 0) < val:
                b.rd[sem] = val
        for b in writes:
            b.wr = {sem: val}
            b.rd = {}

    def op(self, eng, fn, reads=(), writes=(), nosame=False):
        if eng not in self.esem or self.ecnt[eng] >= SEM_ROT:
            self.esem[eng] = self.newsem()
            self.sem_eng[self.esem[eng]] = eng
            self.ecnt[eng] = 0
        w = self._deps(reads, writes)
        self.ecnt[eng] += 1
        sem = self.esem[eng]
        val = self.ecnt[eng]
        self._push(eng, fn, w, sem, 1, nosame)
        self._mark(reads, writes, sem, val)
        self.last[("e", eng)] = (sem, val)

    def raw(self, eng, fn):
        self.q[eng].append(([], fn, None, 0))

    def dma(self, eng, key, out, in_, reads=(), writes=()):
        if key not in self.dsem:
            self.dsem[key] = self.newsem()
            self.dcnt[key] = 0
        w = self._deps(reads, writes)
        self.dcnt[key] += 16
        sem = self.dsem[key]
        val = self.dcnt[key]
        self._push(eng, lambda e: e.dma_start(out=out, in_=in_), w, sem, 16)
        self._mark(reads, writes, sem, val)
        self.last[("d", key)] = (sem, val)

    def group_done(self, key, bufs):
        sem = self.dsem[key]
        val = self.dcnt[key]
        for b in bufs:
            b.wr = {sem: val}

    def barrier(self):
        toks = {}
        for (sem, val) in self.last.values():
            if toks.get(sem, 0) < val:
                toks[sem] = val
        for eng in self.q:
            self.pend[eng] = dict(toks)

    def final(self):
        self.barrier()
        for eng in self.q:
            self._push(eng, lambda e: e.nop(), {}, None, 0)

    def emit(self, block):
        decs = {"pe": block.tensor, "act": block.scalar, "dve": block.vector,
                "pool": block.gpsimd, "sp": block.sync}
        for k, dec in decs.items():
            def f(e, k=k):
                sems = self.sems
                for ws, fn, sem, inc in self.q[k]:
                    for (wi, wv) in ws:
                        e.wait_ge(sems[wi], wv)
                    ins = fn(e)
                    if sem is not None:
                        ins.then_inc(sems[sem], inc)
            dec(f)

    def mmg(self, out, pairs, reads, writes, start=True, stop=True):
        n = len(pairs)
        for i, (l, r) in enumerate(pairs):
            st = start and i == 0
            sp = stop and i == n - 1
            fn = (lambda e, l=l, r=r, st=st, sp=sp: e.matmul(out, l, r, start=st, stop=sp))
            if n == 1:
                self.op("pe", fn, reads, writes)
            elif i == 0:
                w = self._deps(reads, writes)
                self._push("pe", fn, w, None, 0)
            elif i == n - 1:
                self.op("pe", fn, reads, writes)
            else:
                self.raw("pe", fn)

    def act(self, out, in_, func, reads, writes, scale=1.0, bias=0.0, nosame=False):
        self.op("act", lambda e: e.activation(out=out, in_=in_, func=func, bias=bias, scale=scale),
                reads, writes, nosame)

    def tt(self, eng, out, in0, in1, op, reads, writes):
        self.op(eng, lambda e: e.tensor_tensor(out=out, in0=in0, in1=in1, op=op), reads, writes)

    def ts(self, eng, out, in0, s1, s2, op0, op1, reads, writes):
        self.op(eng, lambda e: e.tensor_scalar(out=out, in0=in0, scalar1=s1, scalar2=s2, op0=op0, op1=op1),
                reads, writes)

    def ts1(self, eng, out, in0, s1, op0, reads, writes):
        self.op(eng, lambda e: e.tensor_scalar(out=out, in0=in0, scalar1=s1, scalar2=None, op0=op0),
                reads, writes)

    def stt(self, eng, out, in0, scalar, in1, op0, op1, reads, writes):
        self.op(eng, lambda e: e.scalar_tensor_tensor(out=out, in0=in0, scalar=scalar, in1=in1,
                                                      op0=op0, op1=op1), reads, writes)

    def cp(self, eng, out, in_, reads, writes):
        self.op(eng, lambda e: e.tensor_copy(out=out, in_=in_), reads, writes)


class Rot:
    def __init__(self, items):
        self.items = items
        self.i = 0

    def next(self):
        it = self.items[self.i % len(self.items)]
        self.i += 1
        return it


def build(nlayers=DEPTH, dbg=False, stages="ABDE"):
    nc = bass.Bass("TRN2", target_bir_lowering=False)

    def din(name, shape):
        return nc.dram_tensor(name, shape, F32, kind="ExternalInput").ap()

    xin = din("xT", [D, T])
    w_in = din("w_in", [DEPTH, D, DIN])
    w_pool = din("w_pool", [DEPTH, 4, 64, 64])
    gm_gain = din("gm_gain", [DEPTH, 256])
    w_spT = din("w_spT", [DEPTH, 4, 128, 128])
    b_spatial = din("b_spatial", [DEPTH, 4, 128])
    w_br_sb = din("w_br_sb", [DEPTH, 512, 1024])
    w_br_pool = din("w_br_pool", [DEPTH, 256, 1024])
    w_br_gm = din("w_br_gm", [DEPTH, 256, 1024])
    w_out = din("w_out", [DEPTH, 1024, 1024])
    w_ff_in = din("w_ff_in", [DEPTH, 1024, DFF])
    w_ff_out = din("w_ff_out", [DEPTH, DFF, 1024])
    pvec = din("pvec", [DEPTH, 128, 34])
    c_tril = din("c_tril", [128, 128])
    c_negtri = din("c_negtri", [128, 128])
    c_maskL = din("c_maskL", [128, 4, 512])
    c_invcnt = din("c_invcnt", [2, 256, 512])

    kind_s = "ExternalOutput" if dbg else "Internal"
    yT = nc.dram_tensor("yT", [D, T], F32, kind="ExternalOutput").ap()
    QT = nc.dram_tensor("QT", [512, T], BF16, kind=kind_s).ap()
    KT = nc.dram_tensor("KT", [512, T], BF16, kind=kind_s).ap()
    Vd = nc.dram_tensor("Vd", [T, 512], BF16, kind=kind_s).ap()
    OSB = nc.dram_tensor("OSB", [512, T], BF16, kind=kind_s).ap()
    OPL = nc.dram_tensor("OPL", [256, T], BF16, kind=kind_s).ap()
    OGM = nc.dram_tensor("OGM", [256, T], BF16, kind=kind_s).ap()

    B = Builder(nc)

    def fm(ap):
        return ap.rearrange("(c p) t -> p c t", p=128)

    with ExitStack() as top:
        ps_t = top.enter_context(nc.psum_tensor("ps", [128, 8, 512], F32))
        ps = [ps_t[:, k, :] for k in range(8)]
        PS = [Buf() for _ in range(8)]

        ones_t = top.enter_context(nc.sbuf_tensor("ones", [128, 128], BF16))
        nones_t = top.enter_context(nc.sbuf_tensor("nones", [128, 128], BF16))
        negtri_t = top.enter_context(nc.sbuf_tensor("negtri", [128, 128], BF16))
        tril_t = top.enter_context(nc.sbuf_tensor("tril", [128, 128], F32))
        pv_t = top.enter_context(nc.sbuf_tensor("pv", [128, DEPTH, 34], F32))
        CONST = Buf()
        B.op("pool", lambda e: e.memset(ones_t[:], 1.0), [], [CONST])
        B.op("pool", lambda e: e.memset(nones_t[:], -1.0), [], [CONST])
        B.dma("pool", "w", negtri_t[:], c_negtri, [], [CONST])
        B.dma("pool", "w", tril_t[:], c_tril, [], [CONST])
        B.dma("pool", "w", pv_t[:], pvec.rearrange("l p c -> p l c"), [], [CONST])
        B.group_done("w", [CONST])
        B.barrier()

        NWS = 4
        wst = [top.enter_context(nc.sbuf_tensor(f"wst{i}", [128, 512], F32)) for i in range(NWS)]
        WST = [Buf() for _ in range(NWS)]
        wscr_off = [0]
        wl_i = [0]

        def wload(dst, src, shape, Wb):
            n = shape[1]
            if n > 512:
                for o in range(0, n, 512):
                    wload(dst[:, o:o + 512], src[:, o:o + 512], [128, 512], Wb)
                return
            k = wl_i[0] % NWS
            wl_i[0] += 1
            B.dma("sp", f"wst{k}", wst[k][:, 0:n], src, [], [WST[k]])
            B.cp("dve" if k % 2 == 0 else "pool", dst, wst[k][:, 0:n], [WST[k]], [Wb])

        def norm_tile(xt, XT, sq, SQ, hT, HT, rs, RS, gcol0, l, ncols, msbank):
            B.act(sq[:], xt[:], AF.Square, [XT], [SQ])
            B.mmg(ps[msbank][:, 0:ncols], [(ones_t[:], sq[:, c, :]) for c in range(8)],
                  [SQ, CONST], [PS[msbank]])
            B.act(rs[:], ps[msbank][:, 0:ncols], AF.Ln, [PS[msbank]], [RS], scale=1.0 / D, bias=EPS)
            B.act(rs[:], rs[:], AF.Exp, [RS], [RS], scale=-0.5)
            for c in range(8):
                B.stt("dve", hT[:, c, :], xt[:, c, :], pv_t[:, l, gcol0 + c:gcol0 + c + 1], rs[:],
                      ALU.mult, ALU.mult, [XT, RS, CONST], [HT])

        def gelu(eng2, x, X, tmp, TMP, sg, SG, out, OUT):
            B.tt("dve", tmp, x, x, ALU.mult, [X], [TMP])
            B.ts("dve", tmp, tmp, 0.044715, 1.0, ALU.mult, ALU.add, [TMP], [TMP])
            B.tt("dve", tmp, tmp, x, ALU.mult, [TMP, X], [TMP])
            B.act(sg, tmp, AF.Sigmoid, [TMP], [SG], scale=GELU_C)
            B.tt(eng2, out, x, sg, ALU.mult, [X, SG], [OUT])

        def stage_A(l, xsrc):
            NT = DBG.get("NT", T // 512)
            parts = DBG.get("parts", "qk,pu,v,gv,pool,gm")
            with ExitStack() as st:
                def sb(n, s, d):
                    return st.enter_context(nc.sbuf_tensor(f"A{l}_{n}", s, d))
                wa = sb("wa", [128, 8, 2304], BF16)
                wpbd = sb("wpbd", [128, 2, 128], BF16)
                wsT32 = sb("wsT32", [128, 4, 128], F32)
                wsT = sb("wsT", [128, 4, 128], BF16)
                brep = sb("brep", [128, 2, 128], F32)
                gainrep = sb("gainrep", [128, 256], F32)
                ic_t = sb("ic", [128, 2, 2, 512], F32)
                W = Buf()
                for wch in range(2):
                    B.dma("pool", "w", ic_t[:, wch, :, :], c_invcnt[wch].rearrange("(j p) t -> p j t", p=128), [], [W])
                B.op("pool", lambda e: e.memset(wpbd[:], 0.0), [], [W])
                wv = w_in[l].rearrange("(c p) n -> p c n", p=128)
                for c in range(8):
                    B.dma("pool", "w", wa[:, c, :], wv[:, c, 0:2304], [], [W])
                for g in range(4):
                    j, hh = g // 2, g % 2
                    B.dma("pool", "w", wpbd[hh * 64:(hh + 1) * 64, j, hh * 64:(hh + 1) * 64],
                          w_pool[l, g], [], [W])
                    B.dma("pool", "w", brep[hh * 64:(hh + 1) * 64, j, :],
                          b_spatial[l, g, :].partition_broadcast(64), [], [W])
                B.dma("pool", "w", wsT32[:], w_spT[l].rearrange("g p t -> p g t"), [], [W])
                B.dma("pool", "w", gainrep[:], gm_gain[l, :].partition_broadcast(128), [], [W])
                B.group_done("w", [W])
                for g in range(4):
                    B.tt("dve", wsT[:, g, :], wsT32[:, g, :], tril_t[:], ALU.mult, [W, CONST], [W])

                xts = [sb(f"xt{i}", [128, 8, 512], F32) for i in range(2)]
                XTS = [Buf(), Buf()]
                sq = sb("sq", [128, 8, 512], BF16); SQ = Buf()
                hT = sb("hT", [128, 8, 512], BF16); HT = Buf()
                rs = sb("rs", [128, 512], F32); RS = Buf()
                qk_st = sb("qkst", [128, 8, 512], BF16); QST = Buf(); KST = Buf()
                v_st = sb("vst", [128, 4, 512], BF16); VST = Buf()
                ptile = sb("ptile", [128, 2, 528], F32); PT = Buf()
                s2h = sb("s2h", [128, 2, 528], F32); s4h = sb("s4h", [128, 2, 528], F32)
                s8h = sb("s8h", [128, 2, 528], F32); s16h = sb("s16h", [128, 2, 528], F32)
                SH = Buf()
                ptmp = sb("ptmp", [128, 2, 512], F32); PTMP = Buf()
                pooled = sb("pooled", [128, 2, 512], BF16); PLD = Buf()
                opl_st = sb("oplst", [128, 2, 512], BF16); OPST = Buf()
                uf = sb("uf", [128, 2, 512], F32); UF = Buf()
                utmp = sb("utmp", [128, 2, 512], F32); UTMP = Buf()
                usg = sb("usg", [128, 2, 512], F32); USG = Buf()
                ug = sb("ug", [128, 2, 512], F32); UG = Buf()
                gvf = sb("gvf", [128, 4, 256], F32); GVF = Buf()
                gtmp = sb("gtmp", [128, 4, 256], F32); GTMP = Buf()
                gsg = sb("gsg", [128, 4, 256], F32); GSG = Buf()
                gvg = sb("gvg", [128, 4, 256], F32); GVG = Buf()
                ss = sb("ss", [128, 4], F32); SS = Buf()
                gvn = sb("gvn", [128, 4, 256], BF16); GVN = Buf()
                mtmp = sb("mtmp", [128, 2, 512], F32); MTMP = Buf()
                ogm_st = sb("ogmst", [128, 2, 512], BF16); OGST = Buf()
                B.op("pool", lambda e: e.memset(ptile[:], 0.0), [], [PT])

                rot = Rot([2, 3, 4, 5])
                xv = fm(xsrc)
                def a_loads(i):
                    if i >= NT:
                        return
                    B.dma("sp", f"A.xt{i % 2}", xts[i % 2][:], xv[:, :, i * 512:(i + 1) * 512], [], [XTS[i % 2]])

                def a_norm(i):
                    if i >= NT:
                        return
                    norm_tile(xts[i % 2], XTS[i % 2], sq, SQ, hT, HT, rs, RS, 0, l, 512, 0)

                pts = [ptile, sb("ptile2", [128, 2, 528], F32)]
                PTS = [PT, Buf()]
                B.op("pool", lambda e: e.memset(pts[1][:], 0.0), [], [PTS[1]])
                ufs = [uf, sb("uf2", [128, 2, 512], F32)]
                UFS = [UF, Buf()]
                gvfs = [gvf, sb("gvf2", [128, 4, 256], F32)]
                GVFS = [GVF, Buf()]

                def a_front(i):
                    if i >= NT:
                        return
                    tsl = slice(i * 512, (i + 1) * 512)
                    ptile, PT = pts[i % 2], PTS[i % 2]
                    uf, UF = ufs[i % 2], UFS[i % 2]
                    gvf, GVF = gvfs[i % 2], GVFS[i % 2]
                    a_loads(i + 1)
                    for j in (range(8) if "qk" in parts else []):
                        bk = rot.next()
                        B.mmg(ps[bk], [(wa[:, c, j * 128:(j + 1) * 128], hT[:, c, :]) for c in range(8)],
                              [W, HT], [PS[bk]])
                        if j < 4:
                            B.act(qk_st[:, j, :], ps[bk], AF.Identity, [PS[bk]], [QST], scale=0.125)
                        else:
                            B.cp("dve", qk_st[:, j, :], ps[bk], [PS[bk]], [KST])
                    if "qk" in parts:
                        B.dma("sp", "A.qst", fm(QT)[:, :, tsl], qk_st[:, 0:4, :], [QST], [])
                        B.dma("sp", "A.kst", fm(KT)[:, :, tsl], qk_st[:, 4:8, :], [KST], [])
                    if "pu" in parts:
                        for j in range(2):
                            bk = rot.next()
                            B.mmg(ps[bk], [(wa[:, c, 1536 + j * 128:1536 + (j + 1) * 128], hT[:, c, :])
                                           for c in range(8)], [W, HT], [PS[bk]])
                            B.act(ptile[:, j, 16:528], ps[bk], AF.Identity, [PS[bk]], [PT])
                        for j in range(2):
                            bk = rot.next()
                            B.mmg(ps[bk], [(wa[:, c, 1792 + j * 128:1792 + (j + 1) * 128], hT[:, c, :])
                                           for c in range(8)], [W, HT], [PS[bk]])
                            B.act(uf[:, j, :], ps[bk], AF.Identity, [PS[bk]], [UF])
                    if "v" in parts:
                        for b in range(4):
                            bk = rot.next()
                            B.mmg(ps[bk], [(hT[:, c, b * 128:(b + 1) * 128], wa[:, c, 1024:1536])
                                           for c in range(8)], [W, HT], [PS[bk]])
                            B.cp("dve", v_st[:, b, :], ps[bk], [PS[bk]], [VST])
                        B.dma("sp", "A.vst", Vd.rearrange("(n p) c -> p n c", p=128)[:, i * 4:(i + 1) * 4, :],
                              v_st[:], [VST], [])
                    if "gv" in parts:
                        for b2 in range(2):
                            bk = rot.next()
                            for bb in range(2):
                                b = b2 * 2 + bb
                                B.mmg(ps[bk][:, bb * 256:(bb + 1) * 256],
                                      [(hT[:, c, b * 128:(b + 1) * 128], wa[:, c, 2048:2304]) for c in range(8)],
                                      [W, HT], [PS[bk]])
                            B.act(gvf[:, b2 * 2:b2 * 2 + 2, :], ps[bk].rearrange("p (a n) -> p a n", a=2), AF.Identity,
                                  [PS[bk]], [GVF])
                    a_norm(i + 1)

                def a_tail_elem(i):
                    tsl = slice(i * 512, (i + 1) * 512)
                    ptile, PT = pts[i % 2], PTS[i % 2]
                    uf, UF = ufs[i % 2], UFS[i % 2]
                    gvf, GVF = gvfs[i % 2], GVFS[i % 2]
                    if "pool" in parts:
                        B.tt("pool", s2h[:, :, 1:528], ptile[:, :, 1:528], ptile[:, :, 0:527], ALU.add, [PT], [SH])
                        B.tt("pool", s4h[:, :, 3:528], s2h[:, :, 3:528], s2h[:, :, 1:526], ALU.add, [SH], [SH])
                        B.tt("pool", s8h[:, :, 7:528], s4h[:, :, 7:528], s4h[:, :, 3:524], ALU.add, [SH], [SH])
                        B.tt("pool", s16h[:, :, 15:528], s8h[:, :, 15:528], s8h[:, :, 7:520], ALU.add, [SH], [SH])
                        which = 0 if i == 0 else 1
                        for g, sh in enumerate((s2h, s4h, s8h, s16h)):
                            j, hh = g // 2, g % 2
                            prt = slice(hh * 64, (hh + 1) * 64)
                            B.tt("pool", ptmp[prt, j, :], sh[prt, j, 16:528], ic_t[prt, which, j, :], ALU.mult,
                                 [SH, W], [PTMP])
                        B.tt("pool", pooled[:], ptmp[:], ptile[:, :, 16:528], ALU.subtract, [PTMP, PT], [PLD])
                        B.cp("pool", pts[(i + 1) % 2][:, :, 0:16], ptile[:, :, 512:528], [PT, SH, PLD], [PTS[(i + 1) % 2]])
                    if "gm" in parts:
                        gelu("pool", uf[:], UF, utmp[:], UTMP, usg[:], USG, ug[:], UG)
                        gelu("pool", gvf[:], GVF, gtmp[:], GTMP, gsg[:], GSG, gvg[:], GVG)
                        B.tt("dve", gtmp[:], gvg[:], gvg[:], ALU.mult, [GVG], [GTMP])
                        B.op("dve", lambda e: e.reduce_sum(out=ss[:], in_=gtmp[:], axis=AX.X), [GTMP], [SS])
                        B.act(ss[:], ss[:], AF.Ln, [SS], [SS], scale=1.0 / 256, bias=EPS)
                        B.act(ss[:], ss[:], AF.Exp, [SS], [SS], scale=-0.5)
                        for b in range(4):
                            B.stt("dve", gvn[:, b, :], gvg[:, b, :], ss[:, b:b + 1], gainrep[:], ALU.mult, ALU.mult,
                                  [GVG, SS, W], [GVN])

                def a_tail_mm(i):
                    tsl = slice(i * 512, (i + 1) * 512)
                    ptile, PT = pts[i % 2], PTS[i % 2]
                    uf, UF = ufs[i % 2], UFS[i % 2]
                    gvf, GVF = gvfs[i % 2], GVFS[i % 2]
                    if "pool" in parts:
                        for j in range(2):
                            bk = rot.next()
                            B.mmg(ps[bk], [(wpbd[:, j, :], pooled[:, j, :])], [W, PLD], [PS[bk]])
                            B.ts1("dve", opl_st[:, j, :], ps[bk], pv_t[:, l, 32 + j:33 + j], ALU.mult,
                                  [PS[bk], CONST], [OPST])
                        B.dma("sp", "A.oplst", fm(OPL)[:, :, tsl], opl_st[:], [OPST], [])
                    if "gm" in parts:
                        for b in range(4):
                            for g in range(4):
                                j, hh = g // 2, g % 2
                                B.mmg(ps_t[hh * 64:(hh + 1) * 64, 6 + j, b * 128:(b + 1) * 128],
                                      [(gvn[:, b, g * 64:(g + 1) * 64], wsT[:, g, :])], [GVN, W], [PS[6], PS[7]])
                        for b in range(4):
                            B.tt("dve", mtmp[:, :, b * 128:(b + 1) * 128], ps_t[:, 6:8, b * 128:(b + 1) * 128], brep[:],
                                 ALU.add, [PS[6], PS[7], W], [MTMP])
                        B.tt("pool", ogm_st[:], mtmp[:], ug[:], ALU.mult, [MTMP, UG], [OGST])
                        B.dma("sp", "A.ogmst", fm(OGM)[:, :, tsl], ogm_st[:], [OGST], [])


                a_loads(0)
                a_norm(0)
                a_front(0)
                for i in range(NT):
                    a_front(i + 1)
                    a_tail_elem(i)
                    a_tail_mm(i)
            B.barrier()

        def stage_B(l):
            NG = T // 512
            NS = 3
            with ExitStack() as st:
                def sb(n, s, d):
                    return st.enter_context(nc.sbuf_tensor(f"B{l}_{n}", s, d))
                maskL_t = sb("maskL", [128, 4, 512], BF16)
                MK = Buf()
                B.dma("pool", "w", maskL_t[:], c_maskL, [], [MK])
                B.group_done("w", [MK])
                ktp = [sb(f"ktp{i}", [128, T], BF16) for i in range(2)]
                vp = [sb(f"vp{i}", [128, T // 128, 128], BF16) for i in range(2)]
                KV = [Buf(), Buf()]
                qs = [sb(f"q{i}", [128, 512], BF16) for i in range(2)]
                QS = [Buf(), Buf()]
                ex = [sb(f"ex{i}", [128, 2, 512], F32) for i in range(NS)]
                EX = [Buf() for _ in range(NS)]
                spt = [sb(f"sp{i}", [128, 2, 512], BF16) for i in range(NS)]
                SPT = [Buf() for _ in range(NS)]
                at = [sb(f"at{i}", [128, 2, 512], BF16) for i in range(NS)]
                AT = [Buf() for _ in range(NS)]
                r16 = [sb(f"r16_{i}", [128, 2, 512], BF16) for i in range(NS)]
                R16 = [Buf() for _ in range(NS)]
                ost = [sb(f"ost{i}", [128, 512], BF16) for i in range(2)]
                OST = [Buf(), Buf()]
                zb = [(0, 1), (0, 1)]
                sbks = [(2, 3), (4, 5)]
                ob = [6, 7]
                its = []
                for hp in range(4):
                    for g in range(NG):
                        nkb = 4 * g + 4
                        for pos, kb in enumerate(range(nkb - 1, -1, -1)):
                            its.append(dict(hp=hp, g=g, kb=kb, d=kb - 4 * g, first=(pos == 0), last=(kb == 0),
                                            gi=hp * NG + g, pos=pos))
                N = len(its)
                Vv = Vd.rearrange("(n p) c -> p n c", p=128)

                def load_hp(hp):
                    if hp > 3:
                        return
                    sl = hp % 2
                    B.dma("sp", f"B.kt{sl}", ktp[sl][:], KT[hp * 128:(hp + 1) * 128, :], [], [KV[sl]])
                    B.dma("sp", f"B.kt{sl}", vp[sl][:], Vv[:, :, hp * 128:(hp + 1) * 128], [], [KV[sl]])

                def load_q(gi):
                    if gi >= 4 * NG:
                        return
                    hp, g = gi // NG, gi % NG
                    B.dma("sp", f"B.q{gi % 2}", qs[gi % 2][:], QT[hp * 128:(hp + 1) * 128, g * 512:(g + 1) * 512],
                          [], [QS[gi % 2]])

                def Zf(k):
                    if not 0 <= k < N:
                        return
                    I = its[k]
                    kt, KVb = ktp[I["hp"] % 2], KV[I["hp"] % 2]
                    q, Q = qs[I["gi"] % 2], QS[I["gi"] % 2]
                    ksl = slice(I["kb"] * 128, (I["kb"] + 1) * 128)
                    for h in range(2):
                        pr = slice(h * 64, (h + 1) * 64)
                        zk = zb[k % 2][h]
                        B.mmg(ps[zk], [(kt[pr, ksl], q[pr, :])], [KVb, Q], [PS[zk]])

                def Ef(k):
                    if not 0 <= k < N:
                        return
                    z0, z1 = zb[k % 2]
                    B.act(ex[k % NS][:], ps_t[:, z0:z0 + 2, :], AF.Exp, [PS[z0], PS[z1]], [EX[k % NS]])

                def Lf(k):
                    if not 0 <= k < N:
                        return
                    I = its[k]
                    B.act(spt[k % NS][:], ex[k % NS][:], AF.Ln, [EX[k % NS]], [SPT[k % NS]], bias=1.0, nosame=True)
                    if I["d"] >= 0:
                        d = I["d"]
                        B.tt("dve", spt[k % NS][:], spt[k % NS][:],
                             maskL_t[:, d:d + 1, :].to_broadcast([128, 2, 512]), ALU.mult,
                             [SPT[k % NS], MK], [SPT[k % NS]])

                def Rf(k):
                    if not 0 <= k < N:
                        return
                    I = its[k]
                    if I["last"]:
                        return
                    if I["first"]:
                        B.cp("dve", r16[k % NS][:], spt[k % NS][:], [SPT[k % NS]], [R16[k % NS]])
                    else:
                        B.tt("dve", r16[k % NS][:], r16[(k - 1) % NS][:], spt[k % NS][:], ALU.add,
                             [SPT[k % NS], R16[(k - 1) % NS]], [R16[k % NS]])

                def Sf(k):
                    if not 0 <= k < N:
                        return
                    sbk = sbks[k % 2]
                    I = its[k]
                    kt, KVb = ktp[I["hp"] % 2], KV[I["hp"] % 2]
                    q, Q = qs[I["gi"] % 2], QS[I["gi"] % 2]
                    ksl = slice(I["kb"] * 128, (I["kb"] + 1) * 128)
                    for h in range(2):
                        pr = slice(h * 64, (h + 1) * 64)
                        pairs = [(negtri_t[:], spt[k % NS][:, h, :])]
                        rds = [SPT[k % NS], CONST, KVb, Q]
                        if not I["first"]:
                            pairs.append((nones_t[:], r16[(k - 1) % NS][:, h, :]))
                            rds.append(R16[(k - 1) % NS])
                        pairs.append((kt[pr, ksl], q[pr, :]))
                        B.mmg(ps[sbk[h]], pairs, rds, [PS[sbk[h]]])

                def Xf(k):
                    sbk = sbks[k % 2]
                    I = its[k]
                    B.act(at[k % NS][:], ps_t[:, sbk[0]:sbk[0] + 2, :], AF.Exp, [PS[sbk[0]], PS[sbk[1]]], [AT[k % NS]])
                    if I["d"] >= 0:
                        d = I["d"]
                        B.tt("pool", at[k % NS][:], at[k % NS][:],
                             maskL_t[:, d:d + 1, :].to_broadcast([128, 2, 512]), ALU.mult,
                             [AT[k % NS], MK], [AT[k % NS]])

                def AVf(k):
                    if not 0 <= k < N:
                        return
                    I = its[k]
                    vv, KVb = vp[I["hp"] % 2], KV[I["hp"] % 2]
                    o_bank = ob[I["gi"] % 2]
                    a_t = at[k % NS]
                    for h in range(2):
                        fn = (lambda e, h=h, a_t=a_t, kb=I["kb"], first=I["first"], last=I["last"], o_bank=o_bank, vv=vv:
                              e.matmul(ps_t[h * 64:(h + 1) * 64, o_bank, :], vv[:, kb, h * 64:(h + 1) * 64],
                                       a_t[:, h, :], start=first, stop=last))
                        B.op("pe", fn, [AT[k % NS], KVb], [PS[o_bank]])
                    if I["last"]:
                        gi = I["gi"]
                        B.cp("dve", ost[gi % 2][:], ps[o_bank], [PS[o_bank]], [OST[gi % 2]])
                        B.dma("sp", f"B.ost{gi % 2}",
                              OSB[I["hp"] * 128:(I["hp"] + 1) * 128, I["g"] * 512:(I["g"] + 1) * 512],
                              ost[gi % 2][:], [OST[gi % 2]], [])

                load_hp(0)
                load_q(0)
                Zf(0)
                Ef(0)
                Zf(1)
                Lf(0)
                Rf(0)
                Sf(0)
                for k in range(N):
                    I = its[k]
                    Ef(k + 1)
                    Lf(k + 1)
                    Rf(k + 1)
                    AVf(k - 1)
                    if I["pos"] == 1:
                        load_q(I["gi"] + 1)
                        if I["g"] == 1:
                            load_hp(I["hp"] + 1)
                    Zf(k + 2)
                    Sf(k + 1)
                    Xf(k)
                AVf(N - 1)
            B.barrier()

        def post_norm_residual(xt, XT, ysb, YSB, sq, SQ, rs, RS, tmp, TMP, l, gcol0, ncols, msbank):
            B.mmg(ps[msbank][:, 0:ncols], [(ones_t[:], sq[:, c, :]) for c in range(8)], [SQ, CONST], [PS[msbank]])
            B.act(rs[:], ps[msbank][:, 0:ncols], AF.Ln, [PS[msbank]], [RS], scale=1.0 / D, bias=EPS)
            B.act(rs[:], rs[:], AF.Exp, [RS], [RS], scale=-0.5)
            for c in range(8):
                B.stt("dve", tmp[c % 2][:], ysb[:, c, :], pv_t[:, l, gcol0 + c:gcol0 + c + 1], rs[:],
                      ALU.mult, ALU.mult, [YSB, RS, CONST], [TMP[c % 2]])
                B.tt("pool", xt[:, c, :], xt[:, c, :], tmp[c % 2][:], ALU.add, [TMP[c % 2], XT], [XT])

        def stage_D(l, xsrc):
            NT = DBG.get("NTD", T // 512)
            with ExitStack() as st:
                def sb(n, s, d):
                    return st.enter_context(nc.sbuf_tensor(f"D{l}_{n}", s, d))
                wscr_off[0] = 0
                wg = sb("wg", [128, 8, 3072], BF16)
                wbr = sb("wbr", [128, 8, 1024], BF16)
                wo = sb("wo", [128, 8, 1024], BF16)
                W = Buf()
                wv = w_in[l].rearrange("(c p) n -> p c n", p=128)
                for c in range(8):
                    for hf in range(2):
                        wload(wg[:, c, hf * 1536:(hf + 1) * 1536],
                              wv[:, c, 2304 + hf * 1536:2304 + (hf + 1) * 1536], [128, 1536], W)
                for c in range(4):
                    wload(wbr[:, c, :], w_br_sb[l, c * 128:(c + 1) * 128, :], [128, 1024], W)
                for c in range(2):
                    wload(wbr[:, 4 + c, :], w_br_pool[l, c * 128:(c + 1) * 128, :], [128, 1024], W)
                    wload(wbr[:, 6 + c, :], w_br_gm[l, c * 128:(c + 1) * 128, :], [128, 1024], W)
                for c in range(8):
                    wload(wo[:, c, :], w_out[l, c * 128:(c + 1) * 128, :], [128, 1024], W)
                xts = [sb(f"xt{i}", [128, 8, 512], F32) for i in range(2)]
                XTS = [Buf(), Buf()]
                brin = [sb(f"brin{i}", [128, 8, 512], BF16) for i in range(2)]
                BRIN = [Buf(), Buf()]
                sq = sb("sq", [128, 8, 512], BF16); SQ = Buf()
                hT = sb("hT", [128, 8, 512], BF16); HT = Buf()
                rs = sb("rs", [128, 512], F32); RS = Buf()
                gsb = [sb(f"gsb{i}", [128, 512], BF16) for i in range(3)]
                GSB = [Buf() for _ in range(3)]
                t0 = sb("t0", [128, 512], F32); T0 = Buf()
                t1 = sb("t1", [128, 512], F32); T1 = Buf()
                t2 = sb("t2", [128, 512], F32); T2 = Buf()
                merged = sb("merged", [128, 8, 512], BF16); MG = Buf()
                ysb = sb("ysb", [128, 8, 512], F32); YSB = Buf()
                tmp = [sb(f"tmp{i}", [128, 512], F32) for i in range(2)]; TMP = [Buf(), Buf()]
                xv = fm(xsrc)
                yv = fm(yT)
                yrot = Rot([6, 7])

                def d_loads(i):
                    if i >= NT:
                        return
                    tsl = slice(i * 512, (i + 1) * 512)
                    B.dma("sp", f"D.xt{i % 2}", xts[i % 2][:], xv[:, :, tsl], [], [XTS[i % 2]])
                    B.dma("sp", f"D.br{i % 2}", brin[i % 2][:, 0:4, :], fm(OSB)[:, :, tsl], [], [BRIN[i % 2]])
                    B.dma("sp", f"D.br{i % 2}", brin[i % 2][:, 4:6, :], fm(OPL)[:, :, tsl], [], [BRIN[i % 2]])
                    B.dma("sp", f"D.br{i % 2}", brin[i % 2][:, 6:8, :], fm(OGM)[:, :, tsl], [], [BRIN[i % 2]])

                def d_norm(i):
                    if i >= NT:
                        return
                    norm_tile(xts[i % 2], XTS[i % 2], sq, SQ, hT, HT, rs, RS, 0, l, 512, 6)

                d_loads(0)
                d_norm(0)
                for i in range(NT):
                    tsl = slice(i * 512, (i + 1) * 512)
                    xt, XT = xts[i % 2], XTS[i % 2]
                    bi, BI = brin[i % 2], BRIN[i % 2]
                    d_loads(i + 1)
                    for c in range(8):
                        csl = slice(c * 128, (c + 1) * 128)
                        for b in range(3):
                            B.mmg(ps[b], [(wg[:, k, b * 1024 + c * 128:b * 1024 + (c + 1) * 128], hT[:, k, :])
                                          for k in range(8)], [W, HT], [PS[b]])
                            B.act(gsb[b][:], ps[b], AF.Sigmoid, [PS[b]], [GSB[b]])
                        B.mmg(ps[3], [(wbr[:, k, csl], bi[:, k, :]) for k in range(0, 4)], [W, BI], [PS[3]])
                        B.mmg(ps[4], [(wbr[:, k, csl], bi[:, k, :]) for k in range(4, 6)], [W, BI], [PS[4]])
                        B.mmg(ps[5], [(wbr[:, k, csl], bi[:, k, :]) for k in range(6, 8)], [W, BI], [PS[5]])
                        B.tt("dve", t0[:], ps[3], gsb[0][:], ALU.mult, [PS[3], GSB[0]], [T0])
                        B.tt("dve", t1[:], ps[4], gsb[1][:], ALU.mult, [PS[4], GSB[1]], [T1])
                        B.tt("dve", t2[:], ps[5], gsb[2][:], ALU.mult, [PS[5], GSB[2]], [T2])
                        B.tt("pool", t0[:], t0[:], t1[:], ALU.add, [T1, T0], [T0])
                        B.tt("pool", merged[:, c, :], t0[:], t2[:], ALU.add, [T0, T2], [MG])
                    d_norm(i + 1)
                    for c in range(8):
                        csl = slice(c * 128, (c + 1) * 128)
                        bk = yrot.next()
                        B.mmg(ps[bk], [(wo[:, k, csl], merged[:, k, :]) for k in range(8)], [W, MG], [PS[bk]])
                        B.cp("dve", ysb[:, c, :], ps[bk], [PS[bk]], [YSB])
                        B.act(sq[:, c, :], ysb[:, c, :], AF.Square, [YSB], [SQ])
                    post_norm_residual(xt, XT, ysb, YSB, sq, SQ, rs, RS, tmp, TMP, l, 8, 512, 6)
                    B.dma("sp", f"D.xo{i % 2}", yv[:, :, tsl], xt[:], [XT], [])
            B.barrier()

        def stage_E(l):
            NW = 256
            NT = DBG.get("NTE", T // NW)
            with ExitStack() as st:
                def sb(n, s, d):
                    return st.enter_context(nc.sbuf_tensor(f"E{l}_{n}", s, d))
                wscr_off[0] = 0
                wf1 = sb("wf1", [128, 8, DFF], BF16)
                wf2 = sb("wf2", [128, 32, 1024], BF16)
                W = Buf()
                w1v = w_ff_in[l].rearrange("(c p) n -> p c n", p=128)
                w2v = w_ff_out[l].rearrange("(c p) n -> p c n", p=128)
                for c in range(8 if "nowf1" not in DBG else 0):
                    for hf in range(2):
                        wload(wf1[:, c, hf * 2048:(hf + 1) * 2048], w1v[:, c, hf * 2048:(hf + 1) * 2048], [128, 2048], W)
                for c in range(32 if "nowf2" not in DBG else 0):
                    wload(wf2[:, c, :], w_ff_out[l, c * 128:(c + 1) * 128, :], [128, 1024], W)
                xts = [sb(f"xt{i}", [128, 8, NW], F32) for i in range(2)]
                XTS = [Buf(), Buf()]
                sq = sb("sq", [128, 8, NW], BF16); SQ = Buf()
                hT = sb("hT", [128, 8, NW], BF16); HT = Buf()
                rs = sb("rs", [128, NW], F32); RS = Buf()
                rl = [sb(f"rl{i}", [128, 2, NW], F32) for i in range(2)]
                RL = [Buf(), Buf()]
                aT = sb("aT", [128, 32, NW], BF16); ATB = Buf()
                ysb = sb("ysb", [128, 8, NW], F32); YSB = Buf()
                tmp = [sb(f"tmp{i}", [128, NW], F32) for i in range(2)]; TMP = [Buf(), Buf()]
                yv = fm(yT)
                arot = Rot([0, 1, 2, 3])
                yrot = Rot([4, 5])
                ri = 0
                def e_loads(i):
                    if i >= NT:
                        return
                    B.dma("sp", f"E.xt{i % 2}", xts[i % 2][:], yv[:, :, i * NW:(i + 1) * NW], [], [XTS[i % 2]])

                def e_norm(i):
                    if i >= NT:
                        return
                    norm_tile(xts[i % 2], XTS[i % 2], sq, SQ, hT, HT, rs, RS, 16, l, NW, 6)

                EP = DBG.get("Eparts", "nabps")
                e_loads(0)
                e_norm(0)
                for i in range(NT):
                    tsl = slice(i * NW, (i + 1) * NW)
                    xt, XT = xts[i % 2], XTS[i % 2]
                    e_loads(i + 1)
                    for j2 in (range(16) if "a" in EP else []):
                        bk = arot.next()
                        for jj in range(2):
                            j = j2 * 2 + jj
                            B.mmg(ps[bk][:, jj * NW:(jj + 1) * NW],
                                  [(wf1[:, k, j * 128:(j + 1) * 128], hT[:, k, :]) for k in range(8)],
                                  [W, HT], [PS[bk]])
                        r, R = rl[ri % 2], RL[ri % 2]
                        ri += 1
                        B.act(r[:], ps[bk].rearrange("p (a n) -> p a n", a=2), AF.Relu, [PS[bk]], [R])
                        B.tt("pool", aT[:, j2 * 2:j2 * 2 + 2, :], r[:], r[:], ALU.mult, [R], [ATB])
                    e_norm(i + 1)
                    for c2 in (range(4) if "b" in EP else []):
                        bk = yrot.next()
                        for cc in range(2):
                            c = c2 * 2 + cc
                            B.mmg(ps[bk][:, cc * NW:(cc + 1) * NW],
                                  [(wf2[:, k, c * 128:(c + 1) * 128], aT[:, k, :]) for k in range(32)],
                                  [W, ATB], [PS[bk]])
                        pv2 = ps[bk].rearrange("p (a n) -> p a n", a=2)
                        B.cp("dve", ysb[:, c2 * 2:c2 * 2 + 2, :], pv2, [PS[bk]], [YSB])
                        B.act(sq[:, c2 * 2:c2 * 2 + 2, :], ysb[:, c2 * 2:c2 * 2 + 2, :], AF.Square, [YSB], [SQ])
                    if "p" in EP:
                        post_norm_residual(xt, XT, ysb, YSB, sq, SQ, rs, RS, tmp, TMP, l, 24, NW, 6)
                    if "s" in EP:
                        B.dma("sp", f"E.xo{i % 2}", yv[:, :, tsl], xt[:], [XT], [])
            B.barrier()

        for l in range(nlayers):
            xsrc = xin if l == 0 else yT
            if "A" in stages:
                stage_A(l, xsrc)
            if "B" in stages:
                stage_B(l)
            if "D" in stages:
                stage_D(l, xsrc)
            if "E" in stages:
                stage_E(l)
        B.final()
        with nc.Block() as block:
            B.emit(block)
    return nc


def _consts():
    p = np.arange(128)
    tril = (p[:, None] <= p[None, :]).astype(np.float32)
    negtri = -(p[:, None] >= p[None, :]).astype(np.float32)
    maskL = np.zeros((128, 4, 512), np.float32)
    qidx = np.arange(512)
    for d in range(4):
        kidx = d * 128 + p
        maskL[:, d, :] = (kidx[:, None] < qidx[None, :]).astype(np.float32)
    invcnt = np.zeros((2, 256, 512), np.float32)
    pos = np.arange(512, dtype=np.float32)
    for g, w in enumerate((2, 4, 8, 16)):
        invcnt[0, g * 64:(g + 1) * 64, :] = 1.0 / np.minimum(pos + 1.0, float(w))[None, :]
        invcnt[1, g * 64:(g + 1) * 64, :] = 1.0 / float(w)
    return tril, negtri, maskL, invcnt


_NC_CACHE = {}


def kernel(x, w_in, w_pool, pool_scale, gm_gain, w_spatial, b_spatial, w_br_sb, w_br_pool,
           w_br_gm, w_out, g_mix_pre, g_mix_post, g_ff_pre, g_ff_post, w_ff_in, w_ff_out):
    f = lambda a: np.ascontiguousarray(np.asarray(a, dtype=np.float32))
    x = f(x)
    tril, negtri, maskL, invcnt = _consts()
    def pc(v, n):
        return f(v).reshape(DEPTH, n, 128).transpose(0, 2, 1)
    pvec = np.concatenate([pc(g_mix_pre, 8), pc(g_mix_post, 8), pc(g_ff_pre, 8), pc(g_ff_post, 8),
                           pc(pool_scale, 2)], axis=2)
    shared = {
        "w_in": f(w_in), "w_pool": f(w_pool), "gm_gain": f(gm_gain),
        "w_spT": np.ascontiguousarray(f(w_spatial).transpose(0, 1, 3, 2)),
        "b_spatial": f(b_spatial), "w_br_sb": f(w_br_sb), "w_br_pool": f(w_br_pool),
        "w_br_gm": f(w_br_gm), "w_out": f(w_out), "w_ff_in": f(w_ff_in), "w_ff_out": f(w_ff_out),
        "pvec": np.ascontiguousarray(pvec), "c_tril": tril, "c_negtri": negtri, "c_maskL": maskL,
        "c_invcnt": invcnt,
    }
    if "nc" not in _NC_CACHE:
        _NC_CACHE["nc"] = build()
    nc = _NC_CACHE["nc"]
    in_maps = []
    for c in range(NCORES):
        m = dict(shared)
        m["xT"] = np.ascontiguousarray(x[c].T)
        in_maps.append(m)
    res = run_bass_kernel_spmd(nc, in_maps, core_ids=list(range(NCORES)))
    out = np.stack([np.ascontiguousarray(res.results[c]["yT"].T) for c in range(NCORES)], axis=0)
    return out.astype(np.float32)
```

```python
import numpy as np
from contextlib import ExitStack
import concourse.bass as bass
import concourse.mybir as mybir
from concourse.bass_utils import run_bass_kernel_spmd

F32 = mybir.dt.float32
BF16 = mybir.dt.bfloat16
AF = mybir.ActivationFunctionType
ALU = mybir.AluOpType
AX = mybir.AxisListType

D = 1024
T = 8192
DEPTH = 4
DIN = 5376
DFF = 4096
EPS = 1e-6
SEM_ROT = 30000
NCORES = 4
GELU_C = 1.5957691216057308
DBG = {}


class Buf:
    __slots__ = ("wr", "rd")

    def __init__(self):
        self.wr = {}
        self.rd = {}


class Builder:
    def __init__(self, nc):
        self.nc = nc
        self.q = {k: [] for k in ("pe", "act", "dve", "pool", "sp")}
        self.sems = []
        self.esem = {}
        self.ecnt = {}
        self.dsem = {}
        self.dcnt = {}
        self.last = {}
        self.waited = {k: {} for k in self.q}
        self.pend = {}
        self.sem_eng = {}

    def newsem(self):
        self.sems.append(self.nc.alloc_semaphore(f"sm{len(self.sems)}"))
        return len(self.sems) - 1

    def _push(self, eng, fn, waits, sem, inc, nosame=False):
        pend = self.pend.pop(eng, None)
        if pend:
            for k, v in pend.items():
                if waits.get(k, 0) < v:
                    waits[k] = v
        ws = []
        wd = self.waited[eng]
        for k, v in waits.items():
            if wd.get(k, 0) >= v:
                continue
            if (eng == "pe" or nosame) and self.sem_eng.get(k) == eng:
                continue
            wd[k] = v
            ws.append((k, v))
        self.q[eng].append((ws, fn, sem, inc))

    @staticmethod
    def _merge(dst, src):
        for k, v in src.items():
            if dst.get(k, 0) < v:
                dst[k] = v

    def _deps(self, reads, writes):
        w = {}
        for b in reads:
            self._merge(w, b.wr)
        for b in writes:
            self._merge(w, b.wr)
            self._merge(w, b.rd)
        return w

    def _mark(self, reads, writes, sem, val):
        for b in reads:
            if b.rd.get(sem, 0) < val:
                b.rd[sem] = val
        for b in writes:
            b.wr = {sem: val}
            b.rd = {}

    def op(self, eng, fn, reads=(), writes=(), nosame=False):
        if eng not in self.esem or self.ecnt[eng] >= SEM_ROT:
            self.esem[eng] = self.newsem()
            self.sem_eng[self.esem[eng]] = eng
            self.ecnt[eng] = 0
        w = self._deps(reads, writes)
        self.ecnt[eng] += 1
        sem = self.esem[eng]
        val = self.ecnt[eng]
        self._push(eng, fn, w, sem, 1, nosame)
        self._mark(reads, writes, sem, val)
        self.last[("e", eng)] = (sem, val)

    def raw(self, eng, fn):
        self.q[eng].append(([], fn, None, 0))

    def dma(self, eng, key, out, in_, reads=(), writes=()):
        if key not in self.dsem:
            self.dsem[key] = self.newsem()
            self.dcnt[key] = 0
        w = self._deps(reads, writes)
        self.dcnt[key] += 16
        sem = self.dsem[key]
        val = self.dcnt[key]
        self._push(eng, lambda e: e.dma_start(out=out, in_=in_), w, sem, 16)
        self._mark(reads, writes, sem, val)
        self.last[("d", key)] = (sem, val)

    def group_done(self, key, bufs):
        sem = self.dsem[key]
        val = self.dcnt[key]
        for b in bufs:
            b.wr = {sem: val}

    def barrier(self):
        toks = {}
        for (sem, val) in self.last.values():
            if toks.get(sem, 0) < val:
                toks[sem] = val
        for eng in self.q:
            self.pend[eng] = dict(toks)

    def final(self):
        self.barrier()
        for eng in self.q:
            self._push(eng, lambda e: e.nop(), {}, None, 0)

    def emit(self, block):
        decs = {"pe": block.tensor, "act": block.scalar, "dve": block.vector,
                "pool": block.gpsimd, "sp": block.sync}
        for k, dec in decs.items():
            def f(e, k=k):
                sems = self.sems
                for ws, fn, sem, inc in self.q[k]:
                    for (wi, wv) in ws:
                        e.wait_ge(sems[wi], wv)
                    ins = fn(e)
                    if sem is not None:
                        ins.then_inc(sems[sem], inc)
            dec(f)

    def mmg(self, out, pairs, reads, writes, start=True, stop=True):
        n = len(pairs)
        for i, (l, r) in enumerate(pairs):
            st = start and i == 0
            sp = stop and i == n - 1
            fn = (lambda e, l=l, r=r, st=st, sp=sp: e.matmul(out, l, r, start=st, stop=sp))
            if n == 1:
                self.op("pe", fn, reads, writes)
            elif i == 0:
                w = self._deps(reads, writes)
                self._push("pe", fn, w, None, 0)
            elif i == n - 1:
                self.op("pe", fn, reads, writes)
            else:
                self.raw("pe", fn)

    def act(self, out, in_, func, reads, writes, scale=1.0, bias=0.0, nosame=False):
        self.op("act", lambda e: e.activation(out=out, in_=in_, func=func, bias=bias, scale=scale),
                reads, writes, nosame)

    def tt(self, eng, out, in0, in1, op, reads, writes):
        self.op(eng, lambda e: e.tensor_tensor(out=out, in0=in0, in1=in1, op=op), reads, writes)

    def ts(self, eng, out, in0, s1, s2, op0, op1, reads, writes):
        self.op(eng, lambda e: e.tensor_scalar(out=out, in0=in0, scalar1=s1, scalar2=s2, op0=op0, op1=op1),
                reads, writes)

    def ts1(self, eng, out, in0, s1, op0, reads, writes):
        self.op(eng, lambda e: e.tensor_scalar(out=out, in0=in0, scalar1=s1, scalar2=None, op0=op0),
                reads, writes)

    def stt(self, eng, out, in0, scalar, in1, op0, op1, reads, writes):
        self.op(eng, lambda e: e.scalar_tensor_tensor(out=out, in0=in0, scalar=scalar, in1=in1,
                                                      op0=op0, op1=op1), reads, writes)

    def cp(self, eng, out, in_, reads, writes):
        self.op(eng, lambda e: e.tensor_copy(out=out, in_=in_), reads, writes)


class Rot:
    def __init__(self, items):
        self.items = items
        self.i = 0

    def next(self):
        it = self.items[self.i % len(self.items)]
        self.i += 1
        return it


def build(nlayers=DEPTH, dbg=False, stages="ABDE"):
    nc = bass.Bass("TRN2", target_bir_lowering=False)

    def din(name, shape):
        return nc.dram_tensor(name, shape, F32, kind="ExternalInput").ap()

    xin = din("xT", [D, T])
    w_in = din("w_in", [DEPTH, D, DIN])
    w_pool = din("w_pool", [DEPTH, 4, 64, 64])
    gm_gain = din("gm_gain", [DEPTH, 256])
    w_spT = din("w_spT", [DEPTH, 4, 128, 128])
    b_spatial = din("b_spatial", [DEPTH, 4, 128])
    w_br_sb = din("w_br_sb", [DEPTH, 512, 1024])
    w_br_pool = din("w_br_pool", [DEPTH, 256, 1024])
    w_br_gm = din("w_br_gm", [DEPTH, 256, 1024])
    w_out = din("w_out", [DEPTH, 1024, 1024])
    w_ff_in = din("w_ff_in", [DEPTH, 1024, DFF])
    w_ff_out = din("w_ff_out", [DEPTH, DFF, 1024])
    pvec = din("pvec", [DEPTH, 128, 34])
    c_tril = din("c_tril", [128, 128])
    c_negtri = din("c_negtri", [128, 128])
    c_maskL = din("c_maskL", [128, 4, 512])
    c_invcnt = din("c_invcnt", [2, 256, 512])

    kind_s = "ExternalOutput" if dbg else "Internal"
    yT = nc.dram_tensor("yT", [D, T], F32, kind="ExternalOutput").ap()
    QT = nc.dram_tensor("QT", [512, T], BF16, kind=kind_s).ap()
    KT = nc.dram_tensor("KT", [512, T], BF16, kind=kind_s).ap()
    Vd = nc.dram_tensor("Vd", [T, 512], BF16, kind=kind_s).ap()
    OSB = nc.dram_tensor("OSB", [512, T], BF16, kind=kind_s).ap()
    OPL = nc.dram_tensor("OPL", [256, T], BF16, kind=kind_s).ap()
    OGM = nc.dram_tensor("OGM", [256, T], BF16, kind=kind_s).ap()

    B = Builder(nc)

    def fm(ap):
        return ap.rearrange("(c p) t -> p c t", p=128)

    with ExitStack() as top:
        ps_t = top.enter_context(nc.psum_tensor("ps", [128, 8, 512], F32))
        ps = [ps_t[:, k, :] for k in range(8)]
        PS = [Buf() for _ in range(8)]

        ones_t = top.enter_context(nc.sbuf_tensor("ones", [128, 128], BF16))
        nones_t = top.enter_context(nc.sbuf_tensor("nones", [128, 128], BF16))
        negtri_t = top.enter_context(nc.sbuf_tensor("negtri", [128, 128], BF16))
        tril_t = top.enter_context(nc.sbuf_tensor("tril", [128, 128], F32))
        pv_t = top.enter_context(nc.sbuf_tensor("pv", [128, DEPTH, 34], F32))
        CONST = Buf()
        B.op("pool", lambda e: e.memset(ones_t[:], 1.0), [], [CONST])
        B.op("pool", lambda e: e.memset(nones_t[:], -1.0), [], [CONST])
        B.dma("pool", "w", negtri_t[:], c_negtri, [], [CONST])
        B.dma("pool", "w", tril_t[:], c_tril, [], [CONST])
        B.dma("pool", "w", pv_t[:], pvec.rearrange("l p c -> p l c"), [], [CONST])
        B.group_done("w", [CONST])
        B.barrier()

        NWS = 4
        wst = [top.enter_context(nc.sbuf_tensor(f"wst{i}", [128, 512], F32)) for i in range(NWS)]
        WST = [Buf() for _ in range(NWS)]
        wscr_off = [0]
        wl_i = [0]

        def wload(dst, src, shape, Wb):
            n = shape[1]
            if n > 512:
                for o in range(0, n, 512):
                    wload(dst[:, o:o + 512], src[:, o:o + 512], [128, 512], Wb)
                return
            k = wl_i[0] % NWS
            wl_i[0] += 1
            B.dma("sp", f"wst{k}", wst[k][:, 0:n], src, [], [WST[k]])
            B.cp("dve" if k % 2 == 0 else "pool", dst, wst[k][:, 0:n], [WST[k]], [Wb])

        def norm_tile(xt, XT, sq, SQ, hT, HT, rs, RS, gcol0, l, ncols, msbank):
            B.act(sq[:], xt[:], AF.Square, [XT], [SQ])
            B.mmg(ps[msbank][:, 0:ncols], [(ones_t[:], sq[:, c, :]) for c in range(8)],
                  [SQ, CONST], [PS[msbank]])
            B.act(rs[:], ps[msbank][:, 0:ncols], AF.Ln, [PS[msbank]], [RS], scale=1.0 / D, bias=EPS)
            B.act(rs[:], rs[:], AF.Exp, [RS], [RS], scale=-0.5)
            for c in range(8):
                B.stt("dve", hT[:, c, :], xt[:, c, :], pv_t[:, l, gcol0 + c:gcol0 + c + 1], rs[:],
                      ALU.mult, ALU.mult, [XT, RS, CONST], [HT])

        def gelu(eng2, x, X, tmp, TMP, sg, SG, out, OUT):
            B.tt("dve", tmp, x, x, ALU.mult, [X], [TMP])
            B.ts("dve", tmp, tmp, 0.044715, 1.0, ALU.mult, ALU.add, [TMP], [TMP])
            B.tt("dve", tmp, tmp, x, ALU.mult, [TMP, X], [TMP])
            B.act(sg, tmp, AF.Sigmoid, [TMP], [SG], scale=GELU_C)
            B.tt(eng2, out, x, sg, ALU.mult, [X, SG], [OUT])

        def stage_A(l, xsrc):
            NT = DBG.get("NT", T // 512)
            parts = DBG.get("parts", "qk,pu,v,gv,pool,gm")
            with ExitStack() as st:
                def sb(n, s, d):
                    return st.enter_context(nc.sbuf_tensor(f"A{l}_{n}", s, d))
                wa = sb("wa", [128, 8, 2304], BF16)
                wpbd = sb("wpbd", [128, 2, 128], BF16)
                wsT32 = sb("wsT32", [128, 4, 128], F32)
                wsT = sb("wsT", [128, 4, 128], BF16)
                brep = sb("brep", [128, 2, 128], F32)
                gainrep = sb("gainrep", [128, 256], F32)
                ic_t = sb("ic", [128, 2, 2, 512], F32)
                W = Buf()
                for wch in range(2):
                    B.dma("pool", "w", ic_t[:, wch, :, :], c_invcnt[wch].rearrange("(j p) t -> p j t", p=128), [], [W])
                B.op("pool", lambda e: e.memset(wpbd[:], 0.0), [], [W])
                wv = w_in[l].rearrange("(c p) n -> p c n", p=128)
                for c in range(8):
                    B.dma("pool", "w", wa[:, c, :], wv[:, c, 0:2304], [], [W])
                for g in range(4):
                    j, hh = g // 2, g % 2
                    B.dma("pool", "w", wpbd[hh * 64:(hh + 1) * 64, j, hh * 64:(hh + 1) * 64],
                          w_pool[l, g], [], [W])
                    B.dma("pool", "w", brep[hh * 64:(hh + 1) * 64, j, :],
                          b_spatial[l, g, :].partition_broadcast(64), [], [W])
                B.dma("pool", "w", wsT32[:], w_spT[l].rearrange("g p t -> p g t"), [], [W])
                B.dma("pool", "w", gainrep[:], gm_gain[l, :].partition_broadcast(128), [], [W])
                B.group_done("w", [W])
                for g in range(4):
                    B.tt("dve", wsT[:, g, :], wsT32[:, g, :], tril_t[:], ALU.mult, [W, CONST], [W])

                xts = [sb(f"xt{i}", [128, 8, 512], F32) for i in range(2)]
                XTS = [Buf(), Buf()]
                sq = sb("sq", [128, 8, 512], BF16); SQ = Buf()
                hT = sb("hT", [128, 8, 512], BF16); HT = Buf()
                rs = sb("rs", [128, 512], F32); RS = Buf()
                qk_st = sb("qkst", [128, 8, 512], BF16); QST = Buf(); KST = Buf()
                v_st = sb("vst", [128, 4, 512], BF16); VST = Buf()
                ptile = sb("ptile", [128, 2, 528], F32); PT = Buf()
                s2h = sb("s2h", [128, 2, 528], F32); s4h = sb("s4h", [128, 2, 528], F32)
                s8h = sb("s8h", [128, 2, 528], F32); s16h = sb("s16h", [128, 2, 528], F32)
                SH = Buf()
                ptmp = sb("ptmp", [128, 2, 512], F32); PTMP = Buf()
                pooled = sb("pooled", [128, 2, 512], BF16); PLD = Buf()
                opl_st = sb("oplst", [128, 2, 512], BF16); OPST = Buf()
                uf = sb("uf", [128, 2, 512], F32); UF = Buf()
                utmp = sb("utmp", [128, 2, 512], F32); UTMP = Buf()
                usg = sb("usg", [128, 2, 512], F32); USG = Buf()
                ug = sb("ug", [128, 2, 512], F32); UG = Buf()
                gvf = sb("gvf", [128, 4, 256], F32); GVF = Buf()
                gtmp = sb("gtmp", [128, 4, 256], F32); GTMP = Buf()
                gsg = sb("gsg", [128, 4, 256], F32); GSG = Buf()
                gvg = sb("gvg", [128, 4, 256], F32); GVG = Buf()
                ss = sb("ss", [128, 4], F32); SS = Buf()
                gvn = sb("gvn", [128, 4, 256], BF16); GVN = Buf()
                mtmp = sb("mtmp", [128, 2, 512], F32); MTMP = Buf()
                ogm_st = sb("ogmst", [128, 2, 512], BF16); OGST = Buf()
                B.op("pool", lambda e: e.memset(ptile[:], 0.0), [], [PT])

                rot = Rot([2, 3, 4, 5])
                xv = fm(xsrc)
                def a_loads(i):
                    if i >= NT:
                        return
                    B.dma("sp", f"A.xt{i % 2}", xts[i % 2][:], xv[:, :, i * 512:(i + 1) * 512], [], [XTS[i % 2]])

                def a_norm(i):
                    if i >= NT:
                        return
                    norm_tile(xts[i % 2], XTS[i % 2], sq, SQ, hT, HT, rs, RS, 0, l, 512, 0)

                pts = [ptile, sb("ptile2", [128, 2, 528], F32)]
                PTS = [PT, Buf()]
                B.op("pool", lambda e: e.memset(pts[1][:], 0.0), [], [PTS[1]])
                ufs = [uf, sb("uf2", [128, 2, 512], F32)]
                UFS = [UF, Buf()]
                gvfs = [gvf, sb("gvf2", [128, 4, 256], F32)]
                GVFS = [GVF, Buf()]

                def a_F1(i):
                    if i < 0 or i >= NT:
                        return
                    tsl = slice(i * 512, (i + 1) * 512)
                    ptile, PT = pts[i % 2], PTS[i % 2]
                    uf, UF = ufs[i % 2], UFS[i % 2]
                    gvf, GVF = gvfs[i % 2], GVFS[i % 2]
                    a_loads(i + 1)
                    for j in (range(8) if "qk" in parts else []):
                        bk = rot.next()
                        B.mmg(ps[bk], [(wa[:, c, j * 128:(j + 1) * 128], hT[:, c, :]) for c in range(8)],
                              [W, HT], [PS[bk]])
                        if j < 4:
                            B.act(qk_st[:, j, :], ps[bk], AF.Identity, [PS[bk]], [QST], scale=0.125)
                        else:
                            B.cp("dve", qk_st[:, j, :], ps[bk], [PS[bk]], [KST])
                    if "qk" in parts:
                        B.dma("sp", "A.qst", fm(QT)[:, :, tsl], qk_st[:, 0:4, :], [QST], [])
                        B.dma("sp", "A.kst", fm(KT)[:, :, tsl], qk_st[:, 4:8, :], [KST], [])

                def a_F2(i):
                    if i < 0 or i >= NT:
                        return
                    tsl = slice(i * 512, (i + 1) * 512)
                    ptile, PT = pts[i % 2], PTS[i % 2]
                    uf, UF = ufs[i % 2], UFS[i % 2]
                    gvf, GVF = gvfs[i % 2], GVFS[i % 2]
                    if "pu" in parts:
                        for j in range(2):
                            bk = rot.next()
                            B.mmg(ps[bk], [(wa[:, c, 1536 + j * 128:1536 + (j + 1) * 128], hT[:, c, :])
                                           for c in range(8)], [W, HT], [PS[bk]])
                            B.act(ptile[:, j, 16:528], ps[bk], AF.Identity, [PS[bk]], [PT])
                        for j in range(2):
                            bk = rot.next()
                            B.mmg(ps[bk], [(wa[:, c, 1792 + j * 128:1792 + (j + 1) * 128], hT[:, c, :])
                                           for c in range(8)], [W, HT], [PS[bk]])
                            B.act(uf[:, j, :], ps[bk], AF.Identity, [PS[bk]], [UF])

                def a_F3(i):
                    if i < 0 or i >= NT:
                        return
                    tsl = slice(i * 512, (i + 1) * 512)
                    ptile, PT = pts[i % 2], PTS[i % 2]
                    uf, UF = ufs[i % 2], UFS[i % 2]
                    gvf, GVF = gvfs[i % 2], GVFS[i % 2]
                    if "v" in parts:
                        for b in range(4):
                            bk = rot.next()
                            B.mmg(ps[bk], [(hT[:, c, b * 128:(b + 1) * 128], wa[:, c, 1024:1536])
                                           for c in range(8)], [W, HT], [PS[bk]])
                            B.cp("dve", v_st[:, b, :], ps[bk], [PS[bk]], [VST])
                        B.dma("sp", "A.vst", Vd.rearrange("(n p) c -> p n c", p=128)[:, i * 4:(i + 1) * 4, :],
                              v_st[:], [VST], [])

                def a_F4(i):
                    if i < 0 or i >= NT:
                        return
                    tsl = slice(i * 512, (i + 1) * 512)
                    ptile, PT = pts[i % 2], PTS[i % 2]
                    uf, UF = ufs[i % 2], UFS[i % 2]
                    gvf, GVF = gvfs[i % 2], GVFS[i % 2]
                    if "gv" in parts:
                        for b2 in range(2):
                            bk = rot.next()
                            for bb in range(2):
                                b = b2 * 2 + bb
                                B.mmg(ps[bk][:, bb * 256:(bb + 1) * 256],
                                      [(hT[:, c, b * 128:(b + 1) * 128], wa[:, c, 2048:2304]) for c in range(8)],
                                      [W, HT], [PS[bk]])
                            B.act(gvf[:, b2 * 2:b2 * 2 + 2, :], ps[bk].rearrange("p (a n) -> p a n", a=2), AF.Identity,
                                  [PS[bk]], [GVF])
                    a_norm(i + 1)

                def a_S1(i):
                    if i < 0 or i >= NT:
                        return
                    tsl = slice(i * 512, (i + 1) * 512)
                    ptile, PT = pts[i % 2], PTS[i % 2]
                    uf, UF = ufs[i % 2], UFS[i % 2]
                    gvf, GVF = gvfs[i % 2], GVFS[i % 2]
                    if "pool" in parts:
                        B.tt("pool", s2h[:, :, 1:528], ptile[:, :, 1:528], ptile[:, :, 0:527], ALU.add, [PT], [SH])
                        B.tt("pool", s4h[:, :, 3:528], s2h[:, :, 3:528], s2h[:, :, 1:526], ALU.add, [SH], [SH])
                        B.tt("pool", s8h[:, :, 7:528], s4h[:, :, 7:528], s4h[:, :, 3:524], ALU.add, [SH], [SH])
                        B.tt("pool", s16h[:, :, 15:528], s8h[:, :, 15:528], s8h[:, :, 7:520], ALU.add, [SH], [SH])

                def a_S2(i):
                    if i < 0 or i >= NT:
                        return
                    tsl = slice(i * 512, (i + 1) * 512)
                    ptile, PT = pts[i % 2], PTS[i % 2]
                    uf, UF = ufs[i % 2], UFS[i % 2]
                    gvf, GVF = gvfs[i % 2], GVFS[i % 2]
                    if "gm" in parts:
                        gelu("pool", uf[:], UF, utmp[:], UTMP, usg[:], USG, ug[:], UG)

                def a_S3(i):
                    if i < 0 or i >= NT:
                        return
                    tsl = slice(i * 512, (i + 1) * 512)
                    ptile, PT = pts[i % 2], PTS[i % 2]
                    uf, UF = ufs[i % 2], UFS[i % 2]
                    gvf, GVF = gvfs[i % 2], GVFS[i % 2]
                    if "gm" in parts:
                        gelu("pool", gvf[:], GVF, gtmp[:], GTMP, gsg[:], GSG, gvg[:], GVG)

                def a_S4(i):
                    if i < 0 or i >= NT:
                        return
                    tsl = slice(i * 512, (i + 1) * 512)
                    ptile, PT = pts[i % 2], PTS[i % 2]
                    uf, UF = ufs[i % 2], UFS[i % 2]
                    gvf, GVF = gvfs[i % 2], GVFS[i % 2]
                    if "pool" in parts:
                        which = 0 if i == 0 else 1
                        for g, sh in enumerate((s2h, s4h, s8h, s16h)):
                            j, hh = g // 2, g % 2
                            prt = slice(hh * 64, (hh + 1) * 64)
                            B.tt("pool", ptmp[prt, j, :], sh[prt, j, 16:528], ic_t[prt, which, j, :], ALU.mult,
                                 [SH, W], [PTMP])
                        B.tt("pool", pooled[:], ptmp[:], ptile[:, :, 16:528], ALU.subtract, [PTMP, PT], [PLD])
                        B.cp("pool", pts[(i + 1) % 2][:, :, 0:16], ptile[:, :, 512:528], [PT, SH, PLD], [PTS[(i + 1) % 2]])

                def a_S5(i):
                    if i < 0 or i >= NT:
                        return
                    tsl = slice(i * 512, (i + 1) * 512)
                    ptile, PT = pts[i % 2], PTS[i % 2]
                    uf, UF = ufs[i % 2], UFS[i % 2]
                    gvf, GVF = gvfs[i % 2], GVFS[i % 2]
                    if "gm" in parts:
                        B.tt("dve", gtmp[:], gvg[:], gvg[:], ALU.mult, [GVG], [GTMP])
                        B.op("dve", lambda e: e.reduce_sum(out=ss[:], in_=gtmp[:], axis=AX.X), [GTMP], [SS])
                        B.act(ss[:], ss[:], AF.Ln, [SS], [SS], scale=1.0 / 256, bias=EPS)
                        B.act(ss[:], ss[:], AF.Exp, [SS], [SS], scale=-0.5)
                        for b in range(4):
                            B.stt("dve", gvn[:, b, :], gvg[:, b, :], ss[:, b:b + 1], gainrep[:], ALU.mult, ALU.mult,
                                  [GVG, SS, W], [GVN])


                def a_tail_mm(i):
                    tsl = slice(i * 512, (i + 1) * 512)
                    ptile, PT = pts[i % 2], PTS[i % 2]
                    uf, UF = ufs[i % 2], UFS[i % 2]
                    gvf, GVF = gvfs[i % 2], GVFS[i % 2]
                    if "pool" in parts:
                        for j in range(2):
                            bk = rot.next()
                            B.mmg(ps[bk], [(wpbd[:, j, :], pooled[:, j, :])], [W, PLD], [PS[bk]])
                            B.ts1("dve", opl_st[:, j, :], ps[bk], pv_t[:, l, 32 + j:33 + j], ALU.mult,
                                  [PS[bk], CONST], [OPST])
                        B.dma("sp", "A.oplst", fm(OPL)[:, :, tsl], opl_st[:], [OPST], [])
                    if "gm" in parts:
                        for b in range(4):
                            for g in range(4):
                                j, hh = g // 2, g % 2
                                B.mmg(ps_t[hh * 64:(hh + 1) * 64, 6 + j, b * 128:(b + 1) * 128],
                                      [(gvn[:, b, g * 64:(g + 1) * 64], wsT[:, g, :])], [GVN, W], [PS[6], PS[7]])
                        for b in range(4):
                            B.tt("dve", mtmp[:, :, b * 128:(b + 1) * 128], ps_t[:, 6:8, b * 128:(b + 1) * 128], brep[:],
                                 ALU.add, [PS[6], PS[7], W], [MTMP])
                        B.tt("pool", ogm_st[:], mtmp[:], ug[:], ALU.mult, [MTMP, UG], [OGST])
                        B.dma("sp", "A.ogmst", fm(OGM)[:, :, tsl], ogm_st[:], [OGST], [])


                a_loads(0)
                a_norm(0)
                for i in range(NT + 1):
                    a_F1(i)
                    a_S1(i - 1)
                    a_S2(i - 1)
                    a_F2(i)
                    a_S3(i - 1)
                    a_S4(i - 1)
                    a_F3(i)
                    a_S5(i - 1)
                    a_F4(i)
                    if i >= 1:
                        a_tail_mm(i - 1)
            B.barrier()

        def stage_B(l):
            NG = T // 512
            NS = 3
            with ExitStack() as st:
                def sb(n, s, d):
                    return st.enter_context(nc.sbuf_tensor(f"B{l}_{n}", s, d))
                maskL_t = sb("maskL", [128, 4, 512], BF16)
                MK = Buf()
                B.dma("pool", "w", maskL_t[:], c_maskL, [], [MK])
                B.group_done("w", [MK])
                ktp = [sb(f"ktp{i}", [128, T], BF16) for i in range(2)]
                vp = [sb(f"vp{i}", [128, T // 128, 128], BF16) for i in range(2)]
                KV = [Buf(), Buf()]
                qs = [sb(f"q{i}", [128, 512], BF16) for i in range(2)]
                QS = [Buf(), Buf()]
                ex = [sb(f"ex{i}", [128, 2, 512], F32) for i in range(NS)]
                EX = [Buf() for _ in range(NS)]
                spt = [sb(f"sp{i}", [128, 2, 512], BF16) for i in range(NS)]
                SPT = [Buf() for _ in range(NS)]
                at = [sb(f"at{i}", [128, 2, 512], BF16) for i in range(NS)]
                AT = [Buf() for _ in range(NS)]
                r16 = [sb(f"r16_{i}", [128, 2, 512], BF16) for i in range(NS)]
                R16 = [Buf() for _ in range(NS)]
                ost = [sb(f"ost{i}", [128, 512], BF16) for i in range(2)]
                OST = [Buf(), Buf()]
                zb = [(0, 1), (0, 1)]
                sbks = [(2, 3), (4, 5)]
                ob = [6, 7]
                its = []
                for hp in range(4):
                    for g in range(NG):
                        nkb = 4 * g + 4
                        for pos, kb in enumerate(range(nkb - 1, -1, -1)):
                            its.append(dict(hp=hp, g=g, kb=kb, d=kb - 4 * g, first=(pos == 0), last=(kb == 0),
                                            gi=hp * NG + g, pos=pos))
                N = len(its)
                Vv = Vd.rearrange("(n p) c -> p n c", p=128)

                def load_hp(hp):
                    if hp > 3:
                        return
                    sl = hp % 2
                    B.dma("sp", f"B.kt{sl}", ktp[sl][:], KT[hp * 128:(hp + 1) * 128, :], [], [KV[sl]])
                    B.dma("sp", f"B.kt{sl}", vp[sl][:], Vv[:, :, hp * 128:(hp + 1) * 128], [], [KV[sl]])

                def load_q(gi):
                    if gi >= 4 * NG:
                        return
                    hp, g = gi // NG, gi % NG
                    B.dma("sp", f"B.q{gi % 2}", qs[gi % 2][:], QT[hp * 128:(hp + 1) * 128, g * 512:(g + 1) * 512],
                          [], [QS[gi % 2]])

                def Zf(k):
                    if not 0 <= k < N:
                        return
                    I = its[k]
                    kt, KVb = ktp[I["hp"] % 2], KV[I["hp"] % 2]
                    q, Q = qs[I["gi"] % 2], QS[I["gi"] % 2]
                    ksl = slice(I["kb"] * 128, (I["kb"] + 1) * 128)
                    for h in range(2):
                        pr = slice(h * 64, (h + 1) * 64)
                        zk = zb[k % 2][h]
                        B.mmg(ps[zk], [(kt[pr, ksl], q[pr, :])], [KVb, Q], [PS[zk]])

                def Ef(k):
                    if not 0 <= k < N:
                        return
                    z0, z1 = zb[k % 2]
                    B.act(ex[k % NS][:], ps_t[:, z0:z0 + 2, :], AF.Exp, [PS[z0], PS[z1]], [EX[k % NS]])

                def Lf(k):
                    if not 0 <= k < N:
                        return
                    I = its[k]
                    B.act(spt[k % NS][:], ex[k % NS][:], AF.Ln, [EX[k % NS]], [SPT[k % NS]], bias=1.0, nosame=True)
                    if I["d"] >= 0:
                        d = I["d"]
                        B.tt("dve", spt[k % NS][:], spt[k % NS][:],
                             maskL_t[:, d:d + 1, :].to_broadcast([128, 2, 512]), ALU.mult,
                             [SPT[k % NS], MK], [SPT[k % NS]])

                def Rf(k):
                    if not 0 <= k < N:
                        return
                    I = its[k]
                    if I["last"]:
                        return
                    if I["first"]:
                        B.cp("dve", r16[k % NS][:], spt[k % NS][:], [SPT[k % NS]], [R16[k % NS]])
                    else:
                        B.tt("dve", r16[k % NS][:], r16[(k - 1) % NS][:], spt[k % NS][:], ALU.add,
                             [SPT[k % NS], R16[(k - 1) % NS]], [R16[k % NS]])

                def Sf(k):
                    if not 0 <= k < N:
                        return
                    sbk = sbks[k % 2]
                    I = its[k]
                    kt, KVb = ktp[I["hp"] % 2], KV[I["hp"] % 2]
                    q, Q = qs[I["gi"] % 2], QS[I["gi"] % 2]
                    ksl = slice(I["kb"] * 128, (I["kb"] + 1) * 128)
                    for h in range(2):
                        pr = slice(h * 64, (h + 1) * 64)
                        pairs = [(negtri_t[:], spt[k % NS][:, h, :])]
                        rds = [SPT[k % NS], CONST, KVb, Q]
                        if not I["first"]:
                            pairs.append((nones_t[:], r16[(k - 1) % NS][:, h, :]))
                            rds.append(R16[(k - 1) % NS])
                        pairs.append((kt[pr, ksl], q[pr, :]))
                        B.mmg(ps[sbk[h]], pairs, rds, [PS[sbk[h]]])

                def Xf(k):
                    sbk = sbks[k % 2]
                    I = its[k]
                    B.act(at[k % NS][:], ps_t[:, sbk[0]:sbk[0] + 2, :], AF.Exp, [PS[sbk[0]], PS[sbk[1]]], [AT[k % NS]])
                    if I["d"] >= 0:
                        d = I["d"]
                        B.tt("pool", at[k % NS][:], at[k % NS][:],
                             maskL_t[:, d:d + 1, :].to_broadcast([128, 2, 512]), ALU.mult,
                             [AT[k % NS], MK], [AT[k % NS]])

                def AVf(k):
                    if not 0 <= k < N:
                        return
                    I = its[k]
                    vv, KVb = vp[I["hp"] % 2], KV[I["hp"] % 2]
                    o_bank = ob[I["gi"] % 2]
                    a_t = at[k % NS]
                    for h in range(2):
                        fn = (lambda e, h=h, a_t=a_t, kb=I["kb"], first=I["first"], last=I["last"], o_bank=o_bank, vv=vv:
                              e.matmul(ps_t[h * 64:(h + 1) * 64, o_bank, :], vv[:, kb, h * 64:(h + 1) * 64],
                                       a_t[:, h, :], start=first, stop=last))
                        B.op("pe", fn, [AT[k % NS], KVb], [PS[o_bank]])
                    if I["last"]:
                        gi = I["gi"]
                        B.cp("dve", ost[gi % 2][:], ps[o_bank], [PS[o_bank]], [OST[gi % 2]])
                        B.dma("sp", f"B.ost{gi % 2}",
                              OSB[I["hp"] * 128:(I["hp"] + 1) * 128, I["g"] * 512:(I["g"] + 1) * 512],
                              ost[gi % 2][:], [OST[gi % 2]], [])

                load_hp(0)
                load_q(0)
                Zf(0)
                Ef(0)
                Zf(1)
                Lf(0)
                Rf(0)
                Sf(0)
                for k in range(N):
                    I = its[k]
                    Ef(k + 1)
                    Lf(k + 1)
                    Rf(k + 1)
                    AVf(k - 1)
                    if I["pos"] == 1:
                        load_q(I["gi"] + 1)
                        if I["g"] == 1:
                            load_hp(I["hp"] + 1)
                    Zf(k + 2)
                    Sf(k + 1)
                    Xf(k)
                AVf(N - 1)
            B.barrier()

        def post_norm_residual(xt, XT, ysb, YSB, sq, SQ, rs, RS, tmp, TMP, l, gcol0, ncols, msbank):
            B.mmg(ps[msbank][:, 0:ncols], [(ones_t[:], sq[:, c, :]) for c in range(8)], [SQ, CONST], [PS[msbank]])
            B.act(rs[:], ps[msbank][:, 0:ncols], AF.Ln, [PS[msbank]], [RS], scale=1.0 / D, bias=EPS)
            B.act(rs[:], rs[:], AF.Exp, [RS], [RS], scale=-0.5)
            for c in range(8):
                B.stt("dve", tmp[c % 2][:], ysb[:, c, :], pv_t[:, l, gcol0 + c:gcol0 + c + 1], rs[:],
                      ALU.mult, ALU.mult, [YSB, RS, CONST], [TMP[c % 2]])
                B.tt("pool", xt[:, c, :], xt[:, c, :], tmp[c % 2][:], ALU.add, [TMP[c % 2], XT], [XT])

        def stage_D(l, xsrc):
            NT = DBG.get("NTD", T // 512)
            with ExitStack() as st:
                def sb(n, s, d):
                    return st.enter_context(nc.sbuf_tensor(f"D{l}_{n}", s, d))
                wscr_off[0] = 0
                wg = sb("wg", [128, 8, 3072], BF16)
                wbr = sb("wbr", [128, 8, 1024], BF16)
                wo = sb("wo", [128, 8, 1024], BF16)
                W = Buf()
                wv = w_in[l].rearrange("(c p) n -> p c n", p=128)
                for c in range(8):
                    for hf in range(2):
                        wload(wg[:, c, hf * 1536:(hf + 1) * 1536],
                              wv[:, c, 2304 + hf * 1536:2304 + (hf + 1) * 1536], [128, 1536], W)
                for c in range(4):
                    wload(wbr[:, c, :], w_br_sb[l, c * 128:(c + 1) * 128, :], [128, 1024], W)
                for c in range(2):
                    wload(wbr[:, 4 + c, :], w_br_pool[l, c * 128:(c + 1) * 128, :], [128, 1024], W)
                    wload(wbr[:, 6 + c, :], w_br_gm[l, c * 128:(c + 1) * 128, :], [128, 1024], W)
                for c in range(8):
                    wload(wo[:, c, :], w_out[l, c * 128:(c + 1) * 128, :], [128, 1024], W)
                xts = [sb(f"xt{i}", [128, 8, 512], F32) for i in range(2)]
                XTS = [Buf(), Buf()]
                brin = [sb(f"brin{i}", [128, 8, 512], BF16) for i in range(2)]
                BRIN = [Buf(), Buf()]
                sq = sb("sq", [128, 8, 512], BF16); SQ = Buf()
                hT = sb("hT", [128, 8, 512], BF16); HT = Buf()
                rs = sb("rs", [128, 512], F32); RS = Buf()
                gsb = [sb(f"gsb{i}", [128, 512], BF16) for i in range(3)]
                GSB = [Buf() for _ in range(3)]
                t0 = sb("t0", [128, 512], F32); T0 = Buf()
                t1 = sb("t1", [128, 512], F32); T1 = Buf()
                t2 = sb("t2", [128, 512], F32); T2 = Buf()
                merged = sb("merged", [128, 8, 512], BF16); MG = Buf()
                ysb = sb("ysb", [128, 8, 512], F32); YSB = Buf()
                tmp = [sb(f"tmp{i}", [128, 512], F32) for i in range(2)]; TMP = [Buf(), Buf()]
                xv = fm(xsrc)
                yv = fm(yT)
                yrot = Rot([6, 7])

                def d_loads(i):
                    if i >= NT:
                        return
                    tsl = slice(i * 512, (i + 1) * 512)
                    B.dma("sp", f"D.xt{i % 2}", xts[i % 2][:], xv[:, :, tsl], [], [XTS[i % 2]])
                    B.dma("sp", f"D.br{i % 2}", brin[i % 2][:, 0:4, :], fm(OSB)[:, :, tsl], [], [BRIN[i % 2]])
                    B.dma("sp", f"D.br{i % 2}", brin[i % 2][:, 4:6, :], fm(OPL)[:, :, tsl], [], [BRIN[i % 2]])
                    B.dma("sp", f"D.br{i % 2}", brin[i % 2][:, 6:8, :], fm(OGM)[:, :, tsl], [], [BRIN[i % 2]])

                def d_norm(i):
                    if i >= NT:
                        return
                    norm_tile(xts[i % 2], XTS[i % 2], sq, SQ, hT, HT, rs, RS, 0, l, 512, 6)

                d_loads(0)
                d_norm(0)
                for i in range(NT):
                    tsl = slice(i * 512, (i + 1) * 512)
                    xt, XT = xts[i % 2], XTS[i % 2]
                    bi, BI = brin[i % 2], BRIN[i % 2]
                    d_loads(i + 1)
                    for c in range(8):
                        csl = slice(c * 128, (c + 1) * 128)
                        for b in range(3):
                            B.mmg(ps[b], [(wg[:, k, b * 1024 + c * 128:b * 1024 + (c + 1) * 128], hT[:, k, :])
                                          for k in range(8)], [W, HT], [PS[b]])
                            B.act(gsb[b][:], ps[b], AF.Sigmoid, [PS[b]], [GSB[b]])
                        B.mmg(ps[3], [(wbr[:, k, csl], bi[:, k, :]) for k in range(0, 4)], [W, BI], [PS[3]])
                        B.mmg(ps[4], [(wbr[:, k, csl], bi[:, k, :]) for k in range(4, 6)], [W, BI], [PS[4]])
                        B.mmg(ps[5], [(wbr[:, k, csl], bi[:, k, :]) for k in range(6, 8)], [W, BI], [PS[5]])
                        B.tt("dve", t0[:], ps[3], gsb[0][:], ALU.mult, [PS[3], GSB[0]], [T0])
                        B.tt("dve", t1[:], ps[4], gsb[1][:], ALU.mult, [PS[4], GSB[1]], [T1])
                        B.tt("dve", t2[:], ps[5], gsb[2][:], ALU.mult, [PS[5], GSB[2]], [T2])
                        B.tt("pool", t0[:], t0[:], t1[:], ALU.add, [T1, T0], [T0])
                        B.tt("pool", merged[:, c, :], t0[:], t2[:], ALU.add, [T0, T2], [MG])
                    d_norm(i + 1)
                    for c in range(8):
                        csl = slice(c * 128, (c + 1) * 128)
                        bk = yrot.next()
                        B.mmg(ps[bk], [(wo[:, k, csl], merged[:, k, :]) for k in range(8)], [W, MG], [PS[bk]])
                        B.cp("dve", ysb[:, c, :], ps[bk], [PS[bk]], [YSB])
                        B.act(sq[:, c, :], ysb[:, c, :], AF.Square, [YSB], [SQ])
                    post_norm_residual(xt, XT, ysb, YSB, sq, SQ, rs, RS, tmp, TMP, l, 8, 512, 6)
                    B.dma("sp", f"D.xo{i % 2}", yv[:, :, tsl], xt[:], [XT], [])
            B.barrier()

        def stage_E(l):
            NW = 256
            NT = DBG.get("NTE", T // NW)
            with ExitStack() as st:
                def sb(n, s, d):
                    return st.enter_context(nc.sbuf_tensor(f"E{l}_{n}", s, d))
                wscr_off[0] = 0
                wf1 = sb("wf1", [128, 8, DFF], BF16)
                wf2 = sb("wf2", [128, 32, 1024], BF16)
                W = Buf()
                w1v = w_ff_in[l].rearrange("(c p) n -> p c n", p=128)
                w2v = w_ff_out[l].rearrange("(c p) n -> p c n", p=128)
                for c in range(8 if "nowf1" not in DBG else 0):
                    for hf in range(2):
                        wload(wf1[:, c, hf * 2048:(hf + 1) * 2048], w1v[:, c, hf * 2048:(hf + 1) * 2048], [128, 2048], W)
                for c in range(32 if "nowf2" not in DBG else 0):
                    wload(wf2[:, c, :], w_ff_out[l, c * 128:(c + 1) * 128, :], [128, 1024], W)
                xts = [sb(f"xt{i}", [128, 8, NW], F32) for i in range(2)]
                XTS = [Buf(), Buf()]
                sq = sb("sq", [128, 8, NW], BF16); SQ = Buf()
                hT = sb("hT", [128, 8, NW], BF16); HT = Buf()
                rs = sb("rs", [128, NW], F32); RS = Buf()
                rl = [sb(f"rl{i}", [128, 2, NW], F32) for i in range(2)]
                RL = [Buf(), Buf()]
                aT = sb("aT", [128, 32, NW], BF16); ATB = Buf()
                ysb = sb("ysb", [128, 8, NW], F32); YSB = Buf()
                tmp = [sb(f"tmp{i}", [128, NW], F32) for i in range(2)]; TMP = [Buf(), Buf()]
                yv = fm(yT)
                arot = Rot([0, 1, 2, 3])
                yrot = Rot([4, 5])
                ri = 0
                def e_loads(i):
                    if i >= NT:
                        return
                    B.dma("sp", f"E.xt{i % 2}", xts[i % 2][:], yv[:, :, i * NW:(i + 1) * NW], [], [XTS[i % 2]])

                def e_norm(i):
                    if i >= NT:
                        return
                    norm_tile(xts[i % 2], XTS[i % 2], sq, SQ, hT, HT, rs, RS, 16, l, NW, 6)

                EP = DBG.get("Eparts", "nabps")
                e_loads(0)
                e_norm(0)
                for i in range(NT):
                    tsl = slice(i * NW, (i + 1) * NW)
                    xt, XT = xts[i % 2], XTS[i % 2]
                    e_loads(i + 1)
                    for j2 in (range(16) if "a" in EP else []):
                        bk = arot.next()
                        for jj in range(2):
                            j = j2 * 2 + jj
                            B.mmg(ps[bk][:, jj * NW:(jj + 1) * NW],
                                  [(wf1[:, k, j * 128:(j + 1) * 128], hT[:, k, :]) for k in range(8)],
                                  [W, HT], [PS[bk]])
                        r, R = rl[ri % 2], RL[ri % 2]
                        ri += 1
                        B.act(r[:], ps[bk].rearrange("p (a n) -> p a n", a=2), AF.Relu, [PS[bk]], [R])
                        B.tt("pool", aT[:, j2 * 2:j2 * 2 + 2, :], r[:], r[:], ALU.mult, [R], [ATB])
                    e_norm(i + 1)
                    for c2 in (range(4) if "b" in EP else []):
                        bk = yrot.next()
                        for cc in range(2):
                            c = c2 * 2 + cc
                            B.mmg(ps[bk][:, cc * NW:(cc + 1) * NW],
                                  [(wf2[:, k, c * 128:(c + 1) * 128], aT[:, k, :]) for k in range(32)],
                                  [W, ATB], [PS[bk]])
                        pv2 = ps[bk].rearrange("p (a n) -> p a n", a=2)
                        B.cp("dve", ysb[:, c2 * 2:c2 * 2 + 2, :], pv2, [PS[bk]], [YSB])
                        B.act(sq[:, c2 * 2:c2 * 2 + 2, :], ysb[:, c2 * 2:c2 * 2 + 2, :], AF.Square, [YSB], [SQ])
                    if "p" in EP:
                        post_norm_residual(xt, XT, ysb, YSB, sq, SQ, rs, RS, tmp, TMP, l, 24, NW, 6)
                    if "s" in EP:
                        B.dma("sp", f"E.xo{i % 2}", yv[:, :, tsl], xt[:], [XT], [])
            B.barrier()

        for l in range(nlayers):
            xsrc = xin if l == 0 else yT
            if "A" in stages:
                stage_A(l, xsrc)
            if "B" in stages:
                stage_B(l)
            if "D" in stages:
                stage_D(l, xsrc)
            if "E" in stages:
                stage_E(l)
        B.final()
        with nc.Block() as block:
            B.emit(block)
    return nc


def _consts():
    p = np.arange(128)
    tril = (p[:, None] <= p[None, :]).astype(np.float32)
    negtri = -(p[:, None] >= p[None, :]).astype(np.float32)
    maskL = np.zeros((128, 4, 512), np.float32)
    qidx = np.arange(512)
    for d in range(4):
        kidx = d * 128 + p
        maskL[:, d, :] = (kidx[:, None] < qidx[None, :]).astype(np.float32)
    invcnt = np.zeros((2, 256, 512), np.float32)
    pos = np.arange(512, dtype=np.float32)
    for g, w in enumerate((2, 4, 8, 16)):
        invcnt[0, g * 64:(g + 1) * 64, :] = 1.0 / np.minimum(pos + 1.0, float(w))[None, :]
        invcnt[1, g * 64:(g + 1) * 64, :] = 1.0 / float(w)
    return tril, negtri, maskL, invcnt


_NC_CACHE = {}


def kernel(x, w_in, w_pool, pool_scale, gm_gain, w_spatial, b_spatial, w_br_sb, w_br_pool,
           w_br_gm, w_out, g_mix_pre, g_mix_post, g_ff_pre, g_ff_post, w_ff_in, w_ff_out):
    f = lambda a: np.ascontiguousarray(np.asarray(a, dtype=np.float32))
    x = f(x)
    tril, negtri, maskL, invcnt = _consts()
    def pc(v, n):
        return f(v).reshape(DEPTH, n, 128).transpose(0, 2, 1)
    pvec = np.concatenate([pc(g_mix_pre, 8), pc(g_mix_post, 8), pc(g_ff_pre, 8), pc(g_ff_post, 8),
                           pc(pool_scale, 2)], axis=2)
    shared = {
        "w_in": f(w_in), "w_pool": f(w_pool), "gm_gain": f(gm_gain),
        "w_spT": np.ascontiguousarray(f(w_spatial).transpose(0, 1, 3, 2)),
        "b_spatial": f(b_spatial), "w_br_sb": f(w_br_sb), "w_br_pool": f(w_br_pool),
        "w_br_gm": f(w_br_gm), "w_out": f(w_out), "w_ff_in": f(w_ff_in), "w_ff_out": f(w_ff_out),
        "pvec": np.ascontiguousarray(pvec), "c_tril": tril, "c_negtri": negtri, "c_maskL": maskL,
        "c_invcnt": invcnt,
    }
    if "nc" not in _NC_CACHE:
        _NC_CACHE["nc"] = build()
    nc = _NC_CACHE["nc"]
    in_maps = []
    for c in range(NCORES):
        m = dict(shared)
        m["xT"] = np.ascontiguousarray(x[c].T)
        in_maps.append(m)
    res = run_bass_kernel_spmd(nc, in_maps, core_ids=list(range(NCORES)))
    out = np.stack([np.ascontiguousarray(res.results[c]["yT"].T) for c in range(NCORES)], axis=0)
    return out.astype(np.float32)
```

```python
import numpy as np
from contextlib import ExitStack
import concourse.bass as bass
import concourse.mybir as mybir
from concourse.bass_utils import run_bass_kernel_spmd

F32 = mybir.dt.float32
BF16 = mybir.dt.bfloat16
AF = mybir.ActivationFunctionType
ALU = mybir.AluOpType
AX = mybir.AxisListType

D = 1024
T = 8192
DEPTH = 4
DIN = 5376
DFF = 4096
EPS = 1e-6
SEM_ROT = 30000
NCORES = 4
GELU_C = 1.5957691216057308
DBG = {}


class Buf:
    __slots__ = ("wr", "rd")

    def __init__(self):
        self.wr = {}
        self.rd = {}


class Builder:
    def __init__(self, nc):
        self.nc = nc
        self.q = {k: [] for k in ("pe", "act", "dve", "pool", "sp")}
        self.sems = []
        self.esem = {}
        self.ecnt = {}
        self.dsem = {}
        self.dcnt = {}
        self.last = {}
        self.waited = {k: {} for k in self.q}
        self.pend = {}
        self.sem_eng = {}

    def newsem(self):
        self.sems.append(self.nc.alloc_semaphore(f"sm{len(self.sems)}"))
        return len(self.sems) - 1

    def _push(self, eng, fn, waits, sem, inc, nosame=False):
        pend = self.pend.pop(eng, None)
        if pend:
            for k, v in pend.items():
                if waits.get(k, 0) < v:
                    waits[k] = v
        ws = []
        wd = self.waited[eng]
        for k, v in waits.items():
            if wd.get(k, 0) >= v:
                continue
            if (eng == "pe" or nosame) and self.sem_eng.get(k) == eng:
                continue
            wd[k] = v
            ws.append((k, v))
        self.q[eng].append((ws, fn, sem, inc))

    @staticmethod
    def _merge(dst, src):
        for k, v in src.items():
            if dst.get(k, 0) < v:
                dst[k] = v

    def _deps(self, reads, writes):
        w = {}
        for b in reads:
            self._merge(w, b.wr)
        for b in writes:
            self._merge(w, b.wr)
            self._merge(w, b.rd)
        return w

    def _mark(self, reads, writes, sem, val):
        for b in reads:
            if b.rd.get(sem, 0) < val:
                b.rd[sem] = val
        for b in writes:
            b.wr = {sem: val}
            b.rd = {}

    def op(self, eng, fn, reads=(), writes=(), nosame=False):
        if eng not in self.esem or self.ecnt[eng] >= SEM_ROT:
            self.esem[eng] = self.newsem()
            self.sem_eng[self.esem[eng]] = eng
            self.ecnt[eng] = 0
        w = self._deps(reads, writes)
        self.ecnt[eng] += 1
        sem = self.esem[eng]
        val = self.ecnt[eng]
        self._push(eng, fn, w, sem, 1, nosame)
        self._mark(reads, writes, sem, val)
        self.last[("e", eng)] = (sem, val)

    def raw(self, eng, fn):
        self.q[eng].append(([], fn, None, 0))

    def dma(self, eng, key, out, in_, reads=(), writes=()):
        if key not in self.dsem:
            self.dsem[key] = self.newsem()
            self.dcnt[key] = 0
        w = self._deps(reads, writes)
        self.dcnt[key] += 16
        sem = self.dsem[key]
        val = self.dcnt[key]
        self._push(eng, lambda e: e.dma_start(out=out, in_=in_), w, sem, 16)
        self._mark(reads, writes, sem, val)
        self.last[("d", key)] = (sem, val)

    def group_done(self, key, bufs):
        sem = self.dsem[key]
        val = self.dcnt[key]
        for b in bufs:
            b.wr = {sem: val}

    def barrier(self):
        toks = {}
        for (sem, val) in self.last.values():
            if toks.get(sem, 0) < val:
                toks[sem] = val
        for eng in self.q:
            self.pend[eng] = dict(toks)

    def final(self):
        self.barrier()
        for eng in self.q:
            self._push(eng, lambda e: e.nop(), {}, None, 0)

    def emit(self, block):
        decs = {"pe": block.tensor, "act": block.scalar, "dve": block.vector,
                "pool": block.gpsimd, "sp": block.sync}
        for k, dec in decs.items():
            def f(e, k=k):
                sems = self.sems
                for ws, fn, sem, inc in self.q[k]:
                    for (wi, wv) in ws:
                        e.wait_ge(sems[wi], wv)
                    ins = fn(e)
                    if sem is not None:
                        ins.then_inc(sems[sem], inc)
            dec(f)

    def mmg(self, out, pairs, reads, writes, start=True, stop=True):
        n = len(pairs)
        for i, (l, r) in enumerate(pairs):
            st = start and i == 0
            sp = stop and i == n - 1
            fn = (lambda e, l=l, r=r, st=st, sp=sp: e.matmul(out, l, r, start=st, stop=sp))
            if n == 1:
                self.op("pe", fn, reads, writes)
            elif i == 0:
                w = self._deps(reads, writes)
                self._push("pe", fn, w, None, 0)
            elif i == n - 1:
                self.op("pe", fn, reads, writes)
            else:
                self.raw("pe", fn)

    def act(self, out, in_, func, reads, writes, scale=1.0, bias=0.0, nosame=False):
        self.op("act", lambda e: e.activation(out=out, in_=in_, func=func, bias=bias, scale=scale),
                reads, writes, nosame)

    def tt(self, eng, out, in0, in1, op, reads, writes):
        self.op(eng, lambda e: e.tensor_tensor(out=out, in0=in0, in1=in1, op=op), reads, writes)

    def ts(self, eng, out, in0, s1, s2, op0, op1, reads, writes):
        self.op(eng, lambda e: e.tensor_scalar(out=out, in0=in0, scalar1=s1, scalar2=s2, op0=op0, op1=op1),
                reads, writes)

    def ts1(self, eng, out, in0, s1, op0, reads, writes):
        self.op(eng, lambda e: e.tensor_scalar(out=out, in0=in0, scalar1=s1, scalar2=None, op0=op0),
                reads, writes)

    def stt(self, eng, out, in0, scalar, in1, op0, op1, reads, writes):
        self.op(eng, lambda e: e.scalar_tensor_tensor(out=out, in0=in0, scalar=scalar, in1=in1,
                                                      op0=op0, op1=op1), reads, writes)

    def cp(self, eng, out, in_, reads, writes):
        self.op(eng, lambda e: e.tensor_copy(out=out, in_=in_), reads, writes)


class Rot:
    def __init__(self, items):
        self.items = items
        self.i = 0

    def next(self):
        it = self.items[self.i % len(self.items)]
        self.i += 1
        return it


def build(nlayers=DEPTH, dbg=False, stages="ABDE"):
    nc = bass.Bass("TRN2", target_bir_lowering=False)

    def din(name, shape):
        return nc.dram_tensor(name, shape, F32, kind="ExternalInput").ap()

    xin = din("xT", [D, T])
    w_in = din("w_in", [DEPTH, D, DIN])
    w_pool = din("w_pool", [DEPTH, 4, 64, 64])
    gm_gain = din("gm_gain", [DEPTH, 256])
    w_spT = din("w_spT", [DEPTH, 4, 128, 128])
    b_spatial = din("b_spatial", [DEPTH, 4, 128])
    w_br_sb = din("w_br_sb", [DEPTH, 512, 1024])
    w_br_pool = din("w_br_pool", [DEPTH, 256, 1024])
    w_br_gm = din("w_br_gm", [DEPTH, 256, 1024])
    w_out = din("w_out", [DEPTH, 1024, 1024])
    w_ff_in = din("w_ff_in", [DEPTH, 1024, DFF])
    w_ff_out = din("w_ff_out", [DEPTH, DFF, 1024])
    pvec = din("pvec", [DEPTH, 128, 34])
    c_tril = din("c_tril", [128, 128])
    c_negtri = din("c_negtri", [128, 128])
    c_maskL = din("c_maskL", [128, 4, 512])
    c_invcnt = din("c_invcnt", [2, 256, 512])

    kind_s = "ExternalOutput" if dbg else "Internal"
    yT = nc.dram_tensor("yT", [D, T], F32, kind="ExternalOutput").ap()
    QT = nc.dram_tensor("QT", [512, T], BF16, kind=kind_s).ap()
    KT = nc.dram_tensor("KT", [512, T], BF16, kind=kind_s).ap()
    Vd = nc.dram_tensor("Vd", [T, 512], BF16, kind=kind_s).ap()
    OSB = nc.dram_tensor("OSB", [512, T], BF16, kind=kind_s).ap()
    OPL = nc.dram_tensor("OPL", [256, T], BF16, kind=kind_s).ap()
    OGM = nc.dram_tensor("OGM", [256, T], BF16, kind=kind_s).ap()

    B = Builder(nc)

    def fm(ap):
        return ap.rearrange("(c p) t -> p c t", p=128)

    with ExitStack() as top:
        ps_t = top.enter_context(nc.psum_tensor("ps", [128, 8, 512], F32))
        ps = [ps_t[:, k, :] for k in range(8)]
        PS = [Buf() for _ in range(8)]

        ones_t = top.enter_context(nc.sbuf_tensor("ones", [128, 128], BF16))
        nones_t = top.enter_context(nc.sbuf_tensor("nones", [128, 128], BF16))
        negtri_t = top.enter_context(nc.sbuf_tensor("negtri", [128, 128], BF16))
        tril_t = top.enter_context(nc.sbuf_tensor("tril", [128, 128], F32))
        pv_t = top.enter_context(nc.sbuf_tensor("pv", [128, DEPTH, 34], F32))
        CONST = Buf()
        B.op("pool", lambda e: e.memset(ones_t[:], 1.0), [], [CONST])
        B.op("pool", lambda e: e.memset(nones_t[:], -1.0), [], [CONST])
        B.dma("pool", "w", negtri_t[:], c_negtri, [], [CONST])
        B.dma("pool", "w", tril_t[:], c_tril, [], [CONST])
        B.dma("pool", "w", pv_t[:], pvec.rearrange("l p c -> p l c"), [], [CONST])
        B.group_done("w", [CONST])
        B.barrier()

        NWS = 6
        wst = [top.enter_context(nc.sbuf_tensor(f"wst{i}", [128, 512], F32)) for i in range(NWS)]
        WST = [Buf() for _ in range(NWS)]
        wscr_off = [0]
        wl_i = [0]

        def wload(dst, src, shape, Wb):
            n = shape[1]
            if n > 512:
                for o in range(0, n, 512):
                    m = min(512, n - o)
                    wload(dst[:, o:o + m], src[:, o:o + m], [128, m], Wb)
                return
            k = wl_i[0] % NWS
            wl_i[0] += 1
            B.dma("sp", f"wst{k}", wst[k][:, 0:n], src, [], [WST[k]])
            B.cp("dve" if k % 2 == 0 else "pool", dst, wst[k][:, 0:n], [WST[k]], [Buf()])

        def wdone(Wb):
            for eng in ("dve", "pool"):
                tok = B.last.get(("e", eng))
                if tok is not None and Wb.wr.get(tok[0], 0) < tok[1]:
                    Wb.wr[tok[0]] = tok[1]

        def norm_tile(xt, XT, sq, SQ, hT, HT, rs, RS, gcol0, l, ncols, msbank):
            B.act(sq[:], xt[:], AF.Square, [XT], [SQ])
            B.mmg(ps[msbank][:, 0:ncols], [(ones_t[:], sq[:, c, :]) for c in range(8)],
                  [SQ, CONST], [PS[msbank]])
            B.act(rs[:], ps[msbank][:, 0:ncols], AF.Ln, [PS[msbank]], [RS], scale=1.0 / D, bias=EPS)
            B.act(rs[:], rs[:], AF.Exp, [RS], [RS], scale=-0.5)
            for c in range(8):
                B.stt("dve", hT[:, c, :], xt[:, c, :], pv_t[:, l, gcol0 + c:gcol0 + c + 1], rs[:],
                      ALU.mult, ALU.mult, [XT, RS, CONST], [HT])

        def gelu(eng2, x, X, tmp, TMP, sg, SG, out, OUT):
            B.tt("dve", tmp, x, x, ALU.mult, [X], [TMP])
            B.ts("dve", tmp, tmp, 0.044715, 1.0, ALU.mult, ALU.add, [TMP], [TMP])
            B.tt("dve", tmp, tmp, x, ALU.mult, [TMP, X], [TMP])
            B.act(sg, tmp, AF.Sigmoid, [TMP], [SG], scale=GELU_C)
            B.tt(eng2, out, x, sg, ALU.mult, [X, SG], [OUT])

        def stage_A(l, xsrc):
            NT = DBG.get("NT", T // 512)
            parts = DBG.get("parts", "qk,pu,v,gv,pool,gm")
            with ExitStack() as st:
                def sb(n, s, d):
                    return st.enter_context(nc.sbuf_tensor(f"A{l}_{n}", s, d))
                wa = sb("wa", [128, 8, 2304], BF16)
                wpbd = sb("wpbd", [128, 2, 128], BF16)
                wsT32 = sb("wsT32", [128, 4, 128], F32)
                wsT = sb("wsT", [128, 4, 128], BF16)
                brep = sb("brep", [128, 2, 128], F32)
                gainrep = sb("gainrep", [128, 256], F32)
                ic_t = sb("ic", [128, 2, 2, 512], F32)
                W = Buf()
                for wch in range(2):
                    B.dma("pool", "w", ic_t[:, wch, :, :], c_invcnt[wch].rearrange("(j p) t -> p j t", p=128), [], [W])
                B.op("pool", lambda e: e.memset(wpbd[:], 0.0), [], [W])
                wv = w_in[l].rearrange("(c p) n -> p c n", p=128)
                for c in range(8):
                    wload(wa[:, c, :], wv[:, c, 0:2304], [128, 2304], W)
                for g in range(4):
                    j, hh = g // 2, g % 2
                    B.dma("pool", "w", wpbd[hh * 64:(hh + 1) * 64, j, hh * 64:(hh + 1) * 64],
                          w_pool[l, g], [], [W])
                    B.dma("pool", "w", brep[hh * 64:(hh + 1) * 64, j, :],
                          b_spatial[l, g, :].partition_broadcast(64), [], [W])
                B.dma("pool", "w", wsT32[:], w_spT[l].rearrange("g p t -> p g t"), [], [W])
                B.dma("pool", "w", gainrep[:], gm_gain[l, :].partition_broadcast(128), [], [W])
                B.group_done("w", [W])
                wdone(W)
                for g in range(4):
                    B.tt("dve", wsT[:, g, :], wsT32[:, g, :], tril_t[:], ALU.mult, [W, CONST], [W])

                xts = [sb(f"xt{i}", [128, 8, 512], F32) for i in range(2)]
                XTS = [Buf(), Buf()]
                sq = sb("sq", [128, 8, 512], BF16); SQ = Buf()
                hT = sb("hT", [128, 8, 512], BF16); HT = Buf()
                rs = sb("rs", [128, 512], F32); RS = Buf()
                qk_st = sb("qkst", [128, 8, 512], BF16); QST = Buf(); KST = Buf()
                v_st = sb("vst", [128, 4, 512], BF16); VST = Buf()
                ptile = sb("ptile", [128, 2, 528], F32); PT = Buf()
                s2h = sb("s2h", [128, 2, 528], F32); s4h = sb("s4h", [128, 2, 528], F32)
                s8h = sb("s8h", [128, 2, 528], F32); s16h = sb("s16h", [128, 2, 528], F32)
                SH = Buf()
                ptmp = sb("ptmp", [128, 2, 512], F32); PTMP = Buf()
                pooled = sb("pooled", [128, 2, 512], BF16); PLD = Buf()
                opl_st = sb("oplst", [128, 2, 512], BF16); OPST = Buf()
                uf = sb("uf", [128, 2, 512], F32); UF = Buf()
                utmp = sb("utmp", [128, 2, 512], F32); UTMP = Buf()
                usg = sb("usg", [128, 2, 512], F32); USG = Buf()
                ug = sb("ug", [128, 2, 512], F32); UG = Buf()
                gvf = sb("gvf", [128, 4, 256], F32); GVF = Buf()
                gtmp = sb("gtmp", [128, 4, 256], F32); GTMP = Buf()
                gsg = sb("gsg", [128, 4, 256], F32); GSG = Buf()
                gvg = sb("gvg", [128, 4, 256], F32); GVG = Buf()
                ss = sb("ss", [128, 4], F32); SS = Buf()
                gvn = sb("gvn", [128, 4, 256], BF16); GVN = Buf()
                mtmp = sb("mtmp", [128, 2, 512], F32); MTMP = Buf()
                ogm_st = sb("ogmst", [128, 2, 512], BF16); OGST = Buf()
                B.op("pool", lambda e: e.memset(ptile[:], 0.0), [], [PT])

                rot = Rot([2, 3, 4, 5])
                xv = fm(xsrc)
                def a_loads(i):
                    if i >= NT:
                        return
                    B.dma("sp", f"A.xt{i % 2}", xts[i % 2][:], xv[:, :, i * 512:(i + 1) * 512], [], [XTS[i % 2]])

                def a_norm(i):
                    if i >= NT:
                        return
                    norm_tile(xts[i % 2], XTS[i % 2], sq, SQ, hT, HT, rs, RS, 0, l, 512, 0)

                pts = [ptile, sb("ptile2", [128, 2, 528], F32)]
                PTS = [PT, Buf()]
                B.op("pool", lambda e: e.memset(pts[1][:], 0.0), [], [PTS[1]])
                ufs = [uf, sb("uf2", [128, 2, 512], F32)]
                UFS = [UF, Buf()]
                gvfs = [gvf, sb("gvf2", [128, 4, 256], F32)]
                GVFS = [GVF, Buf()]

                def a_F1(i):
                    if i < 0 or i >= NT:
                        return
                    tsl = slice(i * 512, (i + 1) * 512)
                    ptile, PT = pts[i % 2], PTS[i % 2]
                    uf, UF = ufs[i % 2], UFS[i % 2]
                    gvf, GVF = gvfs[i % 2], GVFS[i % 2]
                    a_loads(i + 1)
                    for j in (range(8) if "qk" in parts else []):
                        bk = rot.next()
                        B.mmg(ps[bk], [(wa[:, c, j * 128:(j + 1) * 128], hT[:, c, :]) for c in range(8)],
                              [W, HT], [PS[bk]])
                        if j < 4:
                            B.act(qk_st[:, j, :], ps[bk], AF.Identity, [PS[bk]], [QST], scale=0.125)
                        else:
                            B.cp("dve", qk_st[:, j, :], ps[bk], [PS[bk]], [KST])
                    if "qk" in parts:
                        B.dma("sp", "A.qst", fm(QT)[:, :, tsl], qk_st[:, 0:4, :], [QST], [])
                        B.dma("sp", "A.kst", fm(KT)[:, :, tsl], qk_st[:, 4:8, :], [KST], [])

                def a_F2(i):
                    if i < 0 or i >= NT:
                        return
                    tsl = slice(i * 512, (i + 1) * 512)
                    ptile, PT = pts[i % 2], PTS[i % 2]
                    uf, UF = ufs[i % 2], UFS[i % 2]
                    gvf, GVF = gvfs[i % 2], GVFS[i % 2]
                    if "pu" in parts:
                        for j in range(2):
                            bk = rot.next()
                            B.mmg(ps[bk], [(wa[:, c, 1536 + j * 128:1536 + (j + 1) * 128], hT[:, c, :])
                                           for c in range(8)], [W, HT], [PS[bk]])
                            B.act(ptile[:, j, 16:528], ps[bk], AF.Identity, [PS[bk]], [PT])
                        for j in range(2):
                            bk = rot.next()
                            B.mmg(ps[bk], [(wa[:, c, 1792 + j * 128:1792 + (j + 1) * 128], hT[:, c, :])
                                           for c in range(8)], [W, HT], [PS[bk]])
                            B.act(uf[:, j, :], ps[bk], AF.Identity, [PS[bk]], [UF])

                def a_F3(i):
                    if i < 0 or i >= NT:
                        return
                    tsl = slice(i * 512, (i + 1) * 512)
                    ptile, PT = pts[i % 2], PTS[i % 2]
                    uf, UF = ufs[i % 2], UFS[i % 2]
                    gvf, GVF = gvfs[i % 2], GVFS[i % 2]
                    if "v" in parts:
                        for b in range(4):
                            bk = rot.next()
                            B.mmg(ps[bk], [(hT[:, c, b * 128:(b + 1) * 128], wa[:, c, 1024:1536])
                                           for c in range(8)], [W, HT], [PS[bk]])
                            B.cp("dve", v_st[:, b, :], ps[bk], [PS[bk]], [VST])
                        B.dma("sp", "A.vst", Vd.rearrange("(n p) c -> p n c", p=128)[:, i * 4:(i + 1) * 4, :],
                              v_st[:], [VST], [])

                def a_F4(i):
                    if i < 0 or i >= NT:
                        return
                    tsl = slice(i * 512, (i + 1) * 512)
                    ptile, PT = pts[i % 2], PTS[i % 2]
                    uf, UF = ufs[i % 2], UFS[i % 2]
                    gvf, GVF = gvfs[i % 2], GVFS[i % 2]
                    if "gv" in parts:
                        for b2 in range(2):
                            bk = rot.next()
                            for bb in range(2):
                                b = b2 * 2 + bb
                                B.mmg(ps[bk][:, bb * 256:(bb + 1) * 256],
                                      [(hT[:, c, b * 128:(b + 1) * 128], wa[:, c, 2048:2304]) for c in range(8)],
                                      [W, HT], [PS[bk]])
                            B.act(gvf[:, b2 * 2:b2 * 2 + 2, :], ps[bk].rearrange("p (a n) -> p a n", a=2), AF.Identity,
                                  [PS[bk]], [GVF])
                    a_norm(i + 1)

                def a_S1(i):
                    if i < 0 or i >= NT:
                        return
                    tsl = slice(i * 512, (i + 1) * 512)
                    ptile, PT = pts[i % 2], PTS[i % 2]
                    uf, UF = ufs[i % 2], UFS[i % 2]
                    gvf, GVF = gvfs[i % 2], GVFS[i % 2]
                    if "pool" in parts:
                        B.tt("pool", s2h[:, :, 1:528], ptile[:, :, 1:528], ptile[:, :, 0:527], ALU.add, [PT], [SH])
                        B.tt("pool", s4h[:, :, 3:528], s2h[:, :, 3:528], s2h[:, :, 1:526], ALU.add, [SH], [SH])
                        B.tt("pool", s8h[:, :, 7:528], s4h[:, :, 7:528], s4h[:, :, 3:524], ALU.add, [SH], [SH])
                        B.tt("pool", s16h[:, :, 15:528], s8h[:, :, 15:528], s8h[:, :, 7:520], ALU.add, [SH], [SH])

                def a_S2(i):
                    if i < 0 or i >= NT:
                        return
                    tsl = slice(i * 512, (i + 1) * 512)
                    ptile, PT = pts[i % 2], PTS[i % 2]
                    uf, UF = ufs[i % 2], UFS[i % 2]
                    gvf, GVF = gvfs[i % 2], GVFS[i % 2]
                    if "gm" in parts:
                        gelu("pool", uf[:], UF, utmp[:], UTMP, usg[:], USG, ug[:], UG)

                def a_S3(i):
                    if i < 0 or i >= NT:
                        return
                    tsl = slice(i * 512, (i + 1) * 512)
                    ptile, PT = pts[i % 2], PTS[i % 2]
                    uf, UF = ufs[i % 2], UFS[i % 2]
                    gvf, GVF = gvfs[i % 2], GVFS[i % 2]
                    if "gm" in parts:
                        gelu("pool", gvf[:], GVF, gtmp[:], GTMP, gsg[:], GSG, gvg[:], GVG)

                def a_S4(i):
                    if i < 0 or i >= NT:
                        return
                    tsl = slice(i * 512, (i + 1) * 512)
                    ptile, PT = pts[i % 2], PTS[i % 2]
                    uf, UF = ufs[i % 2], UFS[i % 2]
                    gvf, GVF = gvfs[i % 2], GVFS[i % 2]
                    if "pool" in parts:
                        which = 0 if i == 0 else 1
                        for g, sh in enumerate((s2h, s4h, s8h, s16h)):
                            j, hh = g // 2, g % 2
                            prt = slice(hh * 64, (hh + 1) * 64)
                            B.tt("pool", ptmp[prt, j, :], sh[prt, j, 16:528], ic_t[prt, which, j, :], ALU.mult,
                                 [SH, W], [PTMP])
                        B.tt("pool", pooled[:], ptmp[:], ptile[:, :, 16:528], ALU.subtract, [PTMP, PT], [PLD])
                        B.cp("pool", pts[(i + 1) % 2][:, :, 0:16], ptile[:, :, 512:528], [PT, SH, PLD], [PTS[(i + 1) % 2]])

                def a_S5(i):
                    if i < 0 or i >= NT:
                        return
                    tsl = slice(i * 512, (i + 1) * 512)
                    ptile, PT = pts[i % 2], PTS[i % 2]
                    uf, UF = ufs[i % 2], UFS[i % 2]
                    gvf, GVF = gvfs[i % 2], GVFS[i % 2]
                    if "gm" in parts:
                        B.tt("dve", gtmp[:], gvg[:], gvg[:], ALU.mult, [GVG], [GTMP])
                        B.op("dve", lambda e: e.reduce_sum(out=ss[:], in_=gtmp[:], axis=AX.X), [GTMP], [SS])
                        B.act(ss[:], ss[:], AF.Ln, [SS], [SS], scale=1.0 / 256, bias=EPS)
                        B.act(ss[:], ss[:], AF.Exp, [SS], [SS], scale=-0.5)
                        for b in range(4):
                            B.stt("dve", gvn[:, b, :], gvg[:, b, :], ss[:, b:b + 1], gainrep[:], ALU.mult, ALU.mult,
                                  [GVG, SS, W], [GVN])


                def a_tail_mm(i):
                    tsl = slice(i * 512, (i + 1) * 512)
                    ptile, PT = pts[i % 2], PTS[i % 2]
                    uf, UF = ufs[i % 2], UFS[i % 2]
                    gvf, GVF = gvfs[i % 2], GVFS[i % 2]
                    if "pool" in parts:
                        for j in range(2):
                            bk = rot.next()
                            B.mmg(ps[bk], [(wpbd[:, j, :], pooled[:, j, :])], [W, PLD], [PS[bk]])
                            B.ts1("dve", opl_st[:, j, :], ps[bk], pv_t[:, l, 32 + j:33 + j], ALU.mult,
                                  [PS[bk], CONST], [OPST])
                        B.dma("sp", "A.oplst", fm(OPL)[:, :, tsl], opl_st[:], [OPST], [])
                    if "gm" in parts:
                        for b in range(4):
                            for g in range(4):
                                j, hh = g // 2, g % 2
                                B.mmg(ps_t[hh * 64:(hh + 1) * 64, 6 + j, b * 128:(b + 1) * 128],
                                      [(gvn[:, b, g * 64:(g + 1) * 64], wsT[:, g, :])], [GVN, W], [PS[6], PS[7]])
                        for b in range(4):
                            B.tt("dve", mtmp[:, :, b * 128:(b + 1) * 128], ps_t[:, 6:8, b * 128:(b + 1) * 128], brep[:],
                                 ALU.add, [PS[6], PS[7], W], [MTMP])
                        B.tt("pool", ogm_st[:], mtmp[:], ug[:], ALU.mult, [MTMP, UG], [OGST])
                        B.dma("sp", "A.ogmst", fm(OGM)[:, :, tsl], ogm_st[:], [OGST], [])


                a_loads(0)
                a_norm(0)
                for i in range(NT + 1):
                    a_F1(i)
                    a_S1(i - 1)
                    a_S2(i - 1)
                    a_F2(i)
                    a_S3(i - 1)
                    a_S4(i - 1)
                    a_F3(i)
                    a_S5(i - 1)
                    a_F4(i)
                    if i >= 1:
                        a_tail_mm(i - 1)
            B.barrier()

        def stage_B(l):
            NG = T // 512
            NS = 3
            with ExitStack() as st:
                def sb(n, s, d):
                    return st.enter_context(nc.sbuf_tensor(f"B{l}_{n}", s, d))
                maskL_t = sb("maskL", [128, 4, 512], BF16)
                MK = Buf()
                B.dma("pool", "w", maskL_t[:], c_maskL, [], [MK])
                B.group_done("w", [MK])
                ktp = [sb(f"ktp{i}", [128, T], BF16) for i in range(2)]
                vp = [sb(f"vp{i}", [128, T // 128, 128], BF16) for i in range(2)]
                KV = [Buf(), Buf()]
                qs = [sb(f"q{i}", [128, 512], BF16) for i in range(2)]
                QS = [Buf(), Buf()]
                ex = [sb(f"ex{i}", [128, 2, 512], F32) for i in range(NS)]
                EX = [Buf() for _ in range(NS)]
                spt = [sb(f"sp{i}", [128, 2, 512], BF16) for i in range(NS)]
                SPT = [Buf() for _ in range(NS)]
                at = [sb(f"at{i}", [128, 2, 512], BF16) for i in range(NS)]
                AT = [Buf() for _ in range(NS)]
                r16 = [sb(f"r16_{i}", [128, 2, 512], BF16) for i in range(NS)]
                R16 = [Buf() for _ in range(NS)]
                ost = [sb(f"ost{i}", [128, 512], BF16) for i in range(2)]
                OST = [Buf(), Buf()]
                zb = [(0, 1), (0, 1)]
                sbks = [(2, 3), (4, 5)]
                ob = [6, 7]
                its = []
                for hp in range(4):
                    for g in range(NG):
                        nkb = 4 * g + 4
                        for pos, kb in enumerate(range(nkb - 1, -1, -1)):
                            its.append(dict(hp=hp, g=g, kb=kb, d=kb - 4 * g, first=(pos == 0), last=(kb == 0),
                                            gi=hp * NG + g, pos=pos))
                N = len(its)
                Vv = Vd.rearrange("(n p) c -> p n c", p=128)

                def load_hp(hp):
                    if hp > 3:
                        return
                    sl = hp % 2
                    B.dma("sp", f"B.kt{sl}", ktp[sl][:], KT[hp * 128:(hp + 1) * 128, :], [], [KV[sl]])
                    B.dma("sp", f"B.kt{sl}", vp[sl][:], Vv[:, :, hp * 128:(hp + 1) * 128], [], [KV[sl]])

                def load_q(gi):
                    if gi >= 4 * NG:
                        return
                    hp, g = gi // NG, gi % NG
                    B.dma("sp", f"B.q{gi % 2}", qs[gi % 2][:], QT[hp * 128:(hp + 1) * 128, g * 512:(g + 1) * 512],
                          [], [QS[gi % 2]])

                def Zf(k):
                    if not 0 <= k < N:
                        return
                    I = its[k]
                    kt, KVb = ktp[I["hp"] % 2], KV[I["hp"] % 2]
                    q, Q = qs[I["gi"] % 2], QS[I["gi"] % 2]
                    ksl = slice(I["kb"] * 128, (I["kb"] + 1) * 128)
                    for h in range(2):
                        pr = slice(h * 64, (h + 1) * 64)
                        zk = zb[k % 2][h]
                        B.mmg(ps[zk], [(kt[pr, ksl], q[pr, :])], [KVb, Q], [PS[zk]])

                def Ef(k):
                    if not 0 <= k < N:
                        return
                    z0, z1 = zb[k % 2]
                    B.act(ex[k % NS][:], ps_t[:, z0:z0 + 2, :], AF.Exp, [PS[z0], PS[z1]], [EX[k % NS]])

                def Lf(k):
                    if not 0 <= k < N:
                        return
                    I = its[k]
                    B.act(spt[k % NS][:], ex[k % NS][:], AF.Ln, [EX[k % NS]], [SPT[k % NS]], bias=1.0, nosame=True)
                    if I["d"] >= 0:
                        d = I["d"]
                        B.tt("dve", spt[k % NS][:], spt[k % NS][:],
                             maskL_t[:, d:d + 1, :].to_broadcast([128, 2, 512]), ALU.mult,
                             [SPT[k % NS], MK], [SPT[k % NS]])

                def Rf(k):
                    if not 0 <= k < N:
                        return
                    I = its[k]
                    if I["last"]:
                        return
                    if I["first"]:
                        B.cp("dve", r16[k % NS][:], spt[k % NS][:], [SPT[k % NS]], [R16[k % NS]])
                    else:
                        B.tt("dve", r16[k % NS][:], r16[(k - 1) % NS][:], spt[k % NS][:], ALU.add,
                             [SPT[k % NS], R16[(k - 1) % NS]], [R16[k % NS]])

                def Sf(k):
                    if not 0 <= k < N:
                        return
                    sbk = sbks[k % 2]
                    I = its[k]
                    kt, KVb = ktp[I["hp"] % 2], KV[I["hp"] % 2]
                    q, Q = qs[I["gi"] % 2], QS[I["gi"] % 2]
                    ksl = slice(I["kb"] * 128, (I["kb"] + 1) * 128)
                    for h in range(2):
                        pr = slice(h * 64, (h + 1) * 64)
                        pairs = [(negtri_t[:], spt[k % NS][:, h, :])]
                        rds = [SPT[k % NS], CONST, KVb, Q]
                        if not I["first"]:
                            pairs.append((nones_t[:], r16[(k - 1) % NS][:, h, :]))
                            rds.append(R16[(k - 1) % NS])
                        pairs.append((kt[pr, ksl], q[pr, :]))
                        B.mmg(ps[sbk[h]], pairs, rds, [PS[sbk[h]]])

                def Xf(k):
                    sbk = sbks[k % 2]
                    I = its[k]
                    B.act(at[k % NS][:], ps_t[:, sbk[0]:sbk[0] + 2, :], AF.Exp, [PS[sbk[0]], PS[sbk[1]]], [AT[k % NS]])
                    if I["d"] >= 0:
                        d = I["d"]
                        B.tt("pool", at[k % NS][:], at[k % NS][:],
                             maskL_t[:, d:d + 1, :].to_broadcast([128, 2, 512]), ALU.mult,
                             [AT[k % NS], MK], [AT[k % NS]])

                def AVf(k):
                    if not 0 <= k < N:
                        return
                    I = its[k]
                    vv, KVb = vp[I["hp"] % 2], KV[I["hp"] % 2]
                    o_bank = ob[I["gi"] % 2]
                    a_t = at[k % NS]
                    for h in range(2):
                        fn = (lambda e, h=h, a_t=a_t, kb=I["kb"], first=I["first"], last=I["last"], o_bank=o_bank, vv=vv:
                              e.matmul(ps_t[h * 64:(h + 1) * 64, o_bank, :], vv[:, kb, h * 64:(h + 1) * 64],
                                       a_t[:, h, :], start=first, stop=last))
                        B.op("pe", fn, [AT[k % NS], KVb], [PS[o_bank]])
                    if I["last"]:
                        gi = I["gi"]
                        B.cp("dve", ost[gi % 2][:], ps[o_bank], [PS[o_bank]], [OST[gi % 2]])
                        B.dma("sp", f"B.ost{gi % 2}",
                              OSB[I["hp"] * 128:(I["hp"] + 1) * 128, I["g"] * 512:(I["g"] + 1) * 512],
                              ost[gi % 2][:], [OST[gi % 2]], [])

                load_hp(0)
                load_q(0)
                Zf(0)
                Ef(0)
                Zf(1)
                Lf(0)
                Rf(0)
                Sf(0)
                for k in range(N):
                    I = its[k]
                    Ef(k + 1)
                    Lf(k + 1)
                    Rf(k + 1)
                    AVf(k - 1)
                    if I["pos"] == 1:
                        load_q(I["gi"] + 1)
                        if I["g"] == 1:
                            load_hp(I["hp"] + 1)
                    Zf(k + 2)
                    Sf(k + 1)
                    Xf(k)
                AVf(N - 1)
            B.barrier()

        def post_norm_residual(xt, XT, ysb, YSB, sq, SQ, rs, RS, tmp, TMP, l, gcol0, ncols, msbank):
            B.mmg(ps[msbank][:, 0:ncols], [(ones_t[:], sq[:, c, :]) for c in range(8)], [SQ, CONST], [PS[msbank]])
            B.act(rs[:], ps[msbank][:, 0:ncols], AF.Ln, [PS[msbank]], [RS], scale=1.0 / D, bias=EPS)
            B.act(rs[:], rs[:], AF.Exp, [RS], [RS], scale=-0.5)
            for c in range(8):
                B.stt("dve", tmp[c % 2][:], ysb[:, c, :], pv_t[:, l, gcol0 + c:gcol0 + c + 1], rs[:],
                      ALU.mult, ALU.mult, [YSB, RS, CONST], [TMP[c % 2]])
                B.tt("pool", xt[:, c, :], xt[:, c, :], tmp[c % 2][:], ALU.add, [TMP[c % 2], XT], [XT])

        def stage_D(l, xsrc):
            NT = DBG.get("NTD", T // 512)
            with ExitStack() as st:
                def sb(n, s, d):
                    return st.enter_context(nc.sbuf_tensor(f"D{l}_{n}", s, d))
                wscr_off[0] = 0
                wg = sb("wg", [128, 8, 3072], BF16)
                wbr = sb("wbr", [128, 8, 1024], BF16)
                wo = sb("wo", [128, 8, 1024], BF16)
                W = Buf()
                wv = w_in[l].rearrange("(c p) n -> p c n", p=128)
                for c in range(8):
                    for hf in range(2):
                        wload(wg[:, c, hf * 1536:(hf + 1) * 1536],
                              wv[:, c, 2304 + hf * 1536:2304 + (hf + 1) * 1536], [128, 1536], W)
                for c in range(4):
                    wload(wbr[:, c, :], w_br_sb[l, c * 128:(c + 1) * 128, :], [128, 1024], W)
                for c in range(2):
                    wload(wbr[:, 4 + c, :], w_br_pool[l, c * 128:(c + 1) * 128, :], [128, 1024], W)
                    wload(wbr[:, 6 + c, :], w_br_gm[l, c * 128:(c + 1) * 128, :], [128, 1024], W)
                for c in range(8):
                    wload(wo[:, c, :], w_out[l, c * 128:(c + 1) * 128, :], [128, 1024], W)
                wdone(W)
                xts = [sb(f"xt{i}", [128, 8, 512], F32) for i in range(2)]
                XTS = [Buf(), Buf()]
                brin = [sb(f"brin{i}", [128, 8, 512], BF16) for i in range(2)]
                BRIN = [Buf(), Buf()]
                sq = sb("sq", [128, 8, 512], BF16); SQ = Buf()
                hT = sb("hT", [128, 8, 512], BF16); HT = Buf()
                rs = sb("rs", [128, 512], F32); RS = Buf()
                gsb = [sb(f"gsb{i}", [128, 512], BF16) for i in range(3)]
                GSB = [Buf() for _ in range(3)]
                t0 = sb("t0", [128, 512], F32); T0 = Buf()
                t1 = sb("t1", [128, 512], F32); T1 = Buf()
                t2 = sb("t2", [128, 512], F32); T2 = Buf()
                merged = sb("merged", [128, 8, 512], BF16); MG = Buf()
                ysb = sb("ysb", [128, 8, 512], F32); YSB = Buf()
                tmp = [sb(f"tmp{i}", [128, 512], F32) for i in range(2)]; TMP = [Buf(), Buf()]
                xv = fm(xsrc)
                yv = fm(yT)
                yrot = Rot([6, 7])

                def d_loads(i):
                    if i >= NT:
                        return
                    tsl = slice(i * 512, (i + 1) * 512)
                    B.dma("sp", f"D.xt{i % 2}", xts[i % 2][:], xv[:, :, tsl], [], [XTS[i % 2]])
                    B.dma("sp", f"D.br{i % 2}", brin[i % 2][:, 0:4, :], fm(OSB)[:, :, tsl], [], [BRIN[i % 2]])
                    B.dma("sp", f"D.br{i % 2}", brin[i % 2][:, 4:6, :], fm(OPL)[:, :, tsl], [], [BRIN[i % 2]])
                    B.dma("sp", f"D.br{i % 2}", brin[i % 2][:, 6:8, :], fm(OGM)[:, :, tsl], [], [BRIN[i % 2]])

                def d_norm(i):
                    if i >= NT:
                        return
                    norm_tile(xts[i % 2], XTS[i % 2], sq, SQ, hT, HT, rs, RS, 0, l, 512, 6)

                d_loads(0)
                d_norm(0)
                for i in range(NT):
                    tsl = slice(i * 512, (i + 1) * 512)
                    xt, XT = xts[i % 2], XTS[i % 2]
                    bi, BI = brin[i % 2], BRIN[i % 2]
                    d_loads(i + 1)
                    for c in range(8):
                        csl = slice(c * 128, (c + 1) * 128)
                        for b in range(3):
                            B.mmg(ps[b], [(wg[:, k, b * 1024 + c * 128:b * 1024 + (c + 1) * 128], hT[:, k, :])
                                          for k in range(8)], [W, HT], [PS[b]])
                            B.act(gsb[b][:], ps[b], AF.Sigmoid, [PS[b]], [GSB[b]])
                        B.mmg(ps[3], [(wbr[:, k, csl], bi[:, k, :]) for k in range(0, 4)], [W, BI], [PS[3]])
                        B.mmg(ps[4], [(wbr[:, k, csl], bi[:, k, :]) for k in range(4, 6)], [W, BI], [PS[4]])
                        B.mmg(ps[5], [(wbr[:, k, csl], bi[:, k, :]) for k in range(6, 8)], [W, BI], [PS[5]])
                        B.tt("dve", t0[:], ps[3], gsb[0][:], ALU.mult, [PS[3], GSB[0]], [T0])
                        B.tt("dve", t1[:], ps[4], gsb[1][:], ALU.mult, [PS[4], GSB[1]], [T1])
                        B.tt("dve", t2[:], ps[5], gsb[2][:], ALU.mult, [PS[5], GSB[2]], [T2])
                        B.tt("pool", t0[:], t0[:], t1[:], ALU.add, [T1, T0], [T0])
                        B.tt("pool", merged[:, c, :], t0[:], t2[:], ALU.add, [T0, T2], [MG])
                    d_norm(i + 1)
                    for c in range(8):
                        csl = slice(c * 128, (c + 1) * 128)
                        bk = yrot.next()
                        B.mmg(ps[bk], [(wo[:, k, csl], merged[:, k, :]) for k in range(8)], [W, MG], [PS[bk]])
                        B.cp("dve", ysb[:, c, :], ps[bk], [PS[bk]], [YSB])
                        B.act(sq[:, c, :], ysb[:, c, :], AF.Square, [YSB], [SQ])
                    post_norm_residual(xt, XT, ysb, YSB, sq, SQ, rs, RS, tmp, TMP, l, 8, 512, 6)
                    B.dma("sp", f"D.xo{i % 2}", yv[:, :, tsl], xt[:], [XT], [])
            B.barrier()

        def stage_E(l):
            NW = 256
            NT = DBG.get("NTE", T // NW)
            with ExitStack() as st:
                def sb(n, s, d):
                    return st.enter_context(nc.sbuf_tensor(f"E{l}_{n}", s, d))
                wscr_off[0] = 0
                wf1 = sb("wf1", [128, 8, DFF], BF16)
                wf2 = sb("wf2", [128, 32, 1024], BF16)
                W = Buf()
                w1v = w_ff_in[l].rearrange("(c p) n -> p c n", p=128)
                w2v = w_ff_out[l].rearrange("(c p) n -> p c n", p=128)
                for c in range(8 if "nowf1" not in DBG else 0):
                    for hf in range(2):
                        wload(wf1[:, c, hf * 2048:(hf + 1) * 2048], w1v[:, c, hf * 2048:(hf + 1) * 2048], [128, 2048], W)
                for c in range(32 if "nowf2" not in DBG else 0):
                    wload(wf2[:, c, :], w_ff_out[l, c * 128:(c + 1) * 128, :], [128, 1024], W)
                wdone(W)
                xts = [sb(f"xt{i}", [128, 8, NW], F32) for i in range(2)]
                XTS = [Buf(), Buf()]
                sq = sb("sq", [128, 8, NW], BF16); SQ = Buf()
                hT = sb("hT", [128, 8, NW], BF16); HT = Buf()
                rs = sb("rs", [128, NW], F32); RS = Buf()
                rl = [sb(f"rl{i}", [128, 2, NW], F32) for i in range(2)]
                RL = [Buf(), Buf()]
                aT = sb("aT", [128, 32, NW], BF16); ATB = Buf()
                ysb = sb("ysb", [128, 8, NW], F32); YSB = Buf()
                tmp = [sb(f"tmp{i}", [128, NW], F32) for i in range(2)]; TMP = [Buf(), Buf()]
                yv = fm(yT)
                arot = Rot([0, 1, 2, 3])
                yrot = Rot([4, 5])
                ri = 0
                def e_loads(i):
                    if i >= NT:
                        return
                    B.dma("sp", f"E.xt{i % 2}", xts[i % 2][:], yv[:, :, i * NW:(i + 1) * NW], [], [XTS[i % 2]])

                def e_norm(i):
                    if i >= NT:
                        return
                    norm_tile(xts[i % 2], XTS[i % 2], sq, SQ, hT, HT, rs, RS, 16, l, NW, 6)

                EP = DBG.get("Eparts", "nabps")
                e_loads(0)
                e_norm(0)
                for i in range(NT):
                    tsl = slice(i * NW, (i + 1) * NW)
                    xt, XT = xts[i % 2], XTS[i % 2]
                    e_loads(i + 1)
                    for j2 in (range(16) if "a" in EP else []):
                        bk = arot.next()
                        for jj in range(2):
                            j = j2 * 2 + jj
                            B.mmg(ps[bk][:, jj * NW:(jj + 1) * NW],
                                  [(wf1[:, k, j * 128:(j + 1) * 128], hT[:, k, :]) for k in range(8)],
                                  [W, HT], [PS[bk]])
                        r, R = rl[ri % 2], RL[ri % 2]
                        ri += 1
                        B.act(r[:], ps[bk].rearrange("p (a n) -> p a n", a=2), AF.Relu, [PS[bk]], [R])
                        B.tt("pool", aT[:, j2 * 2:j2 * 2 + 2, :], r[:], r[:], ALU.mult, [R], [ATB])
                    e_norm(i + 1)
                    for c2 in (range(4) if "b" in EP else []):
                        bk = yrot.next()
                        for cc in range(2):
                            c = c2 * 2 + cc
                            B.mmg(ps[bk][:, cc * NW:(cc + 1) * NW],
                                  [(wf2[:, k, c * 128:(c + 1) * 128], aT[:, k, :]) for k in range(32)],
                                  [W, ATB], [PS[bk]])
                        pv2 = ps[bk].rearrange("p (a n) -> p a n", a=2)
                        B.cp("dve", ysb[:, c2 * 2:c2 * 2 + 2, :], pv2, [PS[bk]], [YSB])
                        B.act(sq[:, c2 * 2:c2 * 2 + 2, :], ysb[:, c2 * 2:c2 * 2 + 2, :], AF.Square, [YSB], [SQ])
                    if "p" in EP:
                        post_norm_residual(xt, XT, ysb, YSB, sq, SQ, rs, RS, tmp, TMP, l, 24, NW, 6)
                    if "s" in EP:
                        B.dma("sp", f"E.xo{i % 2}", yv[:, :, tsl], xt[:], [XT], [])
            B.barrier()

        for l in range(nlayers):
            xsrc = xin if l == 0 else yT
            if "A" in stages:
                stage_A(l, xsrc)
            if "B" in stages:
                stage_B(l)
            if "D" in stages:
                stage_D(l, xsrc)
            if "E" in stages:
                stage_E(l)
        B.final()
        with nc.Block() as block:
            B.emit(block)
    return nc


def _consts():
    p = np.arange(128)
    tril = (p[:, None] <= p[None, :]).astype(np.float32)
    negtri = -(p[:, None] >= p[None, :]).astype(np.float32)
    maskL = np.zeros((128, 4, 512), np.float32)
    qidx = np.arange(512)
    for d in range(4):
        kidx = d * 128 + p
        maskL[:, d, :] = (kidx[:, None] < qidx[None, :]).astype(np.float32)
    invcnt = np.zeros((2, 256, 512), np.float32)
    pos = np.arange(512, dtype=np.float32)
    for g, w in enumerate((2, 4, 8, 16)):
        invcnt[0, g * 64:(g + 1) * 64, :] = 1.0 / np.minimum(pos + 1.0, float(w))[None, :]
        invcnt[1, g * 64:(g + 1) * 64, :] = 1.0 / float(w)
    return tril, negtri, maskL, invcnt


_NC_CACHE = {}


def kernel(x, w_in, w_pool, pool_scale, gm_gain, w_spatial, b_spatial, w_br_sb, w_br_pool,
           w_br_gm, w_out, g_mix_pre, g_mix_post, g_ff_pre, g_ff_post, w_ff_in, w_ff_out):
    f = lambda a: np.ascontiguousarray(np.asarray(a, dtype=np.float32))
    x = f(x)
    tril, negtri, maskL, invcnt = _consts()
    def pc(v, n):
        return f(v).reshape(DEPTH, n, 128).transpose(0, 2, 1)
    pvec = np.concatenate([pc(g_mix_pre, 8), pc(g_mix_post, 8), pc(g_ff_pre, 8), pc(g_ff_post, 8),
                           pc(pool_scale, 2)], axis=2)
    shared = {
        "w_in": f(w_in), "w_pool": f(w_pool), "gm_gain": f(gm_gain),
        "w_spT": np.ascontiguousarray(f(w_spatial).transpose(0, 1, 3, 2)),
        "b_spatial": f(b_spatial), "w_br_sb": f(w_br_sb), "w_br_pool": f(w_br_pool),
        "w_br_gm": f(w_br_gm), "w_out": f(w_out), "w_ff_in": f(w_ff_in), "w_ff_out": f(w_ff_out),
        "pvec": np.ascontiguousarray(pvec), "c_tril": tril, "c_negtri": negtri, "c_maskL": maskL,
        "c_invcnt": invcnt,
    }
    if "nc" not in _NC_CACHE:
        _NC_CACHE["nc"] = build()
    nc = _NC_CACHE["nc"]
    in_maps = []
    for c in range(NCORES):
        m = dict(shared)
        m["xT"] = np.ascontiguousarray(x[c].T)
        in_maps.append(m)
    res = run_bass_kernel_spmd(nc, in_maps, core_ids=list(range(NCORES)))
    out = np.stack([np.ascontiguousarray(res.results[c]["yT"].T) for c in range(NCORES)], axis=0)
    return out.astype(np.float32)
```
